# Optimizing a Trainium2 kernel written in Bass

```python
import math
import jax, jax.numpy as jnp
from jax import lax
import numpy as np

D_MODEL = 1024
BATCH = 2
SEQ = 8192
DEPTH = 2

CHUNK = 64
Q_BLOCK = 128
ROPE_THETA = 10000.0
RMS_EPS = 1e-6
D_FF = 2816

DA_HEADS = 4
DA_HEAD_DIM = 64
DA_V_DIM = 2 * DA_HEAD_DIM
DA_WIDTH = DA_HEADS * DA_V_DIM

ML_HEADS = 4
ML_QK_DIM = 64
ML_V_DIM = 128
ML_WIDTH = ML_HEADS * ML_V_DIM
ML_CONV = 4
ML_QK_COLS = 2 * ML_HEADS * ML_QK_DIM

MIX_WIDTH = DA_WIDTH + ML_WIDTH

COLS = (DA_HEADS * 2 * DA_HEAD_DIM, DA_HEADS * 2 * DA_HEAD_DIM, DA_WIDTH,
        ML_QK_COLS, ML_WIDTH, ML_WIDTH, 2 * ML_HEADS)
SPLIT_IDX = tuple(int(s) for s in np.cumsum(COLS)[:-1])
IN_COLS = int(sum(COLS))

kernel_name = 'hybrid_diffattn_mlstm_macaron'


def _rmsnorm(x, g):
    xf = x.astype(jnp.float32)
    y = xf * lax.rsqrt(jnp.mean(xf * xf, axis=-1, keepdims=True) + RMS_EPS)
    return (y * g.astype(jnp.float32)).astype(x.dtype)


def _swiglu(h, w_in, w_out):
    gate, up = jnp.split(h @ w_in, 2, axis=-1)
    return (jax.nn.silu(gate) * up) @ w_out


def _rope(t, seq_len):
    d = t.shape[-1]
    inv_freq = ROPE_THETA ** (-jnp.arange(0, d, 2, dtype=jnp.float32) / d)
    ang = jnp.arange(seq_len, dtype=jnp.float32)[:, None] * inv_freq[None, :]
    cos, sin = jnp.cos(ang), jnp.sin(ang)
    t1, t2 = t[..., : d // 2], t[..., d // 2:]
    return jnp.concatenate([t1 * cos - t2 * sin, t2 * cos + t1 * sin], axis=-1)


def _diff_attention(q, k, v, lam_vecs, subln_g, lam_init):
    B, S, _ = q.shape
    q = q.reshape(B, S, DA_HEADS, 2, DA_HEAD_DIM).transpose(0, 2, 3, 1, 4)
    k = k.reshape(B, S, DA_HEADS, 2, DA_HEAD_DIM).transpose(0, 2, 3, 1, 4)
    v = v.reshape(B, S, DA_HEADS, DA_V_DIM).transpose(0, 2, 1, 3)
    q = _rope(q, S) * (DA_HEAD_DIM ** -0.5)
    k = _rope(k, S)
    lam = (jnp.exp(jnp.sum(lam_vecs[0] * lam_vecs[1]))
           - jnp.exp(jnp.sum(lam_vecs[2] * lam_vecs[3])) + lam_init)
    n_blk = S // Q_BLOCK
    q_blocks = q.reshape(B, DA_HEADS, 2, n_blk, Q_BLOCK, DA_HEAD_DIM).transpose(3, 0, 1, 2, 4, 5)
    key_chunk = jnp.arange(S) // CHUNK

    def block(args):
        q_blk, blk = args
        s = jnp.einsum('bhcqd,bhckd->bhcqk', q_blk, k)
        q_chunk = (blk * Q_BLOCK + jnp.arange(Q_BLOCK)) // CHUNK
        mask = key_chunk[None, :] <= q_chunk[:, None]
        p = jax.nn.softmax(jnp.where(mask, s, -jnp.inf), axis=-1)
        a = p[:, :, 0] - lam * p[:, :, 1]
        return jnp.einsum('bhqk,bhkd->bhqd', a, v)

    o = lax.map(block, (q_blocks, jnp.arange(n_blk)))
    o = o.transpose(1, 0, 3, 2, 4).reshape(B, S, DA_HEADS, DA_V_DIM)
    o = _rmsnorm(o, subln_g) * (1.0 - lam_init)
    return o.reshape(B, S, DA_WIDTH)


def _causal_conv(t, w, b):
    S = t.shape[1]
    tp = jnp.pad(t, ((0, 0), (ML_CONV - 1, 0), (0, 0)))
    y = b
    for j in range(ML_CONV):
        y = y + w[j] * tp[:, j:j + S]
    return y


def _mlstm(qk, v, o, gates, conv_w, conv_b, gate_b, norm_g):
    B, S, _ = v.shape
    NC = S // CHUNK
    qk = jax.nn.silu(_causal_conv(qk, conv_w, conv_b))
    q, k = jnp.split(qk, 2, axis=-1)

    def chunks(t, d):
        return t.reshape(B, NC, CHUNK, ML_HEADS, d).transpose(0, 3, 1, 2, 4)

    q = chunks(q, ML_QK_DIM) * (ML_QK_DIM ** -0.5)
    k = chunks(k, ML_QK_DIM)
    v = chunks(v, ML_V_DIM)
    g = (gates + gate_b).reshape(B, NC, CHUNK, 2, ML_HEADS).transpose(3, 0, 4, 1, 2)
    log_i = g[0]
    log_f = jax.nn.log_sigmoid(g[1])
    b = jnp.cumsum(log_f, axis=-1)
    b_last = b[..., -1]
    a = b_last[..., None] - b + log_i

    def step(carry, xs):
        C, n, m = carry
        k_c, v_c, a_c, bl = xs
        m_new = jnp.maximum(bl + m, a_c.max(-1))
        w = jnp.exp(a_c - m_new[..., None])
        decay = jnp.exp(bl + m - m_new)
        C_new = decay[..., None, None] * C + jnp.einsum('bhl,bhle,bhld->bhed', w, v_c, k_c)
        n_new = decay[..., None] * n + jnp.einsum('bhl,bhld->bhd', w, k_c)
        return (C_new, n_new, m_new), (C, n, m)

    init = (jnp.zeros((B, ML_HEADS, ML_V_DIM, ML_QK_DIM), jnp.float32),
            jnp.zeros((B, ML_HEADS, ML_QK_DIM), jnp.float32),
            jnp.zeros((B, ML_HEADS), jnp.float32))
    xs = (k.transpose(2, 0, 1, 3, 4), v.transpose(2, 0, 1, 3, 4),
          a.transpose(2, 0, 1, 3), b_last.transpose(2, 0, 1))
    _, (C_prev, n_prev, m_prev) = lax.scan(step, init, xs)
    C_prev = C_prev.transpose(1, 2, 0, 3, 4)
    n_prev = n_prev.transpose(1, 2, 0, 3)
    m_prev = m_prev.transpose(1, 2, 0)

    causal = jnp.tril(jnp.ones((CHUNK, CHUNK), dtype=bool))
    D = jnp.where(causal, b[..., :, None] - b[..., None, :] + log_i[..., None, :], -jnp.inf)
    m_inter = b + m_prev[..., None]
    m = jnp.maximum(m_inter, D.max(-1))
    w = jnp.exp(D - m[..., None]) * jnp.einsum('bhcjd,bhcsd->bhcjs', q, k)
    decay = jnp.exp(m_inter - m)
    num = (jnp.einsum('bhcjs,bhcse->bhcje', w, v)
           + decay[..., None] * jnp.einsum('bhcjd,bhced->bhcje', q, C_prev))
    den = w.sum(-1) + decay * jnp.einsum('bhcjd,bhcd->bhcj', q, n_prev)
    h = num / jnp.maximum(jnp.abs(den), jnp.exp(-m))[..., None]
    h = h.transpose(0, 2, 3, 1, 4).reshape(B, S, ML_HEADS, ML_V_DIM)
    h = _rmsnorm(h, norm_g).reshape(B, S, ML_WIDTH)
    return jax.nn.sigmoid(o) * h


def _mixer(h, w_in, w_out, lam_vecs, subln_g, lam_init, conv_w, conv_b, gate_b, ml_norm_g):
    proj = (h @ w_in).astype(jnp.float32)
    da_q, da_k, da_v, ml_qk, ml_v, ml_o, ml_g = jnp.split(proj, SPLIT_IDX, axis=-1)
    y_da = _diff_attention(da_q, da_k, da_v, lam_vecs, subln_g, lam_init)
    y_ml = _mlstm(ml_qk, ml_v, ml_o, ml_g, conv_w, conv_b, gate_b, ml_norm_g)
    y = jnp.concatenate([y_da, y_ml], axis=-1).astype(h.dtype)
    return y @ w_out


def setup_inputs(seed: int = 0) -> dict:
    key = jax.random.key(seed)
    ks = jax.random.split(key, 13)
    f32 = jnp.float32
    x = jax.random.normal(ks[0], (BATCH, SEQ, D_MODEL), f32)
    ffn_w_in = jax.random.normal(ks[1], (DEPTH, 2, D_MODEL, 2 * D_FF), f32) * D_MODEL ** -0.5
    ffn_w_out = jax.random.normal(ks[2], (DEPTH, 2, D_FF, D_MODEL), f32) * D_FF ** -0.5
    norm_gains = 1.0 + 0.05 * jax.random.normal(ks[3], (DEPTH, 6, D_MODEL), f32)
    mix_w_in = jax.random.normal(ks[4], (DEPTH, D_MODEL, IN_COLS), f32) * D_MODEL ** -0.5
    mix_w_out = jax.random.normal(ks[5], (DEPTH, MIX_WIDTH, D_MODEL), f32) * MIX_WIDTH ** -0.5
    da_lambda = 0.1 * jax.random.normal(ks[6], (DEPTH, 4, DA_HEAD_DIM), f32)
    da_subln_g = 1.0 + 0.05 * jax.random.normal(ks[7], (DEPTH, DA_V_DIM), f32)
    ml_conv_w = jax.random.normal(ks[8], (DEPTH, ML_CONV, ML_QK_COLS), f32) * ML_CONV ** -0.5
    ml_conv_b = 0.01 * jax.random.normal(ks[9], (DEPTH, ML_QK_COLS), f32)
    ml_gate_b = jnp.concatenate([
        0.1 * jax.random.normal(ks[10], (DEPTH, ML_HEADS), f32),
        jax.random.uniform(ks[11], (DEPTH, ML_HEADS), f32, minval=3.0, maxval=6.0)], axis=-1)
    ml_norm_g = 1.0 + 0.05 * jax.random.normal(ks[12], (DEPTH, ML_V_DIM), f32)
    return {'x': x, 'ffn_w_in': ffn_w_in, 'ffn_w_out': ffn_w_out, 'norm_gains': norm_gains,
            'mix_w_in': mix_w_in, 'mix_w_out': mix_w_out, 'da_lambda': da_lambda,
            'da_subln_g': da_subln_g, 'ml_conv_w': ml_conv_w, 'ml_conv_b': ml_conv_b,
            'ml_gate_b': ml_gate_b, 'ml_norm_g': ml_norm_g}


def reference(x, ffn_w_in, ffn_w_out, norm_gains, mix_w_in, mix_w_out, da_lambda,
              da_subln_g, ml_conv_w, ml_conv_b, ml_gate_b, ml_norm_g):
    for l in range(DEPTH):
        g = norm_gains[l]
        h = _swiglu(_rmsnorm(x, g[0]), ffn_w_in[l, 0], ffn_w_out[l, 0])
        x = x + 0.5 * _rmsnorm(h, g[1])
        lam_init = 0.8 - 0.6 * math.exp(-0.3 * l)
        h = _mixer(_rmsnorm(x, g[2]), mix_w_in[l], mix_w_out[l], da_lambda[l], da_subln_g[l],
                   lam_init, ml_conv_w[l], ml_conv_b[l], ml_gate_b[l], ml_norm_g[l])
        x = x + _rmsnorm(h, g[3])
        h = _swiglu(_rmsnorm(x, g[4]), ffn_w_in[l, 1], ffn_w_out[l, 1])
        x = x + 0.5 * _rmsnorm(h, g[5])
    return x
```

```python
import numpy as np
import concourse.bass as bass
import concourse.mybir as mybir
from concourse.bass_utils import run_bass_kernel_spmd

F32 = mybir.dt.float32
BF16 = mybir.dt.bfloat16
ALU = mybir.AluOpType
AF = mybir.ActivationFunctionType

D = 1024
NCH = 8
DFF = 2816
NF = 22
DEPTH = 2
T = 2048
SEQ = 8192
EPS = 1e-6

ENGS = ("pe", "act", "dve", "pool", "sp")
EPOCH = 3000


class Buf:
    __slots__ = ("name", "w", "r")

    def __init__(self, name, inherit=None):
        self.name = name
        self.w = None
        self.r = list(inherit) if inherit else []


class Ev:
    __slots__ = ("key", "val", "clock", "op")

    def __init__(self, key, val, clock, op):
        self.key, self.val, self.clock, self.op = key, val, clock, op


class DSem:
    def __init__(self, sem):
        self.sem = sem
        self.count = 0


class Op:
    __slots__ = ("eng", "fn", "waits", "ev", "signal", "sig_no", "dsem", "inc")


class _Rec:
    def __init__(self):
        self.call = None

    def __getattr__(self, name):
        def f(*a, **k):
            assert self.call is None
            self.call = (name, a, k)
            return None
        return f


def _compress(events):
    best = {}
    for ev in events:
        cur = best.get(ev.key)
        if cur is None or cur.val < ev.val:
            best[ev.key] = ev
    return list(best.values())


class Prog:
    def __init__(self, nc):
        self.nc = nc
        self.ops = {e: [] for e in ENGS}
        self.known = {e: {} for e in ENGS}
        self.nwaits = 0
        self._nsem = 0

    def new_dsem(self, name="d"):
        self._nsem += 1
        return DSem(self.nc.alloc_semaphore(name=f"{name}_{self._nsem}"))

    def dsem(self, key):
        if not hasattr(self, "_cache"):
            self._cache = {}
        if key not in self._cache:
            self._cache[key] = self.new_dsem(str(key))
        return self._cache[key]

    def _add(self, eng, fn, reads, writes, dsem=None, inc=16, extra=()):
        op = Op()
        if fn is not None:
            rec = _Rec()
            fn(rec)
            assert rec.call is not None
            fn = rec.call
        op.eng, op.fn, op.dsem, op.signal, op.sig_no, op.inc = eng, fn, dsem, False, 0, inc
        deps = {}
        for b in reads:
            if b.w is not None:
                deps[id(b.w)] = b.w
        for b in writes:
            if b.w is not None:
                deps[id(b.w)] = b.w
            for r in b.r:
                deps[id(r)] = r
        for ev in extra:
            deps[id(ev)] = ev
        known = self.known[eng]
        waits = {}
        for ev in deps.values():
            if eng == "pe" and ev.key == "pe":
                continue
            if known.get(ev.key, 0) >= ev.val:
                continue
            cur = waits.get(ev.key)
            if cur is None or cur.val < ev.val:
                waits[ev.key] = ev
        if waits:
            known = dict(known)
            for ev in waits.values():
                for k, v in ev.clock.items():
                    if known.get(k, 0) < v:
                        known[k] = v
                if known.get(ev.key, 0) < ev.val:
                    known[ev.key] = ev.val
            self.known[eng] = known
            self.nwaits += len(waits)
        op.waits = list(waits.values())
        for ev in op.waits:
            if ev.op.dsem is None:
                ev.op.signal = True
        self.ops[eng].append(op)
        idx = len(self.ops[eng])
        if dsem is None:
            ev = Ev(eng, idx, known, op)
        else:
            dsem.count += inc
            ev = Ev(dsem, dsem.count, known, op)
        op.ev = ev
        for b in reads:
            b.r.append(ev)
        for b in writes:
            b.w = ev
            b.r = []
        return ev

    def pe(self, fn, reads, writes):
        return self._add("pe", fn, reads, writes)

    def act(self, fn, reads, writes):
        return self._add("act", fn, reads, writes)

    def dve(self, fn, reads, writes):
        return self._add("dve", fn, reads, writes)

    def dma(self, queue, fn, reads, writes, dsem, inc=16, extra=()):
        return self._add(queue, fn, reads, writes, dsem=dsem, inc=inc, extra=extra)

    def fence(self, eng, events):
        return self._add(eng, None, [], [], extra=events)

    def emit(self):
        nc = self.nc
        esems = {}
        for e in ENGS:
            n = 0
            for op in self.ops[e]:
                if op.signal:
                    n += 1
                    op.sig_no = n
            nep = max(1, (n + EPOCH - 1) // EPOCH)
            esems[e] = [nc.alloc_semaphore(name=f"prog_{e}_{i}") for i in range(nep)]

        def sem_of(ev):
            if isinstance(ev.key, DSem):
                return ev.key.sem, ev.val
            n = ev.op.sig_no
            assert n > 0
            return esems[ev.key][(n - 1) // EPOCH], (n - 1) % EPOCH + 1

        def run(eng_name, eng):
            for op in self.ops[eng_name]:
                for ev in op.waits:
                    s, v = sem_of(ev)
                    eng.wait_ge(s, v)
                if op.fn is None:
                    if op.signal:
                        s, v = sem_of(op.ev)
                        eng.nop().then_inc(s, 1)
                    continue
                name, a, k = op.fn
                ins = getattr(eng, name)(*a, **k)
                if op.dsem is not None:
                    ins.then_inc(op.dsem.sem, op.inc)
                elif op.signal:
                    s, v = sem_of(op.ev)
                    ins.then_inc(s, 1)

        with nc.Block() as block:
            @block.tensor
            def _(e):
                run("pe", e)

            @block.scalar
            def _(e):
                run("act", e)

            @block.vector
            def _(e):
                run("dve", e)

            @block.gpsimd
            def _(e):
                run("pool", e)

            @block.sync
            def _(e):
                run("sp", e)


class Arena:
    def __init__(self, nc, base, top):
        self.nc, self.base, self.top = nc, base, top
        self.live = {}
        self.dead = []
        self.n = 0

    def alloc(self, name, off, free_shape, dtype, nbuf=1):
        esz = 4 if dtype == F32 else 2
        cols = int(np.prod(free_shape))
        start = self.base + off
        end = start + cols * esz
        assert end <= self.top, (name, end, self.top)
        assert start % 32 == 0, (name, start)
        for k, (s, e, _) in self.live.items():
            assert e <= start or s >= end, f"arena overlap {name} vs {k}"
        inherit = []
        keep = []
        for (s, e, evs) in self.dead:
            if not (e <= start or s >= end):
                inherit.extend(evs)
                if s >= start and e <= end:
                    continue
            keep.append((s, e, evs))
        self.dead = keep
        inherit = _compress(inherit)
        self.n += 1
        th = self.nc.alloc_sbuf_tensor_at(f"{name}{self.n}", [128, cols], dtype, offset=start)
        bl = [Buf(f"{name}_{i}", inherit) for i in range(nbuf)]
        self.live[name] = (start, end, bl)
        return th.ap(), bl

    def free(self, name):
        s, e, bl = self.live.pop(name)
        evs = []
        for b in bl:
            if b.w is not None:
                evs.append(b.w)
            evs.extend(b.r)
        self.dead.append((s, e, _compress(evs)))


class WStream:
    NSLOT = 4
    SLOT_ELEMS = 2816

    def __init__(self, P, arena, off):
        self.P = P
        self.ap, self.bufs = arena.alloc("wring", off, [self.NSLOT * self.SLOT_ELEMS], BF16, nbuf=self.NSLOT)
        self.sems = [P.new_dsem("w") for _ in range(self.NSLOT)]
        self.tiles = []
        self.issued = 0
        self.cursor = 0

    def plan(self, tag, dram_ap, nelem):
        assert nelem <= self.SLOT_ELEMS
        self.tiles.append((tag, dram_ap, nelem))

    def _issue(self, i):
        tag, src, nelem = self.tiles[i]
        s = i % self.NSLOT
        dst = self.ap[:, s * self.SLOT_ELEMS: s * self.SLOT_ELEMS + nelem]
        self.P.dma("pool", lambda e, dst=dst, src=src: e.dma_start(out=dst, in_=src),
                   [], [self.bufs[s]], self.sems[s])

    def get(self, tag):
        while self.issued < min(len(self.tiles), self.cursor + self.NSLOT):
            self._issue(self.issued)
            self.issued += 1
        i = self.cursor
        assert self.tiles[i][0] == tag, (self.tiles[i][0], tag)
        self.cursor += 1
        s = i % self.NSLOT
        nelem = self.tiles[i][2]
        return self.ap[:, s * self.SLOT_ELEMS: s * self.SLOT_ELEMS + nelem], self.bufs[s]


NKIND = 6
RG = [[0, 1, 2, 3], [4, 5, 6, 7]]
NT = SEQ // 128
C_EPS = 200
C_SUBG = 208
C_MLG = 210
C_GB = 216
C_CONV = 220
C_GSEL = 240
C_ONE = 244
C_LAM = 256
C_I = 768
C_TRI = 896
C_NEG = 1024
C_SEL = 1152
C_SELSW = 1664
NCST = 2176


def build_program(phases, stop=None):
    import math
    nc = bass.Bass("TRN2", target_bir_lowering=False)
    P = Prog(nc)
    ffn_ids = sorted({ph[1] * 2 + ph[2] for ph in phases if ph[0] == "ffn"})
    mix_ids = sorted({ph[1] for ph in phases if ph[0] == "mix"})
    x_in = nc.dram_tensor("xT", [128, NCH * T], F32, kind="ExternalInput").ap()
    w1_d = nc.dram_tensor("w1", [max(1, len(ffn_ids)) * NF, 128, 2 * NCH * 128], F32, kind="ExternalInput").ap()
    w2_d = nc.dram_tensor("w2", [max(1, len(ffn_ids)) * NCH, 128, NF * 128], F32, kind="ExternalInput").ap()
    wm_d = nc.dram_tensor("wm", [max(1, len(mix_ids)) * 17, 128, 2048], F32, kind="ExternalInput").ap()
    wo_d = nc.dram_tensor("wo", [max(1, len(mix_ids)) * NCH, 128, 1024], F32, kind="ExternalInput").ap()
    rope_d = nc.dram_tensor("rope", [128, 2 * T], F32, kind="ExternalInput").ap()
    cst_d = nc.dram_tensor("cst", [128, NCST], F32, kind="ExternalInput").ap()
    out_d = nc.dram_tensor("out", [128, NCH * T if stop is None else 16], F32, kind="ExternalOutput").ap()
    dbg_d = None
    if stop is not None:
        dbg_d = nc.dram_tensor("dbg", [256, SEQ if stop in ("att", "mls") else 2048], BF16, kind="ExternalOutput").ap()
    NCHK = 13
    gin_c = [nc.dram_tensor(f"gin{i}", [256 if i < 12 else 128, T], BF16).ap() for i in range(NCHK)]
    gout_c = [nc.dram_tensor(f"gout{i}", [4 * (256 if i < 12 else 128), T], BF16).ap() for i in range(NCHK)]
    gin2_c = [nc.dram_tensor(f"gin2_{i}", [256, T], BF16).ap() for i in range(4)]
    gout2_c = [nc.dram_tensor(f"gout2_{i}", [4 * 256, T], BF16).ap() for i in range(4)]
    gin_b = [[Buf(f"gin{k}_{h}") for h in range(4)] for k in range(NKIND)]
    ging_b = Buf("ging")
    gout_b = [Buf(f"gout{i}") for i in range(NCHK)]
    gin2_b = [[Buf(f"gin2_{k}_{q}") for q in range(16)] for k in range(2)]
    gout2_b = [Buf(f"gout2_{i}") for i in range(4)]

    def gin_kh(kind, h):
        return gin_c[kind * 2 + h // 2][(h % 2) * 128:(h % 2) * 128 + 128, :]

    def gout_rkh(r, kind, h):
        base = r * 256 + (h % 2) * 128
        return gout_c[kind * 2 + h // 2][base:base + 128, :]

    A = Arena(nc, 16512, 229344)
    xT, xT_b = A.alloc("xT", 0, [NCH * T], F32, nbuf=4)
    xT3 = xT.rearrange("p (c t) -> p c t", c=NCH)
    W = WStream(P, A, 65536)
    CB = 88064
    cst, cst_b = A.alloc("cst", CB, [768], F32)
    cbf, cbf_b = A.alloc("cbf", CB + 3072, [1552], BF16)
    ones_bf, I_bf, tri_bf, neg_bf, sel_bf = (cbf[:, 0:128], cbf[:, 128:256], cbf[:, 256:384],
                                             cbf[:, 384:512], cbf[:, 512:1024])
    ones_b = cbf_b
    sml, sml_b = A.alloc("sml", CB + 3072 + 3104, [128], F32)
    PH = CB + 3072 + 3104 + 512
    gsel_bf = cbf[:, 1024:1026]
    selsw_bf = cbf[:, 1040:1552]
    psum = [nc.alloc_psum_tensor(f"ps{i}", [128, 512], F32).ap() for i in range(8)]
    ps_b = [Buf(f"ps{i}") for i in range(8)]
    ld = [P.new_dsem("ld") for _ in range(4)]
    misc = P.new_dsem("misc")
    rope_sem = P.new_dsem("rope")
    ging_sem = P.new_dsem("ging")

    for ph in phases:
        kind = ph[0]
        if kind == "ffn":
            _, l, k = ph
            lf = l * 2 + k
            fi = ffn_ids.index(lf)
            for ps_ in range(2):
                for f in range(NF):
                    W.plan(("w1", lf, ps_, f), w1_d[fi * NF + f], 2 * NCH * 128)
                for m in range(NCH):
                    W.plan(("w2", lf, ps_, m), w2_d[fi * NCH + m], NF * 128)
        else:
            l = ph[1]
            mi = mix_ids.index(l)
            for ti in range(17):
                W.plan(("wm", l, ti), wm_d[mi * 17 + ti], 2048)
            if stop is None:
                for m in range(NCH):
                    W.plan(("wo", l, m), wo_d[mi * NCH + m], 1024)

    for tt in range(4):
        P.dma("sp", lambda e, tt=tt: e.dma_start(out=xT3[:, :, tt * 512:(tt + 1) * 512],
                                                 in_=x_in.rearrange("p (c t) -> p c t", c=NCH)[:, :, tt * 512:(tt + 1) * 512]),
              [], [xT_b[tt]], ld[tt])
    P.dma("sp", lambda e: e.dma_start(out=cst, in_=cst_d[:, 0:768]), [], [cst_b[0]], misc)
    ctmp, ctmp_b = A.alloc("ctmp", PH, [NCST - 768], F32)
    ctmp_sem = P.new_dsem("ctmp")
    P.dma("sp", lambda e: e.dma_start(out=ctmp, in_=cst_d[:, 768:NCST]), [], [ctmp_b[0]], ctmp_sem)
    P.dve(lambda e: e.memset(ones_bf, 1.0), [], [cbf_b[0]])
    P.dve(lambda e: e.tensor_copy(cbf[:, 128:1024], ctmp[:, 0:896]), [ctmp_b[0]], [cbf_b[0]])
    P.dve(lambda e: e.tensor_copy(gsel_bf, cst[:, C_GSEL:C_GSEL + 2]), [cst_b[0]], [cbf_b[0]])
    P.dve(lambda e: e.tensor_copy(selsw_bf, ctmp[:, C_SELSW - 768:C_SELSW - 768 + 512]), [ctmp_b[0]], [cbf_b[0]])
    A.free("ctmp")

    eps_col = cst[:, C_EPS:C_EPS + 1]

    def gain(l, k, c):
        i = (l * 6 + k) * NCH + c
        return cst[:, i:i + 1]

    def gain_half(l, k, c):
        i = 96 + (l * 6 + k) * NCH + c
        return cst[:, i:i + 1]

    cst_half_b = Buf("csthalf")
    P.dve(lambda e: e.tensor_scalar(cst[:, 96:192], cst[:, 0:96], 0.5, None, ALU.mult), [cst_b[0]], [cst_half_b])

    def rms_rstd(dst, src_ps, reads, writes, scale=1.0 / D):
        P.act(lambda e: e.activation(dst, src_ps, AF.Sqrt, bias=eps_col, scale=scale), reads + [cst_b[0]], writes)
        P.dve(lambda e: e.reciprocal(dst, dst), writes, writes)

    def pre_norm(l, kpre, ntt, t0, xn3, xn_b, sq3, sq_bl, rs, rs_b, pbank):
        for tt in range(ntt):
            g = (t0 + tt * 512) // 512
            ts = slice(t0 + tt * 512, t0 + (tt + 1) * 512)
            ls = slice(tt * 512, (tt + 1) * 512)
            P.dve(lambda e, ts=ts: e.tensor_tensor(sq3, xT3[:, :, ts], xT3[:, :, ts], ALU.mult), [xT_b[g]], sq_bl)
            pss, pss_b = psum[pbank], ps_b[pbank]
            for c in range(NCH):
                P.pe(lambda e, c=c, pss=pss: e.matmul(pss, ones_bf, sq3[:, c, :], start=(c == 0), stop=(c == NCH - 1)),
                     [ones_b[0]] + sq_bl, [pss_b])
            rms_rstd(rs, pss, [pss_b], [rs_b])
            for c in range(NCH):
                P.dve(lambda e, c=c, ts=ts, ls=ls: e.scalar_tensor_tensor(
                    xn3[:, c, ls], xT3[:, c, ts], gain(l, kpre, c), rs, ALU.mult, ALU.mult),
                    [xT_b[g], rs_b, cst_b[0]], [xn_b[c * ntt + tt]])

    def ffn_phase(l, k):
        lf = l * 2 + k
        kpre, kpost = (0, 1) if k == 0 else (4, 5)
        xn, xn_b = A.alloc("xn", PH, [NCH * 1024], BF16, nbuf=16)
        xn3 = xn.rearrange("p (c t) -> p c t", c=NCH)
        actT, act_b = A.alloc("actT", PH + 16384, [NF * 1024], BF16, nbuf=NF * 2)
        act3 = actT.rearrange("p (f t) -> p f t", f=NF)
        hT, h_b = A.alloc("hT", PH + 61440, [NCH * 1024], F32, nbuf=16)
        h3 = hT.rearrange("p (c t) -> p c t", c=NCH)
        sq, sq_b = A.alloc("sq", PH + 94208, [NCH * 512], BF16, nbuf=2)
        sq3 = sq.rearrange("p (c t) -> p c t", c=NCH)
        sg, sg_b = A.alloc("sg", PH + 102400, [2 * 512], F32, nbuf=2)
        rs, rs_b = A.alloc("rs", PH + 106496, [2 * 512], F32, nbuf=2)
        tmp, tmp_b = A.alloc("tmp", PH + 110592, [2 * 512], F32, nbuf=2)
        for ps_ in range(2):
            t0 = ps_ * 1024
            pre_norm(l, kpre, 2, t0, xn3, xn_b, sq3, [sq_b[0], sq_b[1]], rs[:, 0:512], rs_b[0], 6)
            it = 0
            for f in range(NF):
                wt, wb = W.get(("w1", lf, ps_, f))
                wt4 = wt.rearrange("p (g c n) -> p g c n", g=2, c=NCH)
                for tt in range(2):
                    ls = slice(tt * 512, (tt + 1) * 512)
                    pg, pg_b = psum[(it % 2) * 2], ps_b[(it % 2) * 2]
                    pu, pu_b = psum[(it % 2) * 2 + 1], ps_b[(it % 2) * 2 + 1]
                    for c in range(NCH):
                        P.pe(lambda e, c=c, pg=pg, wt4=wt4, ls=ls: e.matmul(pg, wt4[:, 0, c, :], xn3[:, c, ls],
                                                                             start=(c == 0), stop=(c == NCH - 1)),
                             [wb, xn_b[c * 2 + tt]], [pg_b])
                    for c in range(NCH):
                        P.pe(lambda e, c=c, pu=pu, wt4=wt4, ls=ls: e.matmul(pu, wt4[:, 1, c, :], xn3[:, c, ls],
                                                                             start=(c == 0), stop=(c == NCH - 1)),
                             [wb, xn_b[c * 2 + tt]], [pu_b])
                    sgl = sg[:, (it % 2) * 512:(it % 2 + 1) * 512]
                    P.act(lambda e, sgl=sgl, pg=pg: e.activation(sgl, pg, AF.Silu), [pg_b], [sg_b[it % 2]])
                    P.dve(lambda e, f=f, ls=ls, sgl=sgl, pu=pu: e.tensor_tensor(act3[:, f, ls], sgl, pu, ALU.mult),
                          [sg_b[it % 2], pu_b], [act_b[f * 2 + tt]])
                    it += 1
            it = 0
            for m in range(NCH):
                wt, wb = W.get(("w2", lf, ps_, m))
                wt3 = wt.rearrange("p (f n) -> p f n", f=NF)
                for tt in range(2):
                    ls = slice(tt * 512, (tt + 1) * 512)
                    ph_, ph_b = psum[4 + it % 2], ps_b[4 + it % 2]
                    for f in range(NF):
                        P.pe(lambda e, f=f, ph_=ph_, wt3=wt3, ls=ls: e.matmul(ph_, wt3[:, f, :], act3[:, f, ls],
                                                                               start=(f == 0), stop=(f == NF - 1)),
                             [wb, act_b[f * 2 + tt]], [ph_b])
                    P.act(lambda e, m=m, ls=ls, ph_=ph_: e.activation(h3[:, m, ls], ph_, AF.Copy), [ph_b], [h_b[m * 2 + tt]])
                    sql = sq[:, (it % 2) * 512:(it % 2 + 1) * 512]
                    P.act(lambda e, sql=sql, ph_=ph_: e.activation(sql, ph_, AF.Square), [ph_b], [sq_b[it % 2]])
                    pss, pss_b = psum[6 + tt], ps_b[6 + tt]
                    P.pe(lambda e, m=m, pss=pss, sql=sql: e.matmul(pss, ones_bf, sql, start=(m == 0), stop=(m == NCH - 1)),
                         [ones_b[0], sq_b[it % 2]], [pss_b])
                    it += 1
            for tt in range(2):
                g = ps_ * 2 + tt
                ts = slice(t0 + tt * 512, t0 + (tt + 1) * 512)
                ls = slice(tt * 512, (tt + 1) * 512)
                rsl = rs[:, tt * 512:(tt + 1) * 512]
                rms_rstd(rsl, psum[6 + tt], [ps_b[6 + tt]], [rs_b[tt]])
                for m in range(NCH):
                    tl = tmp[:, (m % 2) * 512:(m % 2 + 1) * 512]
                    P.dve(lambda e, m=m, ls=ls, tl=tl, rsl=rsl: e.tensor_tensor(tl, h3[:, m, ls], rsl, ALU.mult),
                          [h_b[m * 2 + tt], rs_b[tt]], [tmp_b[m % 2]])
                    P.dve(lambda e, m=m, ts=ts, tl=tl: e.scalar_tensor_tensor(
                        xT3[:, m, ts], tl, gain_half(l, kpost, m), xT3[:, m, ts], ALU.mult, ALU.add),
                        [tmp_b[m % 2], cst_half_b, xT_b[g]], [xT_b[g]])
        for n in ("xn", "actT", "hT", "sq", "sg", "rs", "tmp"):
            A.free(n)

    cnt = {"ev": 0, "ps": 0}

    def evac(dst, src, reads, writes, func=None):
        if func is not None:
            P.act(lambda e: e.activation(dst, src, func), reads, writes)
        else:
            cnt["ev"] += 1
            if cnt["ev"] % 2:
                P.act(lambda e: e.activation(dst, src, AF.Copy), reads, writes)
            else:
                P.dve(lambda e: e.tensor_copy(dst, src), reads, writes)

    def mix_phase(l):
        lam_init = 0.8 - 0.6 * math.exp(-0.3 * l)
        lamv = cst[:, C_LAM + 256 * l: C_LAM + 256 * (l + 1)]
        P.dve(lambda e: e.tensor_tensor(sml[:, 8:72], lamv[:, 0:64], lamv[:, 64:128], ALU.mult), [cst_b[0]], [sml_b[0]])
        P.dve(lambda e: e.reduce_sum(sml[:, 2:3], sml[:, 8:72], axis=mybir.AxisListType.X), [sml_b[0]], [sml_b[0]])
        P.dve(lambda e: e.tensor_tensor(sml[:, 8:72], lamv[:, 128:192], lamv[:, 192:256], ALU.mult), [cst_b[0], sml_b[0]], [sml_b[0]])
        P.dve(lambda e: e.reduce_sum(sml[:, 3:4], sml[:, 8:72], axis=mybir.AxisListType.X), [sml_b[0]], [sml_b[0]])
        P.act(lambda e: e.activation(sml[:, 2:4], sml[:, 2:4], AF.Exp), [sml_b[0]], [sml_b[0]])
        P.dve(lambda e: e.tensor_tensor(sml[:, 0:1], sml[:, 3:4], sml[:, 2:3], ALU.subtract), [sml_b[0]], [sml_b[0]])
        P.dve(lambda e: e.tensor_scalar(sml[:, 0:1], sml[:, 0:1], -lam_init, None, ALU.add), [sml_b[0]], [sml_b[0]])
        P.dve(lambda e: e.tensor_scalar(sml[:, 1:2], cst[:, C_SUBG + l:C_SUBG + l + 1], 1.0 - lam_init, None, ALU.mult),
              [cst_b[0], sml_b[0]], [sml_b[0]])
        neglam, gsub = sml[:, 0:1], sml[:, 1:2]

        xn, xn_b = A.alloc("xn2", PH, [NCH * T], BF16, nbuf=NCH * 4)
        xn3 = xn.rearrange("p (c t) -> p c t", c=NCH)
        rope, rope_b = A.alloc("rope", PH + 32768, [2 * T], F32)
        rope3 = rope.rearrange("p (k t) -> p k t", k=2)
        stg, stg_b = A.alloc("stg", PH + 49152, [4 * T], BF16, nbuf=4)
        sq, sq_b = A.alloc("sq", PH + 65536, [NCH * 512], BF16)
        sq3 = sq.rearrange("p (c t) -> p c t", c=NCH)
        rs, rs_b = A.alloc("rs", PH + 73728, [512], F32)
        t12, t12_b = A.alloc("t12", PH + 75776, [4 * 512], F32, nbuf=4)
        stgg, stgg_b = A.alloc("stgg", PH + 83968, [T], BF16)
        P.dma("sp", lambda e: e.dma_start(out=rope, in_=rope_d), [], [rope_b[0]], rope_sem)
        pre_norm(l, 2, 4, 0, xn3, xn_b, sq3, [sq_b[0]], rs, rs_b[0], 7)
        stg_sem = [P.dsem(("sg", i)) for i in range(4)]
        si = 0
        pc = 0
        for ti in range(17):
            wt, wb = W.get(("wm", l, ti))
            wt4 = wt.rearrange("p (g c n) -> p g c n", g=2, c=NCH)
            if ti < 8:
                kind, h = (0, ti) if ti < 4 else (1, ti - 4)
                slot = si % 4
                si += 1
                sl = stg[:, slot * T:(slot + 1) * T]
                for tt in range(4):
                    ts = slice(tt * 512, (tt + 1) * 512)
                    ba, bb = (tt % 2) * 2, (tt % 2) * 2 + 1
                    for g_, bk in ((0, ba), (1, bb)):
                        for c in range(NCH):
                            P.pe(lambda e, c=c, g_=g_, bk=bk, wt4=wt4, ts=ts: e.matmul(
                                psum[bk], wt4[:, g_, c, :], xn3[:, c, ts], start=(c == 0), stop=(c == NCH - 1)),
                                [wb, xn_b[c * 4 + tt]], [ps_b[bk]])
                    t1 = t12[:, ba * 512:(ba + 1) * 512]
                    t2 = t12[:, bb * 512:(bb + 1) * 512]
                    P.dve(lambda e, t1=t1, ba=ba, ts=ts: e.tensor_tensor(t1, psum[ba], rope3[:, 0, ts], ALU.mult),
                          [ps_b[ba], rope_b[0]], [t12_b[ba]])
                    P.dve(lambda e, t2=t2, bb=bb, ts=ts: e.tensor_tensor(t2, psum[bb], rope3[:, 1, ts], ALU.mult),
                          [ps_b[bb], rope_b[0]], [t12_b[bb]])
                    P.dve(lambda e, t1=t1, t2=t2, sl=sl, ts=ts: e.tensor_tensor(sl[:, ts], t1, t2, ALU.add),
                          [t12_b[ba], t12_b[bb]], [stg_b[slot]])
                P.dma("sp", lambda e, kind=kind, h=h, sl=sl: e.dma_start(out=gin_kh(kind, h), in_=sl),
                      [stg_b[slot]], [gin_b[kind][h]], stg_sem[slot])
            elif ti < 16:
                kind = 2 + (ti - 8) // 2
                for g_ in range(2):
                    h = 2 * ((ti - 8) % 2) + g_
                    slot = si % 4
                    si += 1
                    sl = stg[:, slot * T:(slot + 1) * T]
                    for tt in range(4):
                        ts = slice(tt * 512, (tt + 1) * 512)
                        bk = 4 + pc % 2
                        pc += 1
                        for c in range(NCH):
                            P.pe(lambda e, c=c, g_=g_, bk=bk, wt4=wt4, ts=ts: e.matmul(
                                psum[bk], wt4[:, g_, c, :], xn3[:, c, ts], start=(c == 0), stop=(c == NCH - 1)),
                                [wb, xn_b[c * 4 + tt]], [ps_b[bk]])
                        evac(sl[:, ts], psum[bk], [ps_b[bk]], [stg_b[slot]], AF.Sigmoid if kind == 5 else None)
                    P.dma("sp", lambda e, kind=kind, h=h, sl=sl: e.dma_start(out=gin_kh(kind, h), in_=sl),
                          [stg_b[slot]], [gin_b[kind][h]], stg_sem[slot])
            else:
                slot = si % 4
                si += 1
                sl = stg[:, slot * T:(slot + 1) * T]
                for tt in range(4):
                    ts = slice(tt * 512, (tt + 1) * 512)
                    for c in range(NCH):
                        P.pe(lambda e, c=c, wt4=wt4, ts=ts: e.matmul(
                            psum[6][0:40, :], wt4[:, 0, c, 0:40], xn3[:, c, ts], start=(c == 0), stop=(c == NCH - 1)),
                            [wb, xn_b[c * 4 + tt]], [ps_b[6]])
                    P.act(lambda e, ts=ts, sl=sl: e.activation(sl[0:8, ts], psum[6][0:8, :], AF.Copy), [ps_b[6]], [stg_b[slot]])
                    P.act(lambda e, ts=ts: e.activation(stgg[32:40, ts], psum[6][32:40, :], AF.Copy), [ps_b[6]], [stgg_b[0]])
                    P.dve(lambda e, ts=ts, sl=sl: e.tensor_tensor(sl[32:40, ts], psum[6][32:40, :], stgg[32:40, ts], ALU.subtract),
                          [ps_b[6], stgg_b[0]], [stg_b[slot]])
                P.dma("sp", lambda e, sl=sl: e.dma_start(out=gin_c[12], in_=sl), [stg_b[slot]], [ging_b], stg_sem[slot])
        for n in ("xn2", "rope", "stg", "sq", "rs", "t12", "stgg"):
            A.free(n)

        if stop == "proj":
            return
        for ci_ in range(NCHK):
            rd = [ging_b] if ci_ == 12 else [gin_b[ci_ // 2][2 * (ci_ % 2)], gin_b[ci_ // 2][2 * (ci_ % 2) + 1]]
            order = [2, 3, 0, 1, 4, 5, 6, 7, 8, 9, 10, 11, 12]
            cj = order[ci_]
            rd = [ging_b] if cj == 12 else [gin_b[cj // 2][2 * (cj % 2)], gin_b[cj // 2][2 * (cj % 2) + 1]]
            P.dma("pool", lambda e, cj=cj: e.collective_compute("AllGather", ALU.bypass, replica_groups=RG,
                                                               ins=[gin_c[cj].opt()], outs=[gout_c[cj].opt()]),
                  rd, [gout_b[cj]], P.dsem(("cc1", cj)), inc=1)
        if stop == "gather":
            return
        KT, KT_b = A.alloc("KT", PH, [SEQ], BF16, nbuf=16)
        QT, QT_b = A.alloc("QT", PH + 16384, [SEQ], BF16, nbuf=16)
        VT, VT_b = A.alloc("VT", PH + 32768, [SEQ], BF16, nbuf=16)
        cand, cand_b = A.alloc("cand", PH + 49152, [3 * 2048], BF16, nbuf=3)
        cand_sem = [P.dsem(("cd", i)) for i in range(3)]
        ci = [0]

        def load_cand(kind, r, tt):
            slot = ci[0] % 3
            ci[0] += 1
            cs = cand[:, slot * 2048:(slot + 1) * 2048].rearrange("p (h t) -> p h t", h=4)
            for half in range(2):
                P.dma("sp", lambda e, cs=cs, kind=kind, r=r, tt=tt, half=half: e.dma_start(
                    out=cs[:, 2 * half:2 * half + 2, :],
                    in_=gout_c[kind * 2 + half][r * 256:(r + 1) * 256, tt * 512:(tt + 1) * 512].rearrange("(h p) t -> p h t", h=2)),
                    [gout_b[kind * 2 + half]], [cand_b[slot]], cand_sem[slot])
            return cs, cand_b[slot]

        def select_fm(kind, dst, dst_b):
            for r in range(4):
                for tt in range(4):
                    cs, cb = load_cand(kind, r, tt)
                    bk = cnt["ps"] % 2
                    cnt["ps"] += 1
                    for h in range(4):
                        P.pe(lambda e, h=h, cs=cs, bk=bk: e.matmul(psum[bk], sel_bf[:, h * 128:(h + 1) * 128], cs[:, h, :],
                                                                  start=(h == 0), stop=(h == 3)),
                             [cbf_b[0], cb], [ps_b[bk]])
                    g = r * 4 + tt
                    evac(dst[:, g * 512:(g + 1) * 512], psum[bk], [ps_b[bk]], [dst_b[g]])

        def select_tok(kind, dst, dst_b):
            for r in range(4):
                for tt in range(4):
                    cs, cb = load_cand(kind, r, tt)
                    bk = cnt["ps"] % 2
                    cnt["ps"] += 1
                    for sub in range(4):
                        for h in range(4):
                            P.pe(lambda e, h=h, sub=sub, cs=cs, bk=bk: e.matmul(
                                psum[bk][:, sub * 128:(sub + 1) * 128], cs[:, h, sub * 128:(sub + 1) * 128],
                                sel_bf[:, h * 128:(h + 1) * 128], start=(h == 0), stop=(h == 3)),
                                [cbf_b[0], cb], [ps_b[bk]])
                    g = r * 4 + tt
                    evac(dst[:, g * 512:(g + 1) * 512], psum[bk], [ps_b[bk]], [dst_b[g]])

        select_fm(1, KT, KT_b)
        select_fm(0, QT, QT_b)
        select_tok(2, VT, VT_b)

        pbuf, pbuf_b = A.alloc("pbuf", PH + 61440, [6 * 512], BF16, nbuf=6)
        rr, rr_b = A.alloc("rr", PH + 67584, [2 * 512], F32, nbuf=2)
        oo, oo_b = A.alloc("oo", PH + 71680, [2 * 512], F32, nbuf=2)
        sqa, sqa_b = A.alloc("sqa", PH + 75776, [512], BF16)
        rsa, rsa_b = A.alloc("rsa", PH + 76800, [512], F32)
        yst, yst_b = A.alloc("yst", PH + 78848, [2 * 512], BF16, nbuf=2)
        yst_sem = [P.dsem(("ys", i)) for i in range(2)]
        step = 0
        for gq in range(16):
            q0 = gq * 512
            nkt = 4 * gq + 4
            for kt in range(nkt):
                r = kt - 4 * gq
                off = 0 if r < 0 else 128 * r
                sb_ = 4 + (step % 2) * 2
                pslot = (step % 3) * 2
                step += 1
                for c in range(2):
                    P.pe(lambda e, c=c, kt=kt, off=off, sb_=sb_: e.matmul(
                        psum[sb_ + c][:, off:512], KT[c * 64:(c + 1) * 64, kt * 128:(kt + 1) * 128],
                        QT[c * 64:(c + 1) * 64, q0 + off:q0 + 512], start=True, stop=True),
                        [KT_b[kt // 4], QT_b[gq]], [ps_b[sb_ + c]])
                for c in range(2):
                    pb_ = pbuf[:, (pslot + c) * 512:(pslot + c + 1) * 512]
                    P.act(lambda e, c=c, off=off, sb_=sb_, pb_=pb_: e.activation(
                        pb_[:, off:512], psum[sb_ + c][:, off:512], AF.Exp, scale=0.125),
                        [ps_b[sb_ + c]], [pbuf_b[pslot + c]])
                    if r >= 0:
                        P.dve(lambda e, off=off, pb_=pb_: e.memset(pb_[64:128, off:off + 64], 0.0),
                              [pbuf_b[pslot + c]], [pbuf_b[pslot + c]])
                for c in range(2):
                    pb_ = pbuf[:, (pslot + c) * 512:(pslot + c + 1) * 512]
                    P.pe(lambda e, c=c, kt=kt, off=off, pb_=pb_: e.matmul(
                        psum[c][:, off:512], VT[:, kt * 128:(kt + 1) * 128], pb_[:, off:512],
                        start=(kt == 0), stop=(kt == nkt - 1)),
                        [VT_b[kt // 4], pbuf_b[pslot + c]], [ps_b[c]])
                    P.pe(lambda e, c=c, kt=kt, off=off, pb_=pb_: e.matmul(
                        psum[2 + c][:, off:512], ones_bf, pb_[:, off:512],
                        start=(kt == 0), stop=(kt == nkt - 1)),
                        [cbf_b[0], pbuf_b[pslot + c]], [ps_b[2 + c]])
            for c in range(2):
                rl = rr[:, c * 512:(c + 1) * 512]
                ol = oo[:, c * 512:(c + 1) * 512]
                P.dve(lambda e, c=c, rl=rl: e.reciprocal(rl, psum[2 + c]), [ps_b[2 + c]], [rr_b[c]])
                P.dve(lambda e, c=c, rl=rl, ol=ol: e.tensor_tensor(ol, psum[c], rl, ALU.mult), [ps_b[c], rr_b[c]], [oo_b[c]])
            o1, o2 = oo[:, 0:512], oo[:, 512:1024]
            P.dve(lambda e: e.scalar_tensor_tensor(o1, o2, neglam, o1, ALU.mult, ALU.add), [oo_b[0], oo_b[1], sml_b[0]], [oo_b[0]])
            P.act(lambda e: e.activation(sqa, o1, AF.Square), [oo_b[0]], [sqa_b[0]])
            sb_ = 4 + (step % 2) * 2
            P.pe(lambda e, sb_=sb_: e.matmul(psum[sb_], ones_bf, sqa, start=True, stop=True), [cbf_b[0], sqa_b[0]], [ps_b[sb_]])
            rms_rstd(rsa, psum[sb_], [ps_b[sb_]], [rsa_b[0]], scale=1.0 / 128)
            ys = gq % 2
            yl = yst[:, ys * 512:(ys + 1) * 512]
            P.dve(lambda e, yl=yl: e.scalar_tensor_tensor(yl, o1, gsub, rsa, ALU.mult, ALU.mult),
                  [oo_b[0], rsa_b[0], sml_b[0]], [yst_b[ys]])
            P.dma("sp", lambda e, yl=yl, q0=q0: e.dma_start(out=gin2_c[q0 // T][0:128, q0 % T:q0 % T + 512], in_=yl),
                  [yst_b[ys]], [gin2_b[0][gq]], yst_sem[ys])
        for n in ("KT", "QT", "VT", "pbuf", "rr", "oo", "sqa", "rsa", "yst"):
            A.free(n)
        A.free("cand")
        if stop == "att":
            return

        NPQ = 8208
        PQK, PQK_b = A.alloc("PQK", PH, [NPQ], BF16, nbuf=17)
        PKQ, PKQ_b = A.alloc("PKQ", PH + 16416, [NPQ], BF16, nbuf=17)
        MV, MV_b = A.alloc("MV", PH + 32832, [64 * 129], BF16, nbuf=17)
        MV3 = MV.rearrange("p (t d) -> p t d", t=64)
        MO, MO_b = A.alloc("MO", PH + 49344, [SEQ], BF16, nbuf=16)
        cand, cand_b = A.alloc("cand", PH + 65728, [3 * 2048], BF16, nbuf=3)
        GG, GG_b = A.alloc("GG", PH + 78016, [2 * 2048], BF16, nbuf=2)
        SO = PH + 86208
        gts, gts_b = A.alloc("gts", SO, [128], F32)
        gw, gw_b = A.alloc("gw", SO + 512, [6 * 64], F32)
        gwb, gwb_b = A.alloc("gwb", SO + 2048, [2 * 64], BF16)
        lfb, lfb_b = A.alloc("lfb", SO + 2304, [2 * 256], BF16, nbuf=2)
        dex, dex_b = A.alloc("dex", SO + 3328, [2 * 128], F32, nbuf=2)
        wTt, wT_b = A.alloc("wT", SO + 4352, [2 * 128], BF16, nbuf=2)
        ebt, eb_b = A.alloc("eb", SO + 4864, [2 * 128], F32, nbuf=2)
        qtt, qt_b = A.alloc("qt", SO + 5888, [2 * 128], BF16, nbuf=2)
        sct, sc_b = A.alloc("sc", SO + 6400, [2 * 8], F32, nbuf=2)
        ktt, kt_b = A.alloc("kt", SO + 6464, [2 * 64], BF16, nbuf=2)
        Cs, Cs_b = A.alloc("Cs", SO + 6720, [136], F32)
        Cbf, Cbf_b = A.alloc("Cbf", SO + 7264, [2 * 256], BF16, nbuf=2)
        rdn, rdn_b = A.alloc("rdn", SO + 8288, [2 * 128], F32, nbuf=2)
        acc, acc_b = A.alloc("acc", SO + 9312, [2 * 512], F32, nbuf=2)
        aqk, aqk_b = A.alloc("aqk", SO + 13408, [2 * 512], F32, nbuf=2)
        qkb, qkb_b = A.alloc("qkb", SO + 17504, [2 * 512], BF16, nbuf=2)
        hTm, hTm_b = A.alloc("hTm", SO + 19552, [512], F32)
        sqm, sqm_b = A.alloc("sqm", SO + 21600, [512], BF16)
        rsm, rsm_b = A.alloc("rsm", SO + 22624, [512], F32)
        ysm, ysm_b = A.alloc("ysm", SO + 24672, [2 * 512], BF16, nbuf=2)
        cand_sem = [P.dsem(("cd", i)) for i in range(3)]
        ci[0] = 0

        def load_cand2(kind, r, tt):
            slot = ci[0] % 3
            ci[0] += 1
            cs = cand[:, slot * 2048:(slot + 1) * 2048].rearrange("p (h t) -> p h t", h=4)
            for half in range(2):
                P.dma("sp", lambda e, cs=cs, kind=kind, r=r, tt=tt, half=half: e.dma_start(
                    out=cs[:, 2 * half:2 * half + 2, :],
                    in_=gout_c[kind * 2 + half][r * 256:(r + 1) * 256, tt * 512:(tt + 1) * 512].rearrange("(h p) t -> p h t", h=2)),
                    [gout_b[kind * 2 + half]], [cand_b[slot]], cand_sem[slot])
            return cs, cand_b[slot]

        P.dve(lambda e: e.memset(PQK[:, 0:3], 0.0), [], [PQK_b[16]])
        P.dve(lambda e: e.memset(PKQ[:, 0:3], 0.0), [], [PKQ_b[16]])
        swap_bf = None
        for r in range(4):
            for tt in range(4):
                g = r * 4 + tt
                cs, cb = load_cand2(3, r, tt)
                for dst, dstb, selm in ((PQK, PQK_b, 0), (PKQ, PKQ_b, 1)):
                    bk = cnt["ps"] % 2
                    cnt["ps"] += 1
                    for h in range(4):
                        lhs = sel_bf[:, h * 128:(h + 1) * 128] if selm == 0 else selsw_bf[:, h * 128:(h + 1) * 128]
                        P.pe(lambda e, h=h, cs=cs, bk=bk, lhs=lhs: e.matmul(psum[bk], lhs, cs[:, h, :], start=(h == 0), stop=(h == 3)),
                             [cbf_b[0], cb], [ps_b[bk]])
                    evac(dst[:, 3 + g * 512: 3 + (g + 1) * 512], psum[bk], [ps_b[bk]], [dstb[g]])
        for r in range(4):
            for tt in range(4):
                g = r * 4 + tt
                cs, cb = load_cand2(5, r, tt)
                bk = cnt["ps"] % 2
                cnt["ps"] += 1
                for h in range(4):
                    P.pe(lambda e, h=h, cs=cs, bk=bk: e.matmul(psum[bk], sel_bf[:, h * 128:(h + 1) * 128], cs[:, h, :],
                                                              start=(h == 0), stop=(h == 3)), [cbf_b[0], cb], [ps_b[bk]])
                evac(MO[:, g * 512:(g + 1) * 512], psum[bk], [ps_b[bk]], [MO_b[g]])
        P.dve(lambda e: e.memset(MV3[:, :, 128:129], 1.0), [], [MV_b[16]])
        for r in range(4):
            for tt in range(4):
                g = r * 4 + tt
                cs, cb = load_cand2(4, r, tt)
                bk = cnt["ps"] % 2
                cnt["ps"] += 1
                for sub in range(4):
                    for h in range(4):
                        P.pe(lambda e, h=h, sub=sub, cs=cs, bk=bk: e.matmul(
                            psum[bk][:, sub * 128:(sub + 1) * 128], cs[:, h, sub * 128:(sub + 1) * 128],
                            sel_bf[:, h * 128:(h + 1) * 128], start=(h == 0), stop=(h == 3)), [cbf_b[0], cb], [ps_b[bk]])
                evac(MV3[:, g * 4:(g + 1) * 4, 0:128], psum[bk].rearrange("p (t d) -> p t d", t=4), [ps_b[bk]], [MV_b[g]])
        gg_sem = [P.dsem(("gg", i)) for i in range(2)]
        P.dve(lambda e: e.memset(GG, 0.0), [], [GG_b[0], GG_b[1]])
        for r in range(4):
            sl_ = r % 2
            ggl = GG[:, sl_ * 2048:(sl_ + 1) * 2048]
            P.dma("sp", lambda e, ggl=ggl, r=r: e.dma_start(out=ggl[0:40, :], in_=gout_c[12][r * 128:r * 128 + 40, :]),
                  [gout_b[12]], [GG_b[sl_]], gg_sem[sl_])
            for t in range(16):
                T_ = r * 16 + t
                P.pe(lambda e, T_=T_, t=t, ggl=ggl: e.matmul(psum[3][:, 2 * T_:2 * T_ + 2], ggl[0:8, t * 128:(t + 1) * 128],
                                                            gsel_bf[0:8, :], start=True, stop=False), [GG_b[sl_], cbf_b[0]], [ps_b[3]])
                P.pe(lambda e, T_=T_, t=t, ggl=ggl: e.matmul(psum[3][:, 2 * T_:2 * T_ + 2], ggl[32:40, t * 128:(t + 1) * 128],
                                                            gsel_bf[32:40, :], start=False, stop=True), [GG_b[sl_], cbf_b[0]], [ps_b[3]])
        g3v = gts.rearrange("p (t k) -> p t k", k=2)
        p3v = psum[3][:, 0:128].rearrange("p (t k) -> p t k", k=2)
        for k_ in range(2):
            P.dve(lambda e, k_=k_: e.tensor_scalar(g3v[:, :, k_:k_ + 1], p3v[:, :, k_:k_ + 1],
                                                    cst[:, C_GB + 2 * l + k_:C_GB + 2 * l + k_ + 1], None, ALU.add),
                  [ps_b[3], cst_b[0]], [gts_b[0]])
        e1, lf, lfhf, lfl, ccv = (gw[:, i * 64:(i + 1) * 64] for i in range(5))
        lfh_bf, lfl_bf = gwb[:, 0:64], gwb[:, 64:128]
        gfv = g3v[:, :, 1:2].rearrange("p t k -> p (t k)")
        giv = g3v[:, :, 0:1].rearrange("p t k -> p (t k)")
        P.act(lambda e: e.activation(e1, gfv, AF.Exp, scale=-1.0), [gts_b[0]], [gw_b[0]])
        P.act(lambda e: e.activation(lf, e1, AF.Ln, bias=cst[:, C_ONE:C_ONE + 1], scale=1.0), [gw_b[0], cst_b[0]], [gw_b[0]])
        P.dve(lambda e: e.tensor_scalar(lf, lf, -1.0, None, ALU.mult), [gw_b[0]], [gw_b[0]])
        P.dve(lambda e: e.tensor_copy(lfh_bf, lf), [gw_b[0]], [gwb_b[0]])
        P.dve(lambda e: e.tensor_copy(lfhf, lfh_bf), [gwb_b[0]], [gw_b[0]])
        P.dve(lambda e: e.tensor_tensor(lfl, lf, lfhf, ALU.subtract), [gw_b[0]], [gw_b[0]])
        P.dve(lambda e: e.tensor_copy(lfl_bf, lfl), [gw_b[0]], [gwb_b[0]])
        P.pe(lambda e: e.matmul(psum[2][:, 0:64], tri_bf, lfh_bf, start=True, stop=False), [cbf_b[0], gwb_b[0]], [ps_b[2]])
        P.pe(lambda e: e.matmul(psum[2][:, 0:64], tri_bf, lfl_bf, start=False, stop=True), [cbf_b[0], gwb_b[0]], [ps_b[2]])
        P.dve(lambda e: e.tensor_tensor(ccv, giv, psum[2][:, 0:64], ALU.subtract), [gts_b[0], ps_b[2]], [gw_b[0]])

        P.dve(lambda e: e.memset(Cs, 0.0), [], [Cs_b[0]])
        P.dve(lambda e: e.memset(Cbf, 0.0), [], [Cbf_b[0], Cbf_b[1]])
        ysm_sem = [P.dsem(("ym", i)) for i in range(2)]
        mlg = cst[:, C_MLG + l:C_MLG + l + 1]
        for G in range(16):
            g0 = G * 512
            gs = G % 2
            for v, (src, srcb) in enumerate(((PQK, PQK_b), (PKQ, PKQ_b))):
                c0 = C_CONV + 5 * (2 * l + v)
                al = acc[:, v * 512:(v + 1) * 512]
                rdl = [srcb[G], srcb[G - 1] if G > 0 else srcb[16]]
                P.dve(lambda e, al=al, src=src, c0=c0: e.tensor_scalar(al, src[:, g0 + 3:g0 + 3 + 512], cst[:, c0 + 3:c0 + 4],
                                                                        cst[:, c0 + 4:c0 + 5], ALU.mult, ALU.add),
                      rdl + [cst_b[0]], [acc_b[v]])
                for k_ in range(3):
                    P.dve(lambda e, al=al, src=src, c0=c0, k_=k_: e.scalar_tensor_tensor(
                        al, src[:, g0 + k_:g0 + k_ + 512], cst[:, c0 + k_:c0 + k_ + 1], al, ALU.mult, ALU.add),
                        rdl + [cst_b[0], acc_b[v]], [acc_b[v]])
                aql = aqk[:, v * 512:(v + 1) * 512]
                P.act(lambda e, al=al, aql=aql: e.activation(aql, al, AF.Silu), [acc_b[v]], [aqk_b[v]])
                qkl = qkb[:, v * 512:(v + 1) * 512]
                P.dve(lambda e, aql=aql, qkl=qkl: e.tensor_copy(qkl[0:64, :], aql[0:64, :]), [aqk_b[v]], [qkb_b[v]])
            q32 = aqk[0:64, 0:512]
            qb_, kb_ = qkb[0:64, 0:512], qkb[0:64, 512:1024]
            for tt in range(4):
                T_ = G * 4 + tt
                cs_ = slice(tt * 128, (tt + 1) * 128)
                s2 = T_ % 2
                b0, b1, b2 = (0, 1, 2) if s2 == 0 else (4, 5, 6)
                lfbl = lfb[:, s2 * 256:(s2 + 1) * 256]
                P.dve(lambda e, lfbl=lfbl, T_=T_: e.tensor_scalar(lfbl[:, 0:128], ones_bf, lfhf[:, T_:T_ + 1], None, ALU.mult),
                      [cbf_b[0], gw_b[0]], [lfb_b[s2]])
                P.dve(lambda e, lfbl=lfbl, T_=T_: e.tensor_scalar(lfbl[:, 128:256], ones_bf, lfl[:, T_:T_ + 1], None, ALU.mult),
                      [cbf_b[0], gw_b[0]], [lfb_b[s2]])
                pb0, pb1 = psum[b0][:, 0:128], psum[b0][:, 128:256]
                P.pe(lambda e, pb0=pb0, lfbl=lfbl: e.matmul(pb0, lfbl[:, 0:128], tri_bf, start=True, stop=False), [lfb_b[s2], cbf_b[0]], [ps_b[b0]])
                P.pe(lambda e, pb0=pb0, lfbl=lfbl: e.matmul(pb0, lfbl[:, 128:256], tri_bf, start=False, stop=True), [lfb_b[s2], cbf_b[0]], [ps_b[b0]])
                P.pe(lambda e, pb1=pb1, lfbl=lfbl: e.matmul(pb1, lfbl[:, 0:128], tri_bf, start=True, stop=False), [lfb_b[s2], cbf_b[0]], [ps_b[b0]])
                P.pe(lambda e, pb1=pb1, lfbl=lfbl: e.matmul(pb1, lfbl[:, 128:256], tri_bf, start=False, stop=False), [lfb_b[s2], cbf_b[0]], [ps_b[b0]])
                P.pe(lambda e, pb1=pb1: e.matmul(pb1, I_bf, neg_bf, start=False, stop=True), [cbf_b[0]], [ps_b[b0]])
                dxl = dex[:, s2 * 128:(s2 + 1) * 128]
                ebl = ebt[0:64, s2 * 128:(s2 + 1) * 128]
                scl = sct[:, s2 * 8:(s2 + 1) * 8]
                P.act(lambda e, dxl=dxl, pb1=pb1, T_=T_: e.activation(dxl, pb1, AF.Exp, bias=ccv[:, T_:T_ + 1], scale=1.0),
                      [ps_b[b0], gw_b[0]], [dex_b[s2]])
                P.act(lambda e, ebl=ebl, pb0=pb0: e.activation(ebl, pb0[0:64, :], AF.Exp), [ps_b[b0]], [eb_b[s2]])
                P.act(lambda e, scl=scl, pb0=pb0, T_=T_: e.activation(scl[:, 0:1], pb0[:, 127:128], AF.Exp, bias=ccv[:, T_:T_ + 1], scale=1.0),
                      [ps_b[b0], gw_b[0]], [sc_b[s2]])
                P.act(lambda e, scl=scl, pb0=pb0: e.activation(scl[0:64, 1:2], pb0[0:64, 127:128], AF.Exp), [ps_b[b0]], [sc_b[s2]])
                P.pe(lambda e, cs_=cs_: e.matmul(psum[b1][:, 0:128], kb_[:, cs_], qb_[:, cs_], start=True, stop=True),
                     [qkb_b[0], qkb_b[1]], [ps_b[b1]])
                P.pe(lambda e, cs_=cs_: e.matmul(psum[b1][:, 128:192], kb_[:, cs_], I_bf[0:64, 0:64], start=True, stop=True),
                     [qkb_b[1], cbf_b[0]], [ps_b[b1]])
                wl = wTt[:, s2 * 128:(s2 + 1) * 128]
                ql = qtt[0:64, s2 * 128:(s2 + 1) * 128]
                kl = ktt[:, s2 * 64:(s2 + 1) * 64]
                P.dve(lambda e, wl=wl, dxl=dxl: e.scalar_tensor_tensor(wl, psum[b1][:, 0:128], 0.125, dxl, ALU.mult, ALU.mult),
                      [ps_b[b1], dex_b[s2]], [wT_b[s2]])
                P.dve(lambda e, ql=ql, ebl=ebl, cs_=cs_: e.scalar_tensor_tensor(ql, q32[:, cs_], 0.125, ebl, ALU.mult, ALU.mult),
                      [aqk_b[0], eb_b[s2]], [qt_b[s2]])
                P.dve(lambda e, kl=kl, scl=scl: e.tensor_scalar(kl, psum[b1][:, 128:192], scl[:, 0:1], None, ALU.mult),
                      [ps_b[b1], sc_b[s2]], [kt_b[s2]])
                cprev = Cbf[0:64, s2 * 256:(s2 + 1) * 256]
                cnext = Cbf[0:64, (1 - s2) * 256:(2 - s2) * 256]
                P.pe(lambda e, T_=T_, wl=wl: e.matmul(psum[b2][:, 0:128], MV3[:, T_, 0:128], wl, start=True, stop=False),
                     [MV_b[T_ // 4], MV_b[16], wT_b[s2]], [ps_b[b2]])
                P.pe(lambda e, ql=ql, cprev=cprev: e.matmul(psum[b2][:, 0:128], cprev[:, 0:128], ql, start=False, stop=True),
                     [Cbf_b[s2], qt_b[s2]], [ps_b[b2]])
                P.pe(lambda e, wl=wl: e.matmul(psum[b2][:, 128:256], ones_bf, wl, start=True, stop=False), [cbf_b[0], wT_b[s2]], [ps_b[b2]])
                P.pe(lambda e, ql=ql, cprev=cprev: e.matmul(psum[b2][:, 128:256], cprev[:, 128:256], ql, start=False, stop=True),
                     [Cbf_b[s2], qt_b[s2]], [ps_b[b2]])
                P.pe(lambda e, T_=T_, kl=kl: e.matmul(psum[b1][0:64, 256:385], kl, MV3[:, T_, :], start=True, stop=True),
                     [kt_b[s2], MV_b[T_ // 4], MV_b[16]], [ps_b[b1]])
                P.dve(lambda e, scl=scl: e.scalar_tensor_tensor(Cs[0:64, 0:129], Cs[0:64, 0:129], scl[0:64, 1:2], psum[b1][0:64, 256:385],
                                                                ALU.mult, ALU.add), [Cs_b[0], sc_b[s2], ps_b[b1]], [Cs_b[0]])
                P.dve(lambda e, cnext=cnext: e.tensor_copy(cnext[:, 0:128], Cs[0:64, 0:128]), [Cs_b[0]], [Cbf_b[1 - s2]])
                P.dve(lambda e, cnext=cnext: e.tensor_scalar(cnext[:, 128:256], ones_bf[0:64, :], Cs[0:64, 128:129], None, ALU.mult),
                      [Cs_b[0], cbf_b[0]], [Cbf_b[1 - s2]])
                rl = rdn[:, s2 * 128:(s2 + 1) * 128]
                P.act(lambda e, rl=rl: e.activation(rl, psum[b2][:, 128:256], AF.Abs), [ps_b[b2]], [rdn_b[s2]])
                P.dve(lambda e, rl=rl: e.tensor_scalar(rl, rl, 1.0, None, ALU.max), [rdn_b[s2]], [rdn_b[s2]])
                P.dve(lambda e, rl=rl: e.reciprocal(rl, rl), [rdn_b[s2]], [rdn_b[s2]])
                P.dve(lambda e, rl=rl, cs_=cs_: e.tensor_tensor(hTm[:, cs_], psum[b2][:, 0:128], rl, ALU.mult),
                      [ps_b[b2], rdn_b[s2]], [hTm_b[0]])
            P.act(lambda e: e.activation(sqm, hTm, AF.Square), [hTm_b[0]], [sqm_b[0]])
            bs = 3 if gs == 0 else 7
            P.pe(lambda e, bs=bs: e.matmul(psum[bs], ones_bf, sqm, start=True, stop=True), [cbf_b[0], sqm_b[0]], [ps_b[bs]])
            rms_rstd(rsm, psum[bs], [ps_b[bs]], [rsm_b[0]], scale=1.0 / 128)
            P.dve(lambda e: e.scalar_tensor_tensor(hTm, hTm, mlg, rsm, ALU.mult, ALU.mult), [hTm_b[0], rsm_b[0], cst_b[0]], [hTm_b[0]])
            yl = ysm[:, gs * 512:(gs + 1) * 512]
            P.dve(lambda e, yl=yl: e.tensor_tensor(yl, hTm, MO[:, g0:g0 + 512], ALU.mult), [hTm_b[0], MO_b[G]], [ysm_b[gs]])
            P.dma("sp", lambda e, yl=yl: e.dma_start(out=gin2_c[g0 // T][128:256, g0 % T:g0 % T + 512], in_=yl),
                  [ysm_b[gs]], [gin2_b[1][G]], ysm_sem[gs])
        for n in ("PQK", "PKQ", "MV", "MO", "cand", "GG", "gts", "gw", "gwb", "lfb", "dex", "wT", "eb", "qt", "sc", "kt",
                  "Cs", "Cbf", "rdn", "acc", "aqk", "qkb", "hTm", "sqm", "rsm", "ysm"):
            A.free(n)
        if stop == "mls":
            return

        for b_ in range(4):
            rd = [gin2_b[k_][b_ * 4 + i] for k_ in range(2) for i in range(4)]
            P.dma("pool", lambda e, b_=b_: e.collective_compute("AllGather", ALU.bypass, replica_groups=RG,
                                                               ins=[gin2_c[b_].opt()], outs=[gout2_c[b_].opt()]),
                  rd, [gout2_b[b_]], P.dsem(("cc2", b_)), inc=1)

        yT, yT_b = A.alloc("yT", PH, [NCH * T], BF16, nbuf=NCH * 4)
        yT3 = yT.rearrange("p (c t) -> p c t", c=NCH)
        hO, hO_b = A.alloc("hO", PH + 32768, [NCH * T], F32, nbuf=NCH * 4)
        hO3 = hO.rearrange("p (c t) -> p c t", c=NCH)
        cand, cand_b = A.alloc("cand", PH + 98304, [3 * 2048], BF16, nbuf=3)
        sqo, sqo_b = A.alloc("sqo", PH + 110592, [2 * 512], BF16, nbuf=2)
        rso, rso_b = A.alloc("rso", PH + 112640, [512], F32)
        cand_sem = [P.dsem(("cd", i)) for i in range(3)]
        ci[0] = 0
        for c8 in range(NCH):
            kind_, r = divmod(c8, 4)
            for tt in range(4):
                slot = ci[0] % 3
                ci[0] += 1
                cs = cand[:, slot * 2048:(slot + 1) * 2048].rearrange("p (h t) -> p h t", h=4)
                for b_ in range(4):
                    P.dma("sp", lambda e, cs=cs, b_=b_, r=r, kind_=kind_, tt=tt: e.dma_start(
                        out=cs[:, b_, :], in_=gout2_c[b_][r * 256 + kind_ * 128:r * 256 + kind_ * 128 + 128, tt * 512:(tt + 1) * 512]),
                        [gout2_b[b_]], [cand_b[slot]], cand_sem[slot])
                bk = cnt["ps"] % 2
                cnt["ps"] += 1
                for b_ in range(4):
                    P.pe(lambda e, b_=b_, cs=cs, bk=bk: e.matmul(psum[bk], sel_bf[:, b_ * 128:(b_ + 1) * 128], cs[:, b_, :],
                                                                start=(b_ == 0), stop=(b_ == 3)), [cbf_b[0], cand_b[slot]], [ps_b[bk]])
                evac(yT3[:, c8, tt * 512:(tt + 1) * 512], psum[bk], [ps_b[bk]], [yT_b[c8 * 4 + tt]])
        it = 0
        for m in range(NCH):
            wt, wb = W.get(("wo", l, m))
            wt3 = wt.rearrange("p (c n) -> p c n", c=NCH)
            for tt in range(4):
                ts = slice(tt * 512, (tt + 1) * 512)
                bk = 2 + it % 2
                for c in range(NCH):
                    P.pe(lambda e, c=c, bk=bk, wt3=wt3, ts=ts: e.matmul(psum[bk], wt3[:, c, :], yT3[:, c, ts],
                                                                         start=(c == 0), stop=(c == NCH - 1)),
                         [wb, yT_b[c * 4 + tt]], [ps_b[bk]])
                P.act(lambda e, m=m, ts=ts, bk=bk: e.activation(hO3[:, m, ts], psum[bk], AF.Copy), [ps_b[bk]], [hO_b[m * 4 + tt]])
                sql = sqo[:, (it % 2) * 512:(it % 2 + 1) * 512]
                P.act(lambda e, sql=sql, bk=bk: e.activation(sql, psum[bk], AF.Square), [ps_b[bk]], [sqo_b[it % 2]])
                P.pe(lambda e, m=m, tt=tt, sql=sql: e.matmul(psum[4 + tt], ones_bf, sql, start=(m == 0), stop=(m == NCH - 1)),
                     [cbf_b[0], sqo_b[it % 2]], [ps_b[4 + tt]])
                it += 1
        for tt in range(4):
            ts = slice(tt * 512, (tt + 1) * 512)
            rms_rstd(rso, psum[4 + tt], [ps_b[4 + tt]], [rso_b[0]])
            for m in range(NCH):
                P.dve(lambda e, m=m, ts=ts: e.tensor_tensor(hO3[:, m, ts], hO3[:, m, ts], rso, ALU.mult),
                      [hO_b[m * 4 + tt], rso_b[0]], [hO_b[m * 4 + tt]])
                P.dve(lambda e, m=m, ts=ts: e.scalar_tensor_tensor(xT3[:, m, ts], hO3[:, m, ts], gain(l, 3, m), xT3[:, m, ts],
                                                                   ALU.mult, ALU.add),
                      [hO_b[m * 4 + tt], cst_b[0], xT_b[tt]], [xT_b[tt]])
        for n in ("yT", "hO", "cand", "sqo", "rso"):
            A.free(n)

    for ph in phases:
        if ph[0] == "ffn":
            ffn_phase(ph[1], ph[2])
        else:
            mix_phase(ph[1])

    st = P.new_dsem("st")
    evs = []
    if stop == "att":
        for b_ in range(4):
            evs.append(P.dma("sp", lambda e, b_=b_: e.dma_start(out=dbg_d[0:128, b_ * T:(b_ + 1) * T], in_=gin2_c[b_][0:128, :]),
                             [b for b in gin2_b[0]], [], st))
    if stop == "mls":
        for b_ in range(4):
            evs.append(P.dma("sp", lambda e, b_=b_: e.dma_start(out=dbg_d[:, b_ * T:(b_ + 1) * T], in_=gin2_c[b_]),
                             [b for k_ in range(2) for b in gin2_b[k_]], [], st))
    if stop == "proj":
        evs.append(P.dma("sp", lambda e: e.dma_start(out=dbg_d[0:128, 0:T], in_=gin_kh(1, 1)), [gin_b[1][1]], [], st))
        evs.append(P.dma("sp", lambda e: e.dma_start(out=dbg_d[128:256, 0:T], in_=gin_kh(0, 2)), [gin_b[0][2]], [], st))
    if stop == "gather":
        for r in range(4):
            evs.append(P.dma("sp", lambda e, r=r: e.dma_start(out=dbg_d[0:128, r * 512:(r + 1) * 512], in_=gout_rkh(r, 1, 1)[:, 0:512]),
                             gout_b, [], st))
            evs.append(P.dma("sp", lambda e, r=r: e.dma_start(out=dbg_d[128:256, r * 512:(r + 1) * 512], in_=gout_rkh(r, 0, 2)[:, 0:512]),
                             gout_b, [], st))
    if stop is not None:
        evs.append(P.dma("sp", lambda e: e.dma_start(out=out_d, in_=xT[:, 0:16]), [xT_b[0]], [], st))
    for tt in range(4 if stop is None else 0):
        evs.append(P.dma("sp", lambda e, tt=tt: e.dma_start(
            out=out_d.rearrange("p (c t) -> p c t", c=NCH)[:, :, tt * 512:(tt + 1) * 512],
            in_=xT3[:, :, tt * 512:(tt + 1) * 512]), [xT_b[tt]], [], st))
    P.fence("sp", evs)
    P.emit()
    return nc, P


def _mix_tile_cols():
    tiles = []
    rng = np.arange(128)
    perm = (rng // 64) * 64 + ((rng % 64) + 32) % 64
    for base in (0, 512):
        for h in range(4):
            tiles.append((base + h * 128 + rng, base + h * 128 + perm))
    for h0 in (0, 2):
        tiles.append((1024 + h0 * 128 + rng, 1024 + (h0 + 1) * 128 + rng))
    r64 = np.arange(64)
    for h0 in (0, 2):
        tiles.append(tuple(np.concatenate([1536 + h * 64 + r64, 1792 + h * 64 + r64]) for h in (h0, h0 + 1)))
    for base in (2048, 2560):
        for h0 in (0, 2):
            tiles.append((base + h0 * 128 + rng, base + (h0 + 1) * 128 + rng))
    gcols = np.full(128, -1)
    gcols[:8] = 3072 + np.arange(8)
    gcols[32:40] = 3072 + np.arange(8)
    tiles.append((gcols, np.full(128, -1)))
    return tiles


def _prep_shared(inp, ffn_ids, mix_ids):
    f = np.float32
    out = {}
    w_in = np.asarray(inp["ffn_w_in"], f).reshape(DEPTH * 2, D, 2 * DFF)
    w_out = np.asarray(inp["ffn_w_out"], f).reshape(DEPTH * 2, DFF, D)
    if ffn_ids:
        wi = w_in[ffn_ids]
        n = len(ffn_ids)
        w1 = wi.reshape(n, NCH, 128, 2, NF, 128).transpose(0, 4, 2, 3, 1, 5)
        out["w1"] = np.ascontiguousarray(w1).reshape(n * NF, 128, 2 * NCH * 128)
        wo_ = w_out[ffn_ids]
        w2 = wo_.reshape(n, NF, 128, NCH, 128).transpose(0, 3, 2, 1, 4)
        out["w2"] = np.ascontiguousarray(w2).reshape(n * NCH, 128, NF * 128)
    else:
        out["w1"] = np.zeros((NF, 128, 2 * NCH * 128), f)
        out["w2"] = np.zeros((NCH, 128, NF * 128), f)
    if mix_ids:
        tiles = _mix_tile_cols()
        wm = np.zeros((len(mix_ids), 17, 128, 2, NCH, 128), f)
        wo = np.zeros((len(mix_ids), NCH, 128, NCH, 128), f)
        for i, l in enumerate(mix_ids):
            Wl = np.asarray(inp["mix_w_in"][l], f)
            Wp = np.concatenate([Wl, np.zeros((D, 1), f)], axis=1)
            for ti, groups in enumerate(tiles):
                for g, cols in enumerate(groups):
                    blk = Wp[:, cols]
                    wm[i, ti, :, g] = blk.reshape(NCH, 128, 128).transpose(1, 0, 2)
            Wo = np.asarray(inp["mix_w_out"][l], f)
            wo[i] = Wo.reshape(NCH, 128, NCH, 128).transpose(2, 1, 0, 3)
        out["wm"] = wm.reshape(len(mix_ids) * 17, 128, 2048)
        out["wo"] = wo.reshape(len(mix_ids) * NCH, 128, 1024)
    else:
        out["wm"] = np.zeros((17, 128, 2048), f)
        out["wo"] = np.zeros((NCH, 128, 1024), f)
    return out


def _prep_core(inp, core):
    f = np.float32
    b, j = divmod(core, 4)
    cst = np.zeros((128, NCST), f)
    g = np.asarray(inp["norm_gains"], f)
    cst[:, 0:96] = g.reshape(DEPTH * 6, NCH, 128).transpose(2, 0, 1).reshape(128, 96)
    cst[:, C_EPS] = EPS
    cst[:, C_ONE] = 1.0
    cw = np.asarray(inp["ml_conv_w"], f)
    cb = np.asarray(inp["ml_conv_b"], f)
    gb = np.asarray(inp["ml_gate_b"], f)
    for l in range(DEPTH):
        cst[:, C_SUBG + l] = np.asarray(inp["da_subln_g"][l], f)
        cst[:, C_MLG + l] = np.asarray(inp["ml_norm_g"][l], f)
        cst[:, C_GB + 2 * l] = gb[l, j]
        cst[:, C_GB + 2 * l + 1] = gb[l, 4 + j]
        qi = j * 64 + np.arange(64)
        ki = 256 + j * 64 + np.arange(64)
        for v, idx in enumerate((np.concatenate([qi, ki]), np.concatenate([ki, qi]))):
            c0 = C_CONV + 5 * (2 * l + v)
            cst[:, c0:c0 + 4] = cw[l][:, idx].T
            cst[:, c0 + 4] = cb[l][idx]
        cst[:, C_LAM + 256 * l: C_LAM + 256 * (l + 1)] = np.asarray(inp["da_lambda"][l], f).reshape(1, 256)
    cst[j, C_GSEL] = 1.0
    cst[4 + j, C_GSEL + 1] = 1.0
    cst[32 + j, C_GSEL] = 1.0
    cst[32 + 4 + j, C_GSEL + 1] = 1.0
    sw = np.zeros((128, 128), f)
    sw[(np.arange(128) + 64) % 128, np.arange(128)] = 1.0
    cst[:, C_SELSW + j * 128: C_SELSW + (j + 1) * 128] = sw
    cst[:, C_I:C_I + 128] = np.eye(128, dtype=f)
    ii = np.arange(128)
    cst[:, C_TRI:C_TRI + 128] = (ii[:, None] <= ii[None, :]).astype(f)
    cst[:, C_NEG:C_NEG + 128] = np.where(ii[:, None] > ii[None, :], -30000.0, 0.0).astype(f)
    cst[:, C_SEL + j * 128: C_SEL + (j + 1) * 128] = np.eye(128, dtype=f)
    pos = (j * T + np.arange(T)).astype(np.float64)
    inv = 10000.0 ** (-np.arange(0, 64, 2, dtype=np.float64) / 64.0)
    d = np.arange(128) % 64
    ang = pos[None, :] * inv[d % 32][:, None]
    sgn = np.where(d < 32, -1.0, 1.0)[:, None]
    rope = np.concatenate([np.cos(ang), sgn * np.sin(ang)], axis=1).astype(f)
    return {"cst": cst, "rope": rope}


def _shard_x(x):
    xs = []
    for c in range(8):
        b, q = divmod(c, 4)
        xc = np.asarray(x[b, q * T:(q + 1) * T, :], np.float32)
        xs.append(np.ascontiguousarray(xc.reshape(T, NCH, 128).transpose(2, 1, 0)).reshape(128, NCH * T))
    return xs


def _unshard(outs):
    y = np.zeros((2, SEQ, D), np.float32)
    for c in range(8):
        b, q = divmod(c, 4)
        o = np.asarray(outs[c], np.float32).reshape(128, NCH, T).transpose(2, 1, 0).reshape(T, D)
        y[b, q * T:(q + 1) * T, :] = o
    return y


ALL_PHASES = [("ffn", 0, 0), ("mix", 0), ("ffn", 0, 1), ("ffn", 1, 0), ("mix", 1), ("ffn", 1, 1)]


def run_phases(inputs, phases, stop=None):
    ffn_ids = sorted({ph[1] * 2 + ph[2] for ph in phases if ph[0] == "ffn"})
    mix_ids = sorted({ph[1] for ph in phases if ph[0] == "mix"})
    shared = _prep_shared(inputs, ffn_ids, mix_ids)
    xs = _shard_x(inputs["x"])
    nc, P = build_program(phases, stop=stop)
    in_maps = [dict(shared, xT=xs[c], **_prep_core(inputs, c)) for c in range(8)]
    res = run_bass_kernel_spmd(nc, in_maps, core_ids=list(range(8)))
    if stop is not None:
        return None, [np.asarray(r["dbg"]) for r in res.results]
    return _unshard([r["out"] for r in res.results])


def kernel(**inputs):
    return run_phases(inputs, ALL_PHASES)
```

```python
import numpy as np
import concourse.bass as bass
import concourse.mybir as mybir
from concourse.bass_utils import run_bass_kernel_spmd

F32 = mybir.dt.float32
BF16 = mybir.dt.bfloat16
ALU = mybir.AluOpType
AF = mybir.ActivationFunctionType

D = 1024
NCH = 8
DFF = 2816
NF = 22
DEPTH = 2
T = 2048
SEQ = 8192
EPS = 1e-6

ENGS = ("pe", "act", "dve", "pool", "sp")
EPOCH = 3000


class Buf:
    __slots__ = ("name", "w", "r")

    def __init__(self, name, inherit=None):
        self.name = name
        self.w = None
        self.r = list(inherit) if inherit else []


class Ev:
    __slots__ = ("key", "val", "clock", "op")

    def __init__(self, key, val, clock, op):
        self.key, self.val, self.clock, self.op = key, val, clock, op


class DSem:
    def __init__(self, sem):
        self.sem = sem
        self.count = 0


class Op:
    __slots__ = ("eng", "fn", "waits", "ev", "signal", "sig_no", "dsem", "inc")


class _Rec:
    def __init__(self):
        self.call = None

    def __getattr__(self, name):
        def f(*a, **k):
            assert self.call is None
            self.call = (name, a, k)
            return None
        return f


def _compress(events):
    best = {}
    for ev in events:
        cur = best.get(ev.key)
        if cur is None or cur.val < ev.val:
            best[ev.key] = ev
    return list(best.values())


class Prog:
    def __init__(self, nc):
        self.nc = nc
        self.ops = {e: [] for e in ENGS}
        self.known = {e: {} for e in ENGS}
        self.nwaits = 0
        self._nsem = 0

    def new_dsem(self, name="d"):
        self._nsem += 1
        return DSem(self.nc.alloc_semaphore(name=f"{name}_{self._nsem}"))

    def dsem(self, key):
        if not hasattr(self, "_cache"):
            self._cache = {}
        if key not in self._cache:
            self._cache[key] = self.new_dsem(str(key))
        return self._cache[key]

    def _add(self, eng, fn, reads, writes, dsem=None, inc=16, extra=()):
        op = Op()
        if fn is not None:
            rec = _Rec()
            fn(rec)
            assert rec.call is not None
            fn = rec.call
        op.eng, op.fn, op.dsem, op.signal, op.sig_no, op.inc = eng, fn, dsem, False, 0, inc
        deps = {}
        for b in reads:
            if b.w is not None:
                deps[id(b.w)] = b.w
        for b in writes:
            if b.w is not None:
                deps[id(b.w)] = b.w
            for r in b.r:
                deps[id(r)] = r
        for ev in extra:
            deps[id(ev)] = ev
        known = self.known[eng]
        waits = {}
        for ev in deps.values():
            if eng == "pe" and ev.key == "pe":
                continue
            if known.get(ev.key, 0) >= ev.val:
                continue
            cur = waits.get(ev.key)
            if cur is None or cur.val < ev.val:
                waits[ev.key] = ev
        if waits:
            known = dict(known)
            for ev in waits.values():
                for k, v in ev.clock.items():
                    if known.get(k, 0) < v:
                        known[k] = v
                if known.get(ev.key, 0) < ev.val:
                    known[ev.key] = ev.val
            self.known[eng] = known
            self.nwaits += len(waits)
        op.waits = list(waits.values())
        for ev in op.waits:
            if ev.op.dsem is None:
                ev.op.signal = True
        self.ops[eng].append(op)
        idx = len(self.ops[eng])
        if dsem is None:
            ev = Ev(eng, idx, known, op)
        else:
            dsem.count += inc
            ev = Ev(dsem, dsem.count, known, op)
        op.ev = ev
        for b in reads:
            b.r.append(ev)
        for b in writes:
            b.w = ev
            b.r = []
        return ev

    def pe(self, fn, reads, writes):
        return self._add("pe", fn, reads, writes)

    def act(self, fn, reads, writes):
        return self._add("act", fn, reads, writes)

    def dve(self, fn, reads, writes):
        return self._add("dve", fn, reads, writes)

    def dma(self, queue, fn, reads, writes, dsem, inc=16, extra=()):
        return self._add(queue, fn, reads, writes, dsem=dsem, inc=inc, extra=extra)

    def fence(self, eng, events):
        return self._add(eng, None, [], [], extra=events)

    def emit(self):
        nc = self.nc
        esems = {}
        for e in ENGS:
            n = 0
            for op in self.ops[e]:
                if op.signal:
                    n += 1
                    op.sig_no = n
            nep = max(1, (n + EPOCH - 1) // EPOCH)
            esems[e] = [nc.alloc_semaphore(name=f"prog_{e}_{i}") for i in range(nep)]

        def sem_of(ev):
            if isinstance(ev.key, DSem):
                return ev.key.sem, ev.val
            n = ev.op.sig_no
            assert n > 0
            return esems[ev.key][(n - 1) // EPOCH], (n - 1) % EPOCH + 1

        def run(eng_name, eng):
            for op in self.ops[eng_name]:
                for ev in op.waits:
                    s, v = sem_of(ev)
                    eng.wait_ge(s, v)
                if op.fn is None:
                    if op.signal:
                        s, v = sem_of(op.ev)
                        eng.nop().then_inc(s, 1)
                    continue
                name, a, k = op.fn
                ins = getattr(eng, name)(*a, **k)
                if op.dsem is not None:
                    ins.then_inc(op.dsem.sem, op.inc)
                elif op.signal:
                    s, v = sem_of(op.ev)
                    ins.then_inc(s, 1)

        with nc.Block() as block:
            @block.tensor
            def _(e):
                run("pe", e)

            @block.scalar
            def _(e):
                run("act", e)

            @block.vector
            def _(e):
                run("dve", e)

            @block.gpsimd
            def _(e):
                run("pool", e)

            @block.sync
            def _(e):
                run("sp", e)


class Arena:
    def __init__(self, nc, base, top):
        self.nc, self.base, self.top = nc, base, top
        self.live = {}
        self.dead = []
        self.n = 0

    def alloc(self, name, off, free_shape, dtype, nbuf=1):
        esz = 4 if dtype == F32 else 2
        cols = int(np.prod(free_shape))
        start = self.base + off
        end = start + cols * esz
        assert end <= self.top, (name, end, self.top)
        assert start % 32 == 0, (name, start)
        for k, (s, e, _) in self.live.items():
            assert e <= start or s >= end, f"arena overlap {name} vs {k}"
        inherit = []
        keep = []
        for (s, e, evs) in self.dead:
            if not (e <= start or s >= end):
                inherit.extend(evs)
                if s >= start and e <= end:
                    continue
            keep.append((s, e, evs))
        self.dead = keep
        inherit = _compress(inherit)
        self.n += 1
        th = self.nc.alloc_sbuf_tensor_at(f"{name}{self.n}", [128, cols], dtype, offset=start)
        bl = [Buf(f"{name}_{i}", inherit) for i in range(nbuf)]
        self.live[name] = (start, end, bl)
        return th.ap(), bl

    def free(self, name):
        s, e, bl = self.live.pop(name)
        evs = []
        for b in bl:
            if b.w is not None:
                evs.append(b.w)
            evs.extend(b.r)
        self.dead.append((s, e, _compress(evs)))


class WStream:
    NSLOT = 4
    SLOT_ELEMS = 2816

    def __init__(self, P, arena, off):
        self.P = P
        self.ap, self.bufs = arena.alloc("wring", off, [self.NSLOT * self.SLOT_ELEMS], BF16, nbuf=self.NSLOT)
        self.sems = [P.new_dsem("w") for _ in range(self.NSLOT)]
        self.tiles = []
        self.issued = 0
        self.cursor = 0

    def plan(self, tag, dram_ap, nelem):
        assert nelem <= self.SLOT_ELEMS
        self.tiles.append((tag, dram_ap, nelem))

    def _issue(self, i):
        tag, src, nelem = self.tiles[i]
        s = i % self.NSLOT
        dst = self.ap[:, s * self.SLOT_ELEMS: s * self.SLOT_ELEMS + nelem]
        self.P.dma("pool", lambda e, dst=dst, src=src: e.dma_start(out=dst, in_=src),
                   [], [self.bufs[s]], self.sems[s])

    def get(self, tag):
        while self.issued < min(len(self.tiles), self.cursor + self.NSLOT):
            self._issue(self.issued)
            self.issued += 1
        i = self.cursor
        assert self.tiles[i][0] == tag, (self.tiles[i][0], tag)
        self.cursor += 1
        s = i % self.NSLOT
        nelem = self.tiles[i][2]
        return self.ap[:, s * self.SLOT_ELEMS: s * self.SLOT_ELEMS + nelem], self.bufs[s]


NKIND = 6
RG = [[0, 1, 2, 3], [4, 5, 6, 7]]
NT = SEQ // 128
C_EPS = 200
C_SUBG = 208
C_MLG = 210
C_GB = 216
C_CONV = 220
C_GSEL = 240
C_ONE = 244
C_LAM = 256
C_I = 768
C_TRI = 896
C_NEG = 1024
C_SEL = 1152
C_SELSW = 1664
NCST = 2176


def build_program(phases, stop=None):
    import math
    nc = bass.Bass("TRN2", target_bir_lowering=False)
    P = Prog(nc)
    ffn_ids = sorted({ph[1] * 2 + ph[2] for ph in phases if ph[0] == "ffn"})
    mix_ids = sorted({ph[1] for ph in phases if ph[0] == "mix"})
    x_in = nc.dram_tensor("xT", [128, NCH * T], F32, kind="ExternalInput").ap()
    w1_d = nc.dram_tensor("w1", [max(1, len(ffn_ids)) * NF, 128, 2 * NCH * 128], F32, kind="ExternalInput").ap()
    w2_d = nc.dram_tensor("w2", [max(1, len(ffn_ids)) * NCH, 128, NF * 128], F32, kind="ExternalInput").ap()
    wm_d = nc.dram_tensor("wm", [max(1, len(mix_ids)) * 17, 128, 2048], F32, kind="ExternalInput").ap()
    wo_d = nc.dram_tensor("wo", [max(1, len(mix_ids)) * NCH, 128, 1024], F32, kind="ExternalInput").ap()
    rope_d = nc.dram_tensor("rope", [128, 2 * T], F32, kind="ExternalInput").ap()
    cst_d = nc.dram_tensor("cst", [128, NCST], F32, kind="ExternalInput").ap()
    out_d = nc.dram_tensor("out", [128, NCH * T if stop is None else 16], F32, kind="ExternalOutput").ap()
    dbg_d = None
    if stop is not None:
        dbg_d = nc.dram_tensor("dbg", [256, SEQ if stop in ("att", "mls") else 2048], BF16, kind="ExternalOutput").ap()
    NCHK = 13
    gin_c = [nc.dram_tensor(f"gin{i}", [256 if i < 12 else 128, T], BF16).ap() for i in range(NCHK)]
    gout_c = [nc.dram_tensor(f"gout{i}", [4 * (256 if i < 12 else 128), T], BF16).ap() for i in range(NCHK)]
    gin2_c = [nc.dram_tensor(f"gin2_{i}", [256, T], BF16).ap() for i in range(4)]
    gout2_c = [nc.dram_tensor(f"gout2_{i}", [4 * 256, T], BF16).ap() for i in range(4)]
    gin_b = [[Buf(f"gin{k}_{h}") for h in range(4)] for k in range(NKIND)]
    ging_b = Buf("ging")
    gout_b = [Buf(f"gout{i}") for i in range(NCHK)]
    gin2_b = [[Buf(f"gin2_{k}_{q}") for q in range(16)] for k in range(2)]
    gout2_b = [Buf(f"gout2_{i}") for i in range(4)]

    def gin_kh(kind, h):
        return gin_c[kind * 2 + h // 2][(h % 2) * 128:(h % 2) * 128 + 128, :]

    def gout_rkh(r, kind, h):
        base = r * 256 + (h % 2) * 128
        return gout_c[kind * 2 + h // 2][base:base + 128, :]

    A = Arena(nc, 16512, 229344)
    xT, xT_b = A.alloc("xT", 0, [NCH * T], F32, nbuf=4)
    xT3 = xT.rearrange("p (c t) -> p c t", c=NCH)
    W = WStream(P, A, 65536)
    CB = 88064
    cst, cst_b = A.alloc("cst", CB, [768], F32)
    cbf, cbf_b = A.alloc("cbf", CB + 3072, [1552], BF16)
    ones_bf, I_bf, tri_bf, neg_bf, sel_bf = (cbf[:, 0:128], cbf[:, 128:256], cbf[:, 256:384],
                                             cbf[:, 384:512], cbf[:, 512:1024])
    ones_b = cbf_b
    sml, sml_b = A.alloc("sml", CB + 3072 + 3104, [128], F32)
    PH = CB + 3072 + 3104 + 512
    gsel_bf = cbf[:, 1024:1026]
    selsw_bf = cbf[:, 1040:1552]
    psum = [nc.alloc_psum_tensor(f"ps{i}", [128, 512], F32).ap() for i in range(8)]
    ps_b = [Buf(f"ps{i}") for i in range(8)]
    ld = [P.new_dsem("ld") for _ in range(4)]
    misc = P.new_dsem("misc")
    rope_sem = P.new_dsem("rope")
    ging_sem = P.new_dsem("ging")

    for ph in phases:
        kind = ph[0]
        if kind == "ffn":
            _, l, k = ph
            lf = l * 2 + k
            fi = ffn_ids.index(lf)
            for ps_ in range(2):
                for f in range(NF):
                    W.plan(("w1", lf, ps_, f), w1_d[fi * NF + f], 2 * NCH * 128)
                for m in range(NCH):
                    W.plan(("w2", lf, ps_, m), w2_d[fi * NCH + m], NF * 128)
        else:
            l = ph[1]
            mi = mix_ids.index(l)
            for ti in range(17):
                W.plan(("wm", l, ti), wm_d[mi * 17 + ti], 2048)
            if stop is None:
                for m in range(NCH):
                    W.plan(("wo", l, m), wo_d[mi * NCH + m], 1024)

    for tt in range(4):
        P.dma("sp", lambda e, tt=tt: e.dma_start(out=xT3[:, :, tt * 512:(tt + 1) * 512],
                                                 in_=x_in.rearrange("p (c t) -> p c t", c=NCH)[:, :, tt * 512:(tt + 1) * 512]),
              [], [xT_b[tt]], ld[tt])
    P.dma("sp", lambda e: e.dma_start(out=cst, in_=cst_d[:, 0:768]), [], [cst_b[0]], misc)
    ctmp, ctmp_b = A.alloc("ctmp", PH, [NCST - 768], F32)
    ctmp_sem = P.new_dsem("ctmp")
    P.dma("sp", lambda e: e.dma_start(out=ctmp, in_=cst_d[:, 768:NCST]), [], [ctmp_b[0]], ctmp_sem)
    P.dve(lambda e: e.memset(ones_bf, 1.0), [], [cbf_b[0]])
    P.dve(lambda e: e.tensor_copy(cbf[:, 128:1024], ctmp[:, 0:896]), [ctmp_b[0]], [cbf_b[0]])
    P.dve(lambda e: e.tensor_copy(gsel_bf, cst[:, C_GSEL:C_GSEL + 2]), [cst_b[0]], [cbf_b[0]])
    P.dve(lambda e: e.tensor_copy(selsw_bf, ctmp[:, C_SELSW - 768:C_SELSW - 768 + 512]), [ctmp_b[0]], [cbf_b[0]])
    A.free("ctmp")

    eps_col = cst[:, C_EPS:C_EPS + 1]

    def gain(l, k, c):
        i = (l * 6 + k) * NCH + c
        return cst[:, i:i + 1]

    def gain_half(l, k, c):
        i = 96 + (l * 6 + k) * NCH + c
        return cst[:, i:i + 1]

    cst_half_b = Buf("csthalf")
    P.dve(lambda e: e.tensor_scalar(cst[:, 96:192], cst[:, 0:96], 0.5, None, ALU.mult), [cst_b[0]], [cst_half_b])

    def rms_rstd(dst, src_ps, reads, writes, scale=1.0 / D):
        P.act(lambda e: e.activation(dst, src_ps, AF.Sqrt, bias=eps_col, scale=scale), reads + [cst_b[0]], writes)
        P.dve(lambda e: e.reciprocal(dst, dst), writes, writes)

    def pre_norm(l, kpre, ntt, t0, xn3, xn_b, sq3, sq_bl, rs, rs_b, pbank):
        for tt in range(ntt):
            g = (t0 + tt * 512) // 512
            ts = slice(t0 + tt * 512, t0 + (tt + 1) * 512)
            ls = slice(tt * 512, (tt + 1) * 512)
            P.dve(lambda e, ts=ts: e.tensor_tensor(sq3, xT3[:, :, ts], xT3[:, :, ts], ALU.mult), [xT_b[g]], sq_bl)
            pss, pss_b = psum[pbank], ps_b[pbank]
            for c in range(NCH):
                P.pe(lambda e, c=c, pss=pss: e.matmul(pss, ones_bf, sq3[:, c, :], start=(c == 0), stop=(c == NCH - 1)),
                     [ones_b[0]] + sq_bl, [pss_b])
            rms_rstd(rs, pss, [pss_b], [rs_b])
            for c in range(NCH):
                P.dve(lambda e, c=c, ts=ts, ls=ls: e.scalar_tensor_tensor(
                    xn3[:, c, ls], xT3[:, c, ts], gain(l, kpre, c), rs, ALU.mult, ALU.mult),
                    [xT_b[g], rs_b, cst_b[0]], [xn_b[c * ntt + tt]])

    def ffn_phase(l, k):
        lf = l * 2 + k
        kpre, kpost = (0, 1) if k == 0 else (4, 5)
        xn, xn_b = A.alloc("xn", PH, [NCH * 1024], BF16, nbuf=16)
        xn3 = xn.rearrange("p (c t) -> p c t", c=NCH)
        actT, act_b = A.alloc("actT", PH + 16384, [NF * 1024], BF16, nbuf=NF * 2)
        act3 = actT.rearrange("p (f t) -> p f t", f=NF)
        hT, h_b = A.alloc("hT", PH + 61440, [NCH * 1024], F32, nbuf=16)
        h3 = hT.rearrange("p (c t) -> p c t", c=NCH)
        sq, sq_b = A.alloc("sq", PH + 94208, [NCH * 512], BF16, nbuf=2)
        sq3 = sq.rearrange("p (c t) -> p c t", c=NCH)
        sg, sg_b = A.alloc("sg", PH + 102400, [2 * 512], F32, nbuf=2)
        rs, rs_b = A.alloc("rs", PH + 106496, [2 * 512], F32, nbuf=2)
        tmp, tmp_b = A.alloc("tmp", PH + 110592, [2 * 512], F32, nbuf=2)
        for ps_ in range(2):
            t0 = ps_ * 1024
            pre_norm(l, kpre, 2, t0, xn3, xn_b, sq3, [sq_b[0], sq_b[1]], rs[:, 0:512], rs_b[0], 6)
            it = 0
            for f in range(NF):
                wt, wb = W.get(("w1", lf, ps_, f))
                wt4 = wt.rearrange("p (g c n) -> p g c n", g=2, c=NCH)
                for tt in range(2):
                    ls = slice(tt * 512, (tt + 1) * 512)
                    pg, pg_b = psum[(it % 2) * 2], ps_b[(it % 2) * 2]
                    pu, pu_b = psum[(it % 2) * 2 + 1], ps_b[(it % 2) * 2 + 1]
                    for c in range(NCH):
                        P.pe(lambda e, c=c, pg=pg, wt4=wt4, ls=ls: e.matmul(pg, wt4[:, 0, c, :], xn3[:, c, ls],
                                                                             start=(c == 0), stop=(c == NCH - 1)),
                             [wb, xn_b[c * 2 + tt]], [pg_b])
                    for c in range(NCH):
                        P.pe(lambda e, c=c, pu=pu, wt4=wt4, ls=ls: e.matmul(pu, wt4[:, 1, c, :], xn3[:, c, ls],
                                                                             start=(c == 0), stop=(c == NCH - 1)),
                             [wb, xn_b[c * 2 + tt]], [pu_b])
                    sgl = sg[:, (it % 2) * 512:(it % 2 + 1) * 512]
                    P.act(lambda e, sgl=sgl, pg=pg: e.activation(sgl, pg, AF.Silu), [pg_b], [sg_b[it % 2]])
                    P.dve(lambda e, f=f, ls=ls, sgl=sgl, pu=pu: e.tensor_tensor(act3[:, f, ls], sgl, pu, ALU.mult),
                          [sg_b[it % 2], pu_b], [act_b[f * 2 + tt]])
                    it += 1
            it = 0
            for m in range(NCH):
                wt, wb = W.get(("w2", lf, ps_, m))
                wt3 = wt.rearrange("p (f n) -> p f n", f=NF)
                for tt in range(2):
                    ls = slice(tt * 512, (tt + 1) * 512)
                    ph_, ph_b = psum[4 + it % 2], ps_b[4 + it % 2]
                    for f in range(NF):
                        P.pe(lambda e, f=f, ph_=ph_, wt3=wt3, ls=ls: e.matmul(ph_, wt3[:, f, :], act3[:, f, ls],
                                                                               start=(f == 0), stop=(f == NF - 1)),
                             [wb, act_b[f * 2 + tt]], [ph_b])
                    P.act(lambda e, m=m, ls=ls, ph_=ph_: e.activation(h3[:, m, ls], ph_, AF.Copy), [ph_b], [h_b[m * 2 + tt]])
                    sql = sq[:, (it % 2) * 512:(it % 2 + 1) * 512]
                    P.act(lambda e, sql=sql, ph_=ph_: e.activation(sql, ph_, AF.Square), [ph_b], [sq_b[it % 2]])
                    pss, pss_b = psum[6 + tt], ps_b[6 + tt]
                    P.pe(lambda e, m=m, pss=pss, sql=sql: e.matmul(pss, ones_bf, sql, start=(m == 0), stop=(m == NCH - 1)),
                         [ones_b[0], sq_b[it % 2]], [pss_b])
                    it += 1
            for tt in range(2):
                g = ps_ * 2 + tt
                ts = slice(t0 + tt * 512, t0 + (tt + 1) * 512)
                ls = slice(tt * 512, (tt + 1) * 512)
                rsl = rs[:, tt * 512:(tt + 1) * 512]
                rms_rstd(rsl, psum[6 + tt], [ps_b[6 + tt]], [rs_b[tt]])
                for m in range(NCH):
                    tl = tmp[:, (m % 2) * 512:(m % 2 + 1) * 512]
                    P.dve(lambda e, m=m, ls=ls, tl=tl, rsl=rsl: e.tensor_tensor(tl, h3[:, m, ls], rsl, ALU.mult),
                          [h_b[m * 2 + tt], rs_b[tt]], [tmp_b[m % 2]])
                    P.dve(lambda e, m=m, ts=ts, tl=tl: e.scalar_tensor_tensor(
                        xT3[:, m, ts], tl, gain_half(l, kpost, m), xT3[:, m, ts], ALU.mult, ALU.add),
                        [tmp_b[m % 2], cst_half_b, xT_b[g]], [xT_b[g]])
        for n in ("xn", "actT", "hT", "sq", "sg", "rs", "tmp"):
            A.free(n)

    cnt = {"ev": 0, "ps": 0}

    def evac(dst, src, reads, writes, func=None):
        if func is not None:
            P.act(lambda e: e.activation(dst, src, func), reads, writes)
        else:
            cnt["ev"] += 1
            if cnt["ev"] % 2:
                P.act(lambda e: e.activation(dst, src, AF.Copy), reads, writes)
            else:
                P.dve(lambda e: e.tensor_copy(dst, src), reads, writes)

    def mix_phase(l):
        lam_init = 0.8 - 0.6 * math.exp(-0.3 * l)
        lamv = cst[:, C_LAM + 256 * l: C_LAM + 256 * (l + 1)]
        P.dve(lambda e: e.tensor_tensor(sml[:, 8:72], lamv[:, 0:64], lamv[:, 64:128], ALU.mult), [cst_b[0]], [sml_b[0]])
        P.dve(lambda e: e.reduce_sum(sml[:, 2:3], sml[:, 8:72], axis=mybir.AxisListType.X), [sml_b[0]], [sml_b[0]])
        P.dve(lambda e: e.tensor_tensor(sml[:, 8:72], lamv[:, 128:192], lamv[:, 192:256], ALU.mult), [cst_b[0], sml_b[0]], [sml_b[0]])
        P.dve(lambda e: e.reduce_sum(sml[:, 3:4], sml[:, 8:72], axis=mybir.AxisListType.X), [sml_b[0]], [sml_b[0]])
        P.act(lambda e: e.activation(sml[:, 2:4], sml[:, 2:4], AF.Exp), [sml_b[0]], [sml_b[0]])
        P.dve(lambda e: e.tensor_tensor(sml[:, 0:1], sml[:, 3:4], sml[:, 2:3], ALU.subtract), [sml_b[0]], [sml_b[0]])
        P.dve(lambda e: e.tensor_scalar(sml[:, 0:1], sml[:, 0:1], -lam_init, None, ALU.add), [sml_b[0]], [sml_b[0]])
        P.dve(lambda e: e.tensor_scalar(sml[:, 1:2], cst[:, C_SUBG + l:C_SUBG + l + 1], 1.0 - lam_init, None, ALU.mult),
              [cst_b[0], sml_b[0]], [sml_b[0]])
        neglam, gsub = sml[:, 0:1], sml[:, 1:2]

        xn, xn_b = A.alloc("xn2", PH, [NCH * T], BF16, nbuf=NCH * 4)
        xn3 = xn.rearrange("p (c t) -> p c t", c=NCH)
        rope, rope_b = A.alloc("rope", PH + 32768, [2 * T], F32)
        rope3 = rope.rearrange("p (k t) -> p k t", k=2)
        stg, stg_b = A.alloc("stg", PH + 49152, [4 * T], BF16, nbuf=4)
        sq, sq_b = A.alloc("sq", PH + 65536, [NCH * 512], BF16)
        sq3 = sq.rearrange("p (c t) -> p c t", c=NCH)
        rs, rs_b = A.alloc("rs", PH + 73728, [512], F32)
        t12, t12_b = A.alloc("t12", PH + 75776, [4 * 512], F32, nbuf=4)
        stgg, stgg_b = A.alloc("stgg", PH + 83968, [T], BF16)
        P.dma("sp", lambda e: e.dma_start(out=rope, in_=rope_d), [], [rope_b[0]], rope_sem)
        pre_norm(l, 2, 4, 0, xn3, xn_b, sq3, [sq_b[0]], rs, rs_b[0], 7)
        stg_sem = [P.dsem(("sg", i)) for i in range(4)]
        si = 0
        pc = 0
        for ti in range(17):
            wt, wb = W.get(("wm", l, ti))
            wt4 = wt.rearrange("p (g c n) -> p g c n", g=2, c=NCH)
            if ti < 8:
                kind, h = (0, ti) if ti < 4 else (1, ti - 4)
                slot = si % 4
                si += 1
                sl = stg[:, slot * T:(slot + 1) * T]
                for tt in range(4):
                    ts = slice(tt * 512, (tt + 1) * 512)
                    ba, bb = (tt % 2) * 2, (tt % 2) * 2 + 1
                    for g_, bk in ((0, ba), (1, bb)):
                        for c in range(NCH):
                            P.pe(lambda e, c=c, g_=g_, bk=bk, wt4=wt4, ts=ts: e.matmul(
                                psum[bk], wt4[:, g_, c, :], xn3[:, c, ts], start=(c == 0), stop=(c == NCH - 1)),
                                [wb, xn_b[c * 4 + tt]], [ps_b[bk]])
                    t1 = t12[:, ba * 512:(ba + 1) * 512]
                    t2 = t12[:, bb * 512:(bb + 1) * 512]
                    P.dve(lambda e, t1=t1, ba=ba, ts=ts: e.tensor_tensor(t1, psum[ba], rope3[:, 0, ts], ALU.mult),
                          [ps_b[ba], rope_b[0]], [t12_b[ba]])
                    P.dve(lambda e, t2=t2, bb=bb, ts=ts: e.tensor_tensor(t2, psum[bb], rope3[:, 1, ts], ALU.mult),
                          [ps_b[bb], rope_b[0]], [t12_b[bb]])
                    P.dve(lambda e, t1=t1, t2=t2, sl=sl, ts=ts: e.tensor_tensor(sl[:, ts], t1, t2, ALU.add),
                          [t12_b[ba], t12_b[bb]], [stg_b[slot]])
                P.dma("sp", lambda e, kind=kind, h=h, sl=sl: e.dma_start(out=gin_kh(kind, h), in_=sl),
                      [stg_b[slot]], [gin_b[kind][h]], stg_sem[slot])
            elif ti < 16:
                kind = 2 + (ti - 8) // 2
                for g_ in range(2):
                    h = 2 * ((ti - 8) % 2) + g_
                    slot = si % 4
                    si += 1
                    sl = stg[:, slot * T:(slot + 1) * T]
                    for tt in range(4):
                        ts = slice(tt * 512, (tt + 1) * 512)
                        bk = 4 + pc % 2
                        pc += 1
                        for c in range(NCH):
                            P.pe(lambda e, c=c, g_=g_, bk=bk, wt4=wt4, ts=ts: e.matmul(
                                psum[bk], wt4[:, g_, c, :], xn3[:, c, ts], start=(c == 0), stop=(c == NCH - 1)),
                                [wb, xn_b[c * 4 + tt]], [ps_b[bk]])
                        evac(sl[:, ts], psum[bk], [ps_b[bk]], [stg_b[slot]], AF.Sigmoid if kind == 5 else None)
                    P.dma("sp", lambda e, kind=kind, h=h, sl=sl: e.dma_start(out=gin_kh(kind, h), in_=sl),
                          [stg_b[slot]], [gin_b[kind][h]], stg_sem[slot])
            else:
                slot = si % 4
                si += 1
                sl = stg[:, slot * T:(slot + 1) * T]
                for tt in range(4):
                    ts = slice(tt * 512, (tt + 1) * 512)
                    for c in range(NCH):
                        P.pe(lambda e, c=c, wt4=wt4, ts=ts: e.matmul(
                            psum[6][0:40, :], wt4[:, 0, c, 0:40], xn3[:, c, ts], start=(c == 0), stop=(c == NCH - 1)),
                            [wb, xn_b[c * 4 + tt]], [ps_b[6]])
                    P.act(lambda e, ts=ts, sl=sl: e.activation(sl[0:8, ts], psum[6][0:8, :], AF.Copy), [ps_b[6]], [stg_b[slot]])
                    P.act(lambda e, ts=ts: e.activation(stgg[32:40, ts], psum[6][32:40, :], AF.Copy), [ps_b[6]], [stgg_b[0]])
                    P.dve(lambda e, ts=ts, sl=sl: e.tensor_tensor(sl[32:40, ts], psum[6][32:40, :], stgg[32:40, ts], ALU.subtract),
                          [ps_b[6], stgg_b[0]], [stg_b[slot]])
                P.dma("sp", lambda e, sl=sl: e.dma_start(out=gin_c[12], in_=sl), [stg_b[slot]], [ging_b], stg_sem[slot])
        for n in ("xn2", "rope", "stg", "sq", "rs", "t12", "stgg"):
            A.free(n)

        if stop == "proj":
            return
        for ci_ in range(NCHK):
            rd = [ging_b] if ci_ == 12 else [gin_b[ci_ // 2][2 * (ci_ % 2)], gin_b[ci_ // 2][2 * (ci_ % 2) + 1]]
            order = [2, 3, 0, 1, 4, 5, 6, 7, 8, 9, 10, 11, 12]
            cj = order[ci_]
            rd = [ging_b] if cj == 12 else [gin_b[cj // 2][2 * (cj % 2)], gin_b[cj // 2][2 * (cj % 2) + 1]]
            P.dma("pool", lambda e, cj=cj: e.collective_compute("AllGather", ALU.bypass, replica_groups=RG,
                                                               ins=[gin_c[cj].opt()], outs=[gout_c[cj].opt()]),
                  rd, [gout_b[cj]], P.dsem(("cc1", cj)), inc=1)
        if stop == "gather":
            return
        KT, KT_b = A.alloc("KT", PH, [SEQ], BF16, nbuf=16)
        QT, QT_b = A.alloc("QT", PH + 16384, [SEQ], BF16, nbuf=16)
        VT, VT_b = A.alloc("VT", PH + 32768, [SEQ], BF16, nbuf=16)
        cand, cand_b = A.alloc("cand", PH + 49152, [3 * 2048], BF16, nbuf=3)
        cand_sem = [P.dsem(("cd", i)) for i in range(3)]
        ci = [0]

        def load_cand(kind, r, tt):
            slot = ci[0] % 3
            ci[0] += 1
            cs = cand[:, slot * 2048:(slot + 1) * 2048].rearrange("p (h t) -> p h t", h=4)
            for half in range(2):
                P.dma("sp", lambda e, cs=cs, kind=kind, r=r, tt=tt, half=half: e.dma_start(
                    out=cs[:, 2 * half:2 * half + 2, :],
                    in_=gout_c[kind * 2 + half][r * 256:(r + 1) * 256, tt * 512:(tt + 1) * 512].rearrange("(h p) t -> p h t", h=2)),
                    [gout_b[kind * 2 + half]], [cand_b[slot]], cand_sem[slot])
            return cs, cand_b[slot]

        def select_fm(kind, dst, dst_b):
            for r in range(4):
                for tt in range(4):
                    cs, cb = load_cand(kind, r, tt)
                    bk = cnt["ps"] % 2
                    cnt["ps"] += 1
                    for h in range(4):
                        P.pe(lambda e, h=h, cs=cs, bk=bk: e.matmul(psum[bk], sel_bf[:, h * 128:(h + 1) * 128], cs[:, h, :],
                                                                  start=(h == 0), stop=(h == 3)),
                             [cbf_b[0], cb], [ps_b[bk]])
                    g = r * 4 + tt
                    evac(dst[:, g * 512:(g + 1) * 512], psum[bk], [ps_b[bk]], [dst_b[g]])

        def select_tok(kind, dst, dst_b):
            for r in range(4):
                for tt in range(4):
                    cs, cb = load_cand(kind, r, tt)
                    bk = cnt["ps"] % 2
                    cnt["ps"] += 1
                    for sub in range(4):
                        for h in range(4):
                            P.pe(lambda e, h=h, sub=sub, cs=cs, bk=bk: e.matmul(
                                psum[bk][:, sub * 128:(sub + 1) * 128], cs[:, h, sub * 128:(sub + 1) * 128],
                                sel_bf[:, h * 128:(h + 1) * 128], start=(h == 0), stop=(h == 3)),
                                [cbf_b[0], cb], [ps_b[bk]])
                    g = r * 4 + tt
                    evac(dst[:, g * 512:(g + 1) * 512], psum[bk], [ps_b[bk]], [dst_b[g]])

        select_fm(1, KT, KT_b)
        select_fm(0, QT, QT_b)
        select_tok(2, VT, VT_b)

        pbuf, pbuf_b = A.alloc("pbuf", PH + 61440, [6 * 512], BF16, nbuf=6)
        rr, rr_b = A.alloc("rr", PH + 67584, [2 * 512], F32, nbuf=2)
        oo, oo_b = A.alloc("oo", PH + 71680, [2 * 512], F32, nbuf=2)
        sqa, sqa_b = A.alloc("sqa", PH + 75776, [512], BF16)
        rsa, rsa_b = A.alloc("rsa", PH + 76800, [512], F32)
        yst, yst_b = A.alloc("yst", PH + 78848, [2 * 512], BF16, nbuf=2)
        yst_sem = [P.dsem(("ys", i)) for i in range(2)]
        steps = [(gq, kt) for gq in range(16) for kt in range(4 * gq + 4)]
        nst = len(steps)

        def att_qk(i):
            gq, kt = steps[i]
            q0 = gq * 512
            r = kt - 4 * gq
            off = 0 if r < 0 else 128 * r
            sb_ = 4 + (i % 2) * 2
            pslot = (i % 3) * 2
            for c in range(2):
                P.pe(lambda e, c=c: e.matmul(
                    psum[sb_ + c][:, off:512], KT[c * 64:(c + 1) * 64, kt * 128:(kt + 1) * 128],
                    QT[c * 64:(c + 1) * 64, q0 + off:q0 + 512], start=True, stop=True),
                    [KT_b[kt // 4], QT_b[gq]], [ps_b[sb_ + c]])
            for c in range(2):
                pb_ = pbuf[:, (pslot + c) * 512:(pslot + c + 1) * 512]
                P.act(lambda e, c=c, pb_=pb_: e.activation(
                    pb_[:, off:512], psum[sb_ + c][:, off:512], AF.Exp, scale=0.125),
                    [ps_b[sb_ + c]], [pbuf_b[pslot + c]])
                if r >= 0:
                    P.dve(lambda e, pb_=pb_: e.memset(pb_[64:128, off:off + 64], 0.0),
                          [pbuf_b[pslot + c]], [pbuf_b[pslot + c]])

        def att_pv(i):
            gq, kt = steps[i]
            q0 = gq * 512
            nkt = 4 * gq + 4
            r = kt - 4 * gq
            off = 0 if r < 0 else 128 * r
            pslot = (i % 3) * 2
            for c in range(2):
                pb_ = pbuf[:, (pslot + c) * 512:(pslot + c + 1) * 512]
                P.pe(lambda e, c=c, pb_=pb_: e.matmul(
                    psum[c][:, off:512], VT[:, kt * 128:(kt + 1) * 128], pb_[:, off:512],
                    start=(kt == 0), stop=(kt == nkt - 1)),
                    [VT_b[kt // 4], pbuf_b[pslot + c]], [ps_b[c]])
                P.pe(lambda e, c=c, pb_=pb_: e.matmul(
                    psum[2 + c][:, off:512], ones_bf, pb_[:, off:512],
                    start=(kt == 0), stop=(kt == nkt - 1)),
                    [cbf_b[0], pbuf_b[pslot + c]], [ps_b[2 + c]])
            if kt != nkt - 1:
                return
            for c in range(2):
                rl = rr[:, c * 512:(c + 1) * 512]
                ol = oo[:, c * 512:(c + 1) * 512]
                P.dve(lambda e, c=c, rl=rl: e.reciprocal(rl, psum[2 + c]), [ps_b[2 + c]], [rr_b[c]])
                P.dve(lambda e, c=c, rl=rl, ol=ol: e.tensor_tensor(ol, psum[c], rl, ALU.mult), [ps_b[c], rr_b[c]], [oo_b[c]])
            o1, o2 = oo[:, 0:512], oo[:, 512:1024]
            P.dve(lambda e: e.scalar_tensor_tensor(o1, o2, neglam, o1, ALU.mult, ALU.add), [oo_b[0], oo_b[1], sml_b[0]], [oo_b[0]])
            P.act(lambda e: e.activation(sqa, o1, AF.Square), [oo_b[0]], [sqa_b[0]])
            sb_ = 4 + (i % 2) * 2
            P.pe(lambda e: e.matmul(psum[sb_], ones_bf, sqa, start=True, stop=True), [cbf_b[0], sqa_b[0]], [ps_b[sb_]])
            rms_rstd(rsa, psum[sb_], [ps_b[sb_]], [rsa_b[0]], scale=1.0 / 128)
            ys = gq % 2
            yl = yst[:, ys * 512:(ys + 1) * 512]
            P.dve(lambda e: e.scalar_tensor_tensor(yl, o1, gsub, rsa, ALU.mult, ALU.mult),
                  [oo_b[0], rsa_b[0], sml_b[0]], [yst_b[ys]])
            P.dma("sp", lambda e: e.dma_start(out=gin2_c[q0 // T][0:128, q0 % T:q0 % T + 512], in_=yl),
                  [yst_b[ys]], [gin2_b[0][gq]], yst_sem[ys])

        for i in range(nst + 1):
            if i < nst:
                att_qk(i)
            if i >= 1:
                att_pv(i - 1)
        for n in ("KT", "QT", "VT", "pbuf", "rr", "oo", "sqa", "rsa", "yst"):
            A.free(n)
        A.free("cand")
        if stop == "att":
            return

        NPQ = 8208
        PQK, PQK_b = A.alloc("PQK", PH, [NPQ], BF16, nbuf=17)
        PKQ, PKQ_b = A.alloc("PKQ", PH + 16416, [NPQ], BF16, nbuf=17)
        MV, MV_b = A.alloc("MV", PH + 32832, [64 * 129], BF16, nbuf=17)
        MV3 = MV.rearrange("p (t d) -> p t d", t=64)
        MO, MO_b = A.alloc("MO", PH + 49344, [SEQ], BF16, nbuf=16)
        cand, cand_b = A.alloc("cand", PH + 65728, [3 * 2048], BF16, nbuf=3)
        GG, GG_b = A.alloc("GG", PH + 78016, [2 * 2048], BF16, nbuf=2)
        SO = PH + 86208
        gts, gts_b = A.alloc("gts", SO, [128], F32)
        gw, gw_b = A.alloc("gw", SO + 512, [6 * 64], F32)
        gwb, gwb_b = A.alloc("gwb", SO + 2048, [2 * 64], BF16)
        lfb, lfb_b = A.alloc("lfb", SO + 2304, [2 * 256], BF16, nbuf=2)
        dex, dex_b = A.alloc("dex", SO + 3328, [2 * 128], F32, nbuf=2)
        wTt, wT_b = A.alloc("wT", SO + 4352, [2 * 128], BF16, nbuf=2)
        ebt, eb_b = A.alloc("eb", SO + 4864, [2 * 128], F32, nbuf=2)
        qtt, qt_b = A.alloc("qt", SO + 5888, [2 * 128], BF16, nbuf=2)
        sct, sc_b = A.alloc("sc", SO + 6400, [2 * 8], F32, nbuf=2)
        ktt, kt_b = A.alloc("kt", SO + 6464, [2 * 64], BF16, nbuf=2)
        Cs, Cs_b = A.alloc("Cs", SO + 6720, [136], F32)
        Cbf, Cbf_b = A.alloc("Cbf", SO + 7264, [2 * 256], BF16, nbuf=2)
        rdn, rdn_b = A.alloc("rdn", SO + 8288, [2 * 128], F32, nbuf=2)
        acc, acc_b = A.alloc("acc", SO + 9312, [2 * 512], F32, nbuf=2)
        aqk, aqk_b = A.alloc("aqk", SO + 13408, [2 * 512], F32, nbuf=2)
        qkb, qkb_b = A.alloc("qkb", SO + 17504, [2 * 512], BF16, nbuf=2)
        hTm, hTm_b = A.alloc("hTm", SO + 19552, [512], F32)
        sqm, sqm_b = A.alloc("sqm", SO + 21600, [512], BF16)
        rsm, rsm_b = A.alloc("rsm", SO + 22624, [512], F32)
        ysm, ysm_b = A.alloc("ysm", SO + 24672, [2 * 512], BF16, nbuf=2)
        cand_sem = [P.dsem(("cd", i)) for i in range(3)]
        ci[0] = 0

        def load_cand2(kind, r, tt):
            slot = ci[0] % 3
            ci[0] += 1
            cs = cand[:, slot * 2048:(slot + 1) * 2048].rearrange("p (h t) -> p h t", h=4)
            for half in range(2):
                P.dma("sp", lambda e, cs=cs, kind=kind, r=r, tt=tt, half=half: e.dma_start(
                    out=cs[:, 2 * half:2 * half + 2, :],
                    in_=gout_c[kind * 2 + half][r * 256:(r + 1) * 256, tt * 512:(tt + 1) * 512].rearrange("(h p) t -> p h t", h=2)),
                    [gout_b[kind * 2 + half]], [cand_b[slot]], cand_sem[slot])
            return cs, cand_b[slot]

        P.dve(lambda e: e.memset(PQK[:, 0:3], 0.0), [], [PQK_b[16]])
        P.dve(lambda e: e.memset(PKQ[:, 0:3], 0.0), [], [PKQ_b[16]])
        swap_bf = None
        for r in range(4):
            for tt in range(4):
                g = r * 4 + tt
                cs, cb = load_cand2(3, r, tt)
                for dst, dstb, selm in ((PQK, PQK_b, 0), (PKQ, PKQ_b, 1)):
                    bk = cnt["ps"] % 2
                    cnt["ps"] += 1
                    for h in range(4):
                        lhs = sel_bf[:, h * 128:(h + 1) * 128] if selm == 0 else selsw_bf[:, h * 128:(h + 1) * 128]
                        P.pe(lambda e, h=h, cs=cs, bk=bk, lhs=lhs: e.matmul(psum[bk], lhs, cs[:, h, :], start=(h == 0), stop=(h == 3)),
                             [cbf_b[0], cb], [ps_b[bk]])
                    evac(dst[:, 3 + g * 512: 3 + (g + 1) * 512], psum[bk], [ps_b[bk]], [dstb[g]])
        for r in range(4):
            for tt in range(4):
                g = r * 4 + tt
                cs, cb = load_cand2(5, r, tt)
                bk = cnt["ps"] % 2
                cnt["ps"] += 1
                for h in range(4):
                    P.pe(lambda e, h=h, cs=cs, bk=bk: e.matmul(psum[bk], sel_bf[:, h * 128:(h + 1) * 128], cs[:, h, :],
                                                              start=(h == 0), stop=(h == 3)), [cbf_b[0], cb], [ps_b[bk]])
                evac(MO[:, g * 512:(g + 1) * 512], psum[bk], [ps_b[bk]], [MO_b[g]])
        P.dve(lambda e: e.memset(MV3[:, :, 128:129], 1.0), [], [MV_b[16]])
        for r in range(4):
            for tt in range(4):
                g = r * 4 + tt
                cs, cb = load_cand2(4, r, tt)
                bk = cnt["ps"] % 2
                cnt["ps"] += 1
                for sub in range(4):
                    for h in range(4):
                        P.pe(lambda e, h=h, sub=sub, cs=cs, bk=bk: e.matmul(
                            psum[bk][:, sub * 128:(sub + 1) * 128], cs[:, h, sub * 128:(sub + 1) * 128],
                            sel_bf[:, h * 128:(h + 1) * 128], start=(h == 0), stop=(h == 3)), [cbf_b[0], cb], [ps_b[bk]])
                evac(MV3[:, g * 4:(g + 1) * 4, 0:128], psum[bk].rearrange("p (t d) -> p t d", t=4), [ps_b[bk]], [MV_b[g]])
        gg_sem = [P.dsem(("gg", i)) for i in range(2)]
        P.dve(lambda e: e.memset(GG, 0.0), [], [GG_b[0], GG_b[1]])
        for r in range(4):
            sl_ = r % 2
            ggl = GG[:, sl_ * 2048:(sl_ + 1) * 2048]
            P.dma("sp", lambda e, ggl=ggl, r=r: e.dma_start(out=ggl[0:40, :], in_=gout_c[12][r * 128:r * 128 + 40, :]),
                  [gout_b[12]], [GG_b[sl_]], gg_sem[sl_])
            for t in range(16):
                T_ = r * 16 + t
                P.pe(lambda e, T_=T_, t=t, ggl=ggl: e.matmul(psum[3][:, 2 * T_:2 * T_ + 2], ggl[0:8, t * 128:(t + 1) * 128],
                                                            gsel_bf[0:8, :], start=True, stop=False), [GG_b[sl_], cbf_b[0]], [ps_b[3]])
                P.pe(lambda e, T_=T_, t=t, ggl=ggl: e.matmul(psum[3][:, 2 * T_:2 * T_ + 2], ggl[32:40, t * 128:(t + 1) * 128],
                                                            gsel_bf[32:40, :], start=False, stop=True), [GG_b[sl_], cbf_b[0]], [ps_b[3]])
        g3v = gts.rearrange("p (t k) -> p t k", k=2)
        p3v = psum[3][:, 0:128].rearrange("p (t k) -> p t k", k=2)
        for k_ in range(2):
            P.dve(lambda e, k_=k_: e.tensor_scalar(g3v[:, :, k_:k_ + 1], p3v[:, :, k_:k_ + 1],
                                                    cst[:, C_GB + 2 * l + k_:C_GB + 2 * l + k_ + 1], None, ALU.add),
                  [ps_b[3], cst_b[0]], [gts_b[0]])
        e1, lf, lfhf, lfl, ccv = (gw[:, i * 64:(i + 1) * 64] for i in range(5))
        lfh_bf, lfl_bf = gwb[:, 0:64], gwb[:, 64:128]
        gfv = g3v[:, :, 1:2].rearrange("p t k -> p (t k)")
        giv = g3v[:, :, 0:1].rearrange("p t k -> p (t k)")
        P.act(lambda e: e.activation(e1, gfv, AF.Exp, scale=-1.0), [gts_b[0]], [gw_b[0]])
        P.act(lambda e: e.activation(lf, e1, AF.Ln, bias=cst[:, C_ONE:C_ONE + 1], scale=1.0), [gw_b[0], cst_b[0]], [gw_b[0]])
        P.dve(lambda e: e.tensor_scalar(lf, lf, -1.0, None, ALU.mult), [gw_b[0]], [gw_b[0]])
        P.dve(lambda e: e.tensor_copy(lfh_bf, lf), [gw_b[0]], [gwb_b[0]])
        P.dve(lambda e: e.tensor_copy(lfhf, lfh_bf), [gwb_b[0]], [gw_b[0]])
        P.dve(lambda e: e.tensor_tensor(lfl, lf, lfhf, ALU.subtract), [gw_b[0]], [gw_b[0]])
        P.dve(lambda e: e.tensor_copy(lfl_bf, lfl), [gw_b[0]], [gwb_b[0]])
        P.pe(lambda e: e.matmul(psum[2][:, 0:64], tri_bf, lfh_bf, start=True, stop=False), [cbf_b[0], gwb_b[0]], [ps_b[2]])
        P.pe(lambda e: e.matmul(psum[2][:, 0:64], tri_bf, lfl_bf, start=False, stop=True), [cbf_b[0], gwb_b[0]], [ps_b[2]])
        P.dve(lambda e: e.tensor_tensor(ccv, giv, psum[2][:, 0:64], ALU.subtract), [gts_b[0], ps_b[2]], [gw_b[0]])

        P.dve(lambda e: e.memset(Cs, 0.0), [], [Cs_b[0]])
        P.dve(lambda e: e.memset(Cbf, 0.0), [], [Cbf_b[0], Cbf_b[1]])
        ysm_sem = [P.dsem(("ym", i)) for i in range(2)]
        mlg = cst[:, C_MLG + l:C_MLG + l + 1]
        for G in range(16):
            g0 = G * 512
            gs = G % 2
            for v, (src, srcb) in enumerate(((PQK, PQK_b), (PKQ, PKQ_b))):
                c0 = C_CONV + 5 * (2 * l + v)
                al = acc[:, v * 512:(v + 1) * 512]
                rdl = [srcb[G], srcb[G - 1] if G > 0 else srcb[16]]
                P.dve(lambda e, al=al, src=src, c0=c0: e.tensor_scalar(al, src[:, g0 + 3:g0 + 3 + 512], cst[:, c0 + 3:c0 + 4],
                                                                        cst[:, c0 + 4:c0 + 5], ALU.mult, ALU.add),
                      rdl + [cst_b[0]], [acc_b[v]])
                for k_ in range(3):
                    P.dve(lambda e, al=al, src=src, c0=c0, k_=k_: e.scalar_tensor_tensor(
                        al, src[:, g0 + k_:g0 + k_ + 512], cst[:, c0 + k_:c0 + k_ + 1], al, ALU.mult, ALU.add),
                        rdl + [cst_b[0], acc_b[v]], [acc_b[v]])
                aql = aqk[:, v * 512:(v + 1) * 512]
                P.act(lambda e, al=al, aql=aql: e.activation(aql, al, AF.Silu), [acc_b[v]], [aqk_b[v]])
                qkl = qkb[:, v * 512:(v + 1) * 512]
                P.dve(lambda e, aql=aql, qkl=qkl: e.tensor_copy(qkl[0:64, :], aql[0:64, :]), [aqk_b[v]], [qkb_b[v]])
            q32 = aqk[0:64, 0:512]
            qb_, kb_ = qkb[0:64, 0:512], qkb[0:64, 512:1024]
            for tt in range(4):
                T_ = G * 4 + tt
                cs_ = slice(tt * 128, (tt + 1) * 128)
                s2 = T_ % 2
                b0, b1, b2 = (0, 1, 2) if s2 == 0 else (4, 5, 6)
                lfbl = lfb[:, s2 * 256:(s2 + 1) * 256]
                P.dve(lambda e, lfbl=lfbl, T_=T_: e.tensor_scalar(lfbl[:, 0:128], ones_bf, lfhf[:, T_:T_ + 1], None, ALU.mult),
                      [cbf_b[0], gw_b[0]], [lfb_b[s2]])
                P.dve(lambda e, lfbl=lfbl, T_=T_: e.tensor_scalar(lfbl[:, 128:256], ones_bf, lfl[:, T_:T_ + 1], None, ALU.mult),
                      [cbf_b[0], gw_b[0]], [lfb_b[s2]])
                pb0, pb1 = psum[b0][:, 0:128], psum[b0][:, 128:256]
                P.pe(lambda e, pb0=pb0, lfbl=lfbl: e.matmul(pb0, lfbl[:, 0:128], tri_bf, start=True, stop=False), [lfb_b[s2], cbf_b[0]], [ps_b[b0]])
                P.pe(lambda e, pb0=pb0, lfbl=lfbl: e.matmul(pb0, lfbl[:, 128:256], tri_bf, start=False, stop=True), [lfb_b[s2], cbf_b[0]], [ps_b[b0]])
                P.pe(lambda e, pb1=pb1, lfbl=lfbl: e.matmul(pb1, lfbl[:, 0:128], tri_bf, start=True, stop=False), [lfb_b[s2], cbf_b[0]], [ps_b[b0]])
                P.pe(lambda e, pb1=pb1, lfbl=lfbl: e.matmul(pb1, lfbl[:, 128:256], tri_bf, start=False, stop=False), [lfb_b[s2], cbf_b[0]], [ps_b[b0]])
                P.pe(lambda e, pb1=pb1: e.matmul(pb1, I_bf, neg_bf, start=False, stop=True), [cbf_b[0]], [ps_b[b0]])
                dxl = dex[:, s2 * 128:(s2 + 1) * 128]
                ebl = ebt[0:64, s2 * 128:(s2 + 1) * 128]
                scl = sct[:, s2 * 8:(s2 + 1) * 8]
                P.act(lambda e, dxl=dxl, pb1=pb1, T_=T_: e.activation(dxl, pb1, AF.Exp, bias=ccv[:, T_:T_ + 1], scale=1.0),
                      [ps_b[b0], gw_b[0]], [dex_b[s2]])
                P.act(lambda e, ebl=ebl, pb0=pb0: e.activation(ebl, pb0[0:64, :], AF.Exp), [ps_b[b0]], [eb_b[s2]])
                P.act(lambda e, scl=scl, pb0=pb0, T_=T_: e.activation(scl[:, 0:1], pb0[:, 127:128], AF.Exp, bias=ccv[:, T_:T_ + 1], scale=1.0),
                      [ps_b[b0], gw_b[0]], [sc_b[s2]])
                P.act(lambda e, scl=scl, pb0=pb0: e.activation(scl[0:64, 1:2], pb0[0:64, 127:128], AF.Exp), [ps_b[b0]], [sc_b[s2]])
                P.pe(lambda e, cs_=cs_: e.matmul(psum[b1][:, 0:128], kb_[:, cs_], qb_[:, cs_], start=True, stop=True),
                     [qkb_b[0], qkb_b[1]], [ps_b[b1]])
                P.pe(lambda e, cs_=cs_: e.matmul(psum[b1][:, 128:192], kb_[:, cs_], I_bf[0:64, 0:64], start=True, stop=True),
                     [qkb_b[1], cbf_b[0]], [ps_b[b1]])
                wl = wTt[:, s2 * 128:(s2 + 1) * 128]
                ql = qtt[0:64, s2 * 128:(s2 + 1) * 128]
                kl = ktt[:, s2 * 64:(s2 + 1) * 64]
                P.dve(lambda e, wl=wl, dxl=dxl: e.scalar_tensor_tensor(wl, psum[b1][:, 0:128], 0.125, dxl, ALU.mult, ALU.mult),
                      [ps_b[b1], dex_b[s2]], [wT_b[s2]])
                P.dve(lambda e, ql=ql, ebl=ebl, cs_=cs_: e.scalar_tensor_tensor(ql, q32[:, cs_], 0.125, ebl, ALU.mult, ALU.mult),
                      [aqk_b[0], eb_b[s2]], [qt_b[s2]])
                P.dve(lambda e, kl=kl, scl=scl: e.tensor_scalar(kl, psum[b1][:, 128:192], scl[:, 0:1], None, ALU.mult),
                      [ps_b[b1], sc_b[s2]], [kt_b[s2]])
                cprev = Cbf[0:64, s2 * 256:(s2 + 1) * 256]
                cnext = Cbf[0:64, (1 - s2) * 256:(2 - s2) * 256]
                P.pe(lambda e, T_=T_, wl=wl: e.matmul(psum[b2][:, 0:128], MV3[:, T_, 0:128], wl, start=True, stop=False),
                     [MV_b[T_ // 4], MV_b[16], wT_b[s2]], [ps_b[b2]])
                P.pe(lambda e, ql=ql, cprev=cprev: e.matmul(psum[b2][:, 0:128], cprev[:, 0:128], ql, start=False, stop=True),
                     [Cbf_b[s2], qt_b[s2]], [ps_b[b2]])
                P.pe(lambda e, wl=wl: e.matmul(psum[b2][:, 128:256], ones_bf, wl, start=True, stop=False), [cbf_b[0], wT_b[s2]], [ps_b[b2]])
                P.pe(lambda e, ql=ql, cprev=cprev: e.matmul(psum[b2][:, 128:256], cprev[:, 128:256], ql, start=False, stop=True),
                     [Cbf_b[s2], qt_b[s2]], [ps_b[b2]])
                P.pe(lambda e, T_=T_, kl=kl: e.matmul(psum[b1][0:64, 256:385], kl, MV3[:, T_, :], start=True, stop=True),
                     [kt_b[s2], MV_b[T_ // 4], MV_b[16]], [ps_b[b1]])
                P.dve(lambda e, scl=scl: e.scalar_tensor_tensor(Cs[0:64, 0:129], Cs[0:64, 0:129], scl[0:64, 1:2], psum[b1][0:64, 256:385],
                                                                ALU.mult, ALU.add), [Cs_b[0], sc_b[s2], ps_b[b1]], [Cs_b[0]])
                P.dve(lambda e, cnext=cnext: e.tensor_copy(cnext[:, 0:128], Cs[0:64, 0:128]), [Cs_b[0]], [Cbf_b[1 - s2]])
                P.dve(lambda e, cnext=cnext: e.tensor_scalar(cnext[:, 128:256], ones_bf[0:64, :], Cs[0:64, 128:129], None, ALU.mult),
                      [Cs_b[0], cbf_b[0]], [Cbf_b[1 - s2]])
                rl = rdn[:, s2 * 128:(s2 + 1) * 128]
                P.act(lambda e, rl=rl: e.activation(rl, psum[b2][:, 128:256], AF.Abs), [ps_b[b2]], [rdn_b[s2]])
                P.dve(lambda e, rl=rl: e.tensor_scalar(rl, rl, 1.0, None, ALU.max), [rdn_b[s2]], [rdn_b[s2]])
                P.dve(lambda e, rl=rl: e.reciprocal(rl, rl), [rdn_b[s2]], [rdn_b[s2]])
                P.dve(lambda e, rl=rl, cs_=cs_: e.tensor_tensor(hTm[:, cs_], psum[b2][:, 0:128], rl, ALU.mult),
                      [ps_b[b2], rdn_b[s2]], [hTm_b[0]])
            P.act(lambda e: e.activation(sqm, hTm, AF.Square), [hTm_b[0]], [sqm_b[0]])
            bs = 3 if gs == 0 else 7
            P.pe(lambda e, bs=bs: e.matmul(psum[bs], ones_bf, sqm, start=True, stop=True), [cbf_b[0], sqm_b[0]], [ps_b[bs]])
            rms_rstd(rsm, psum[bs], [ps_b[bs]], [rsm_b[0]], scale=1.0 / 128)
            P.dve(lambda e: e.scalar_tensor_tensor(hTm, hTm, mlg, rsm, ALU.mult, ALU.mult), [hTm_b[0], rsm_b[0], cst_b[0]], [hTm_b[0]])
            yl = ysm[:, gs * 512:(gs + 1) * 512]
            P.dve(lambda e, yl=yl: e.tensor_tensor(yl, hTm, MO[:, g0:g0 + 512], ALU.mult), [hTm_b[0], MO_b[G]], [ysm_b[gs]])
            P.dma("sp", lambda e, yl=yl: e.dma_start(out=gin2_c[g0 // T][128:256, g0 % T:g0 % T + 512], in_=yl),
                  [ysm_b[gs]], [gin2_b[1][G]], ysm_sem[gs])
        for n in ("PQK", "PKQ", "MV", "MO", "cand", "GG", "gts", "gw", "gwb", "lfb", "dex", "wT", "eb", "qt", "sc", "kt",
                  "Cs", "Cbf", "rdn", "acc", "aqk", "qkb", "hTm", "sqm", "rsm", "ysm"):
            A.free(n)
        if stop == "mls":
            return

        for b_ in range(4):
            rd = [gin2_b[k_][b_ * 4 + i] for k_ in range(2) for i in range(4)]
            P.dma("pool", lambda e, b_=b_: e.collective_compute("AllGather", ALU.bypass, replica_groups=RG,
                                                               ins=[gin2_c[b_].opt()], outs=[gout2_c[b_].opt()]),
                  rd, [gout2_b[b_]], P.dsem(("cc2", b_)), inc=1)

        yT, yT_b = A.alloc("yT", PH, [NCH * T], BF16, nbuf=NCH * 4)
        yT3 = yT.rearrange("p (c t) -> p c t", c=NCH)
        hO, hO_b = A.alloc("hO", PH + 32768, [NCH * T], F32, nbuf=NCH * 4)
        hO3 = hO.rearrange("p (c t) -> p c t", c=NCH)
        cand, cand_b = A.alloc("cand", PH + 98304, [3 * 2048], BF16, nbuf=3)
        sqo, sqo_b = A.alloc("sqo", PH + 110592, [2 * 512], BF16, nbuf=2)
        rso, rso_b = A.alloc("rso", PH + 112640, [512], F32)
        cand_sem = [P.dsem(("cd", i)) for i in range(3)]
        ci[0] = 0
        for c8 in range(NCH):
            kind_, r = divmod(c8, 4)
            for tt in range(4):
                slot = ci[0] % 3
                ci[0] += 1
                cs = cand[:, slot * 2048:(slot + 1) * 2048].rearrange("p (h t) -> p h t", h=4)
                for b_ in range(4):
                    P.dma("sp", lambda e, cs=cs, b_=b_, r=r, kind_=kind_, tt=tt: e.dma_start(
                        out=cs[:, b_, :], in_=gout2_c[b_][r * 256 + kind_ * 128:r * 256 + kind_ * 128 + 128, tt * 512:(tt + 1) * 512]),
                        [gout2_b[b_]], [cand_b[slot]], cand_sem[slot])
                bk = cnt["ps"] % 2
                cnt["ps"] += 1
                for b_ in range(4):
                    P.pe(lambda e, b_=b_, cs=cs, bk=bk: e.matmul(psum[bk], sel_bf[:, b_ * 128:(b_ + 1) * 128], cs[:, b_, :],
                                                                start=(b_ == 0), stop=(b_ == 3)), [cbf_b[0], cand_b[slot]], [ps_b[bk]])
                evac(yT3[:, c8, tt * 512:(tt + 1) * 512], psum[bk], [ps_b[bk]], [yT_b[c8 * 4 + tt]])
        it = 0
        for m in range(NCH):
            wt, wb = W.get(("wo", l, m))
            wt3 = wt.rearrange("p (c n) -> p c n", c=NCH)
            for tt in range(4):
                ts = slice(tt * 512, (tt + 1) * 512)
                bk = 2 + it % 2
                for c in range(NCH):
                    P.pe(lambda e, c=c, bk=bk, wt3=wt3, ts=ts: e.matmul(psum[bk], wt3[:, c, :], yT3[:, c, ts],
                                                                         start=(c == 0), stop=(c == NCH - 1)),
                         [wb, yT_b[c * 4 + tt]], [ps_b[bk]])
                P.act(lambda e, m=m, ts=ts, bk=bk: e.activation(hO3[:, m, ts], psum[bk], AF.Copy), [ps_b[bk]], [hO_b[m * 4 + tt]])
                sql = sqo[:, (it % 2) * 512:(it % 2 + 1) * 512]
                P.act(lambda e, sql=sql, bk=bk: e.activation(sql, psum[bk], AF.Square), [ps_b[bk]], [sqo_b[it % 2]])
                P.pe(lambda e, m=m, tt=tt, sql=sql: e.matmul(psum[4 + tt], ones_bf, sql, start=(m == 0), stop=(m == NCH - 1)),
                     [cbf_b[0], sqo_b[it % 2]], [ps_b[4 + tt]])
                it += 1
        for tt in range(4):
            ts = slice(tt * 512, (tt + 1) * 512)
            rms_rstd(rso, psum[4 + tt], [ps_b[4 + tt]], [rso_b[0]])
            for m in range(NCH):
                P.dve(lambda e, m=m, ts=ts: e.tensor_tensor(hO3[:, m, ts], hO3[:, m, ts], rso, ALU.mult),
                      [hO_b[m * 4 + tt], rso_b[0]], [hO_b[m * 4 + tt]])
                P.dve(lambda e, m=m, ts=ts: e.scalar_tensor_tensor(xT3[:, m, ts], hO3[:, m, ts], gain(l, 3, m), xT3[:, m, ts],
                                                                   ALU.mult, ALU.add),
                      [hO_b[m * 4 + tt], cst_b[0], xT_b[tt]], [xT_b[tt]])
        for n in ("yT", "hO", "cand", "sqo", "rso"):
            A.free(n)

    for ph in phases:
        if ph[0] == "ffn":
            ffn_phase(ph[1], ph[2])
        else:
            mix_phase(ph[1])

    st = P.new_dsem("st")
    evs = []
    if stop == "att":
        for b_ in range(4):
            evs.append(P.dma("sp", lambda e, b_=b_: e.dma_start(out=dbg_d[0:128, b_ * T:(b_ + 1) * T], in_=gin2_c[b_][0:128, :]),
                             [b for b in gin2_b[0]], [], st))
    if stop == "mls":
        for b_ in range(4):
            evs.append(P.dma("sp", lambda e, b_=b_: e.dma_start(out=dbg_d[:, b_ * T:(b_ + 1) * T], in_=gin2_c[b_]),
                             [b for k_ in range(2) for b in gin2_b[k_]], [], st))
    if stop == "proj":
        evs.append(P.dma("sp", lambda e: e.dma_start(out=dbg_d[0:128, 0:T], in_=gin_kh(1, 1)), [gin_b[1][1]], [], st))
        evs.append(P.dma("sp", lambda e: e.dma_start(out=dbg_d[128:256, 0:T], in_=gin_kh(0, 2)), [gin_b[0][2]], [], st))
    if stop == "gather":
        for r in range(4):
            evs.append(P.dma("sp", lambda e, r=r: e.dma_start(out=dbg_d[0:128, r * 512:(r + 1) * 512], in_=gout_rkh(r, 1, 1)[:, 0:512]),
                             gout_b, [], st))
            evs.append(P.dma("sp", lambda e, r=r: e.dma_start(out=dbg_d[128:256, r * 512:(r + 1) * 512], in_=gout_rkh(r, 0, 2)[:, 0:512]),
                             gout_b, [], st))
    if stop is not None:
        evs.append(P.dma("sp", lambda e: e.dma_start(out=out_d, in_=xT[:, 0:16]), [xT_b[0]], [], st))
    for tt in range(4 if stop is None else 0):
        evs.append(P.dma("sp", lambda e, tt=tt: e.dma_start(
            out=out_d.rearrange("p (c t) -> p c t", c=NCH)[:, :, tt * 512:(tt + 1) * 512],
            in_=xT3[:, :, tt * 512:(tt + 1) * 512]), [xT_b[tt]], [], st))
    P.fence("sp", evs)
    P.emit()
    return nc, P


def _mix_tile_cols():
    tiles = []
    rng = np.arange(128)
    perm = (rng // 64) * 64 + ((rng % 64) + 32) % 64
    for base in (0, 512):
        for h in range(4):
            tiles.append((base + h * 128 + rng, base + h * 128 + perm))
    for h0 in (0, 2):
        tiles.append((1024 + h0 * 128 + rng, 1024 + (h0 + 1) * 128 + rng))
    r64 = np.arange(64)
    for h0 in (0, 2):
        tiles.append(tuple(np.concatenate([1536 + h * 64 + r64, 1792 + h * 64 + r64]) for h in (h0, h0 + 1)))
    for base in (2048, 2560):
        for h0 in (0, 2):
            tiles.append((base + h0 * 128 + rng, base + (h0 + 1) * 128 + rng))
    gcols = np.full(128, -1)
    gcols[:8] = 3072 + np.arange(8)
    gcols[32:40] = 3072 + np.arange(8)
    tiles.append((gcols, np.full(128, -1)))
    return tiles


def _prep_shared(inp, ffn_ids, mix_ids):
    f = np.float32
    out = {}
    w_in = np.asarray(inp["ffn_w_in"], f).reshape(DEPTH * 2, D, 2 * DFF)
    w_out = np.asarray(inp["ffn_w_out"], f).reshape(DEPTH * 2, DFF, D)
    if ffn_ids:
        wi = w_in[ffn_ids]
        n = len(ffn_ids)
        w1 = wi.reshape(n, NCH, 128, 2, NF, 128).transpose(0, 4, 2, 3, 1, 5)
        out["w1"] = np.ascontiguousarray(w1).reshape(n * NF, 128, 2 * NCH * 128)
        wo_ = w_out[ffn_ids]
        w2 = wo_.reshape(n, NF, 128, NCH, 128).transpose(0, 3, 2, 1, 4)
        out["w2"] = np.ascontiguousarray(w2).reshape(n * NCH, 128, NF * 128)
    else:
        out["w1"] = np.zeros((NF, 128, 2 * NCH * 128), f)
        out["w2"] = np.zeros((NCH, 128, NF * 128), f)
    if mix_ids:
        tiles = _mix_tile_cols()
        wm = np.zeros((len(mix_ids), 17, 128, 2, NCH, 128), f)
        wo = np.zeros((len(mix_ids), NCH, 128, NCH, 128), f)
        for i, l in enumerate(mix_ids):
            Wl = np.asarray(inp["mix_w_in"][l], f)
            Wp = np.concatenate([Wl, np.zeros((D, 1), f)], axis=1)
            for ti, groups in enumerate(tiles):
                for g, cols in enumerate(groups):
                    blk = Wp[:, cols]
                    wm[i, ti, :, g] = blk.reshape(NCH, 128, 128).transpose(1, 0, 2)
            Wo = np.asarray(inp["mix_w_out"][l], f)
            wo[i] = Wo.reshape(NCH, 128, NCH, 128).transpose(2, 1, 0, 3)
        out["wm"] = wm.reshape(len(mix_ids) * 17, 128, 2048)
        out["wo"] = wo.reshape(len(mix_ids) * NCH, 128, 1024)
    else:
        out["wm"] = np.zeros((17, 128, 2048), f)
        out["wo"] = np.zeros((NCH, 128, 1024), f)
    return out


def _prep_core(inp, core):
    f = np.float32
    b, j = divmod(core, 4)
    cst = np.zeros((128, NCST), f)
    g = np.asarray(inp["norm_gains"], f)
    cst[:, 0:96] = g.reshape(DEPTH * 6, NCH, 128).transpose(2, 0, 1).reshape(128, 96)
    cst[:, C_EPS] = EPS
    cst[:, C_ONE] = 1.0
    cw = np.asarray(inp["ml_conv_w"], f)
    cb = np.asarray(inp["ml_conv_b"], f)
    gb = np.asarray(inp["ml_gate_b"], f)
    for l in range(DEPTH):
        cst[:, C_SUBG + l] = np.asarray(inp["da_subln_g"][l], f)
        cst[:, C_MLG + l] = np.asarray(inp["ml_norm_g"][l], f)
        cst[:, C_GB + 2 * l] = gb[l, j]
        cst[:, C_GB + 2 * l + 1] = gb[l, 4 + j]
        qi = j * 64 + np.arange(64)
        ki = 256 + j * 64 + np.arange(64)
        for v, idx in enumerate((np.concatenate([qi, ki]), np.concatenate([ki, qi]))):
            c0 = C_CONV + 5 * (2 * l + v)
            cst[:, c0:c0 + 4] = cw[l][:, idx].T
            cst[:, c0 + 4] = cb[l][idx]
        cst[:, C_LAM + 256 * l: C_LAM + 256 * (l + 1)] = np.asarray(inp["da_lambda"][l], f).reshape(1, 256)
    cst[j, C_GSEL] = 1.0
    cst[4 + j, C_GSEL + 1] = 1.0
    cst[32 + j, C_GSEL] = 1.0
    cst[32 + 4 + j, C_GSEL + 1] = 1.0
    sw = np.zeros((128, 128), f)
    sw[(np.arange(128) + 64) % 128, np.arange(128)] = 1.0
    cst[:, C_SELSW + j * 128: C_SELSW + (j + 1) * 128] = sw
    cst[:, C_I:C_I + 128] = np.eye(128, dtype=f)
    ii = np.arange(128)
    cst[:, C_TRI:C_TRI + 128] = (ii[:, None] <= ii[None, :]).astype(f)
    cst[:, C_NEG:C_NEG + 128] = np.where(ii[:, None] > ii[None, :], -30000.0, 0.0).astype(f)
    cst[:, C_SEL + j * 128: C_SEL + (j + 1) * 128] = np.eye(128, dtype=f)
    pos = (j * T + np.arange(T)).astype(np.float64)
    inv = 10000.0 ** (-np.arange(0, 64, 2, dtype=np.float64) / 64.0)
    d = np.arange(128) % 64
    ang = pos[None, :] * inv[d % 32][:, None]
    sgn = np.where(d < 32, -1.0, 1.0)[:, None]
    rope = np.concatenate([np.cos(ang), sgn * np.sin(ang)], axis=1).astype(f)
    return {"cst": cst, "rope": rope}


def _shard_x(x):
    xs = []
    for c in range(8):
        b, q = divmod(c, 4)
        xc = np.asarray(x[b, q * T:(q + 1) * T, :], np.float32)
        xs.append(np.ascontiguousarray(xc.reshape(T, NCH, 128).transpose(2, 1, 0)).reshape(128, NCH * T))
    return xs


def _unshard(outs):
    y = np.zeros((2, SEQ, D), np.float32)
    for c in range(8):
        b, q = divmod(c, 4)
        o = np.asarray(outs[c], np.float32).reshape(128, NCH, T).transpose(2, 1, 0).reshape(T, D)
        y[b, q * T:(q + 1) * T, :] = o
    return y


ALL_PHASES = [("ffn", 0, 0), ("mix", 0), ("ffn", 0, 1), ("ffn", 1, 0), ("mix", 1), ("ffn", 1, 1)]


def run_phases(inputs, phases, stop=None):
    ffn_ids = sorted({ph[1] * 2 + ph[2] for ph in phases if ph[0] == "ffn"})
    mix_ids = sorted({ph[1] for ph in phases if ph[0] == "mix"})
    shared = _prep_shared(inputs, ffn_ids, mix_ids)
    xs = _shard_x(inputs["x"])
    nc, P = build_program(phases, stop=stop)
    in_maps = [dict(shared, xT=xs[c], **_prep_core(inputs, c)) for c in range(8)]
    res = run_bass_kernel_spmd(nc, in_maps, core_ids=list(range(8)))
    if stop is not None:
        return None, [np.asarray(r["dbg"]) for r in res.results]
    return _unshard([r["out"] for r in res.results])


def kernel(**inputs):
    return run_phases(inputs, ALL_PHASES)
```

```python
import numpy as np
import concourse.bass as bass
import concourse.mybir as mybir
from concourse.bass_utils import run_bass_kernel_spmd

F32 = mybir.dt.float32
BF16 = mybir.dt.bfloat16
ALU = mybir.AluOpType
AF = mybir.ActivationFunctionType

D = 1024
NCH = 8
DFF = 2816
NF = 22
DEPTH = 2
T = 2048
SEQ = 8192
EPS = 1e-6

ENGS = ("pe", "act", "dve", "pool", "sp")
EPOCH = 3000


class Buf:
    __slots__ = ("name", "w", "r", "excl")

    def __init__(self, name, inherit=None, excl=False):
        self.name = name
        self.w = None
        self.r = list(inherit) if inherit else []
        self.excl = excl


class Ev:
    __slots__ = ("key", "val", "clock", "op")

    def __init__(self, key, val, clock, op):
        self.key, self.val, self.clock, self.op = key, val, clock, op


class DSem:
    def __init__(self, sem):
        self.sem = sem
        self.count = 0


class Op:
    __slots__ = ("eng", "fn", "waits", "ev", "signal", "sig_no", "dsem", "inc")


class _Rec:
    def __init__(self):
        self.call = None

    def __getattr__(self, name):
        def f(*a, **k):
            assert self.call is None
            self.call = (name, a, k)
            return None
        return f


def _compress(events):
    best = {}
    for ev in events:
        cur = best.get(ev.key)
        if cur is None or cur.val < ev.val:
            best[ev.key] = ev
    return list(best.values())


class Prog:
    def __init__(self, nc):
        self.nc = nc
        self.ops = {e: [] for e in ENGS}
        self.known = {e: {} for e in ENGS}
        self.nwaits = 0
        self._nsem = 0

    def new_dsem(self, name="d"):
        self._nsem += 1
        return DSem(self.nc.alloc_semaphore(name=f"{name}_{self._nsem}"))

    def dsem(self, key):
        if not hasattr(self, "_cache"):
            self._cache = {}
        if key not in self._cache:
            self._cache[key] = self.new_dsem(str(key))
        return self._cache[key]

    def _add(self, eng, fn, reads, writes, dsem=None, inc=16, extra=()):
        op = Op()
        if fn is not None:
            rec = _Rec()
            fn(rec)
            assert rec.call is not None
            fn = rec.call
        op.eng, op.fn, op.dsem, op.signal, op.sig_no, op.inc = eng, fn, dsem, False, 0, inc
        deps = {}
        for b in reads:
            if b.w is not None:
                deps[id(b.w)] = b.w
            if b.excl:
                for r in b.r:
                    if r.key != eng:
                        deps[id(r)] = r
        for b in writes:
            if b.w is not None:
                deps[id(b.w)] = b.w
            for r in b.r:
                deps[id(r)] = r
        for ev in extra:
            deps[id(ev)] = ev
        known = self.known[eng]
        waits = {}
        for ev in deps.values():
            if eng == "pe" and ev.key == "pe":
                continue
            if known.get(ev.key, 0) >= ev.val:
                continue
            cur = waits.get(ev.key)
            if cur is None or cur.val < ev.val:
                waits[ev.key] = ev
        if waits:
            known = dict(known)
            for ev in waits.values():
                for k, v in ev.clock.items():
                    if known.get(k, 0) < v:
                        known[k] = v
                if known.get(ev.key, 0) < ev.val:
                    known[ev.key] = ev.val
            self.known[eng] = known
            self.nwaits += len(waits)
        op.waits = list(waits.values())
        for ev in op.waits:
            if ev.op.dsem is None:
                ev.op.signal = True
        self.ops[eng].append(op)
        idx = len(self.ops[eng])
        if dsem is None:
            ev = Ev(eng, idx, known, op)
        else:
            dsem.count += inc
            ev = Ev(dsem, dsem.count, known, op)
        op.ev = ev
        for b in reads:
            b.r.append(ev)
        for b in writes:
            b.w = ev
            b.r = []
        return ev

    def pe(self, fn, reads, writes):
        return self._add("pe", fn, reads, writes)

    def act(self, fn, reads, writes):
        return self._add("act", fn, reads, writes)

    def dve(self, fn, reads, writes):
        return self._add("dve", fn, reads, writes)

    def dma(self, queue, fn, reads, writes, dsem, inc=16, extra=()):
        return self._add(queue, fn, reads, writes, dsem=dsem, inc=inc, extra=extra)

    def fence(self, eng, events):
        return self._add(eng, None, [], [], extra=events)

    def emit(self):
        nc = self.nc
        esems = {}
        for e in ENGS:
            n = 0
            for op in self.ops[e]:
                if op.signal:
                    n += 1
                    op.sig_no = n
            nep = max(1, (n + EPOCH - 1) // EPOCH)
            esems[e] = [nc.alloc_semaphore(name=f"prog_{e}_{i}") for i in range(nep)]

        def sem_of(ev):
            if isinstance(ev.key, DSem):
                return ev.key.sem, ev.val
            n = ev.op.sig_no
            assert n > 0
            return esems[ev.key][(n - 1) // EPOCH], (n - 1) % EPOCH + 1

        def run(eng_name, eng):
            for op in self.ops[eng_name]:
                for ev in op.waits:
                    s, v = sem_of(ev)
                    eng.wait_ge(s, v)
                if op.fn is None:
                    if op.signal:
                        s, v = sem_of(op.ev)
                        eng.nop().then_inc(s, 1)
                    continue
                name, a, k = op.fn
                ins = getattr(eng, name)(*a, **k)
                if op.dsem is not None:
                    ins.then_inc(op.dsem.sem, op.inc)
                elif op.signal:
                    s, v = sem_of(op.ev)
                    ins.then_inc(s, 1)

        with nc.Block() as block:
            @block.tensor
            def _(e):
                run("pe", e)

            @block.scalar
            def _(e):
                run("act", e)

            @block.vector
            def _(e):
                run("dve", e)

            @block.gpsimd
            def _(e):
                run("pool", e)

            @block.sync
            def _(e):
                run("sp", e)


class Arena:
    def __init__(self, nc, base, top):
        self.nc, self.base, self.top = nc, base, top
        self.live = {}
        self.dead = []
        self.n = 0

    def alloc(self, name, off, free_shape, dtype, nbuf=1):
        esz = 4 if dtype == F32 else 2
        cols = int(np.prod(free_shape))
        start = self.base + off
        end = start + cols * esz
        assert end <= self.top, (name, end, self.top)
        assert start % 32 == 0, (name, start)
        for k, (s, e, _) in self.live.items():
            assert e <= start or s >= end, f"arena overlap {name} vs {k}"
        inherit = []
        keep = []
        for (s, e, evs) in self.dead:
            if not (e <= start or s >= end):
                inherit.extend(evs)
                if s >= start and e <= end:
                    continue
            keep.append((s, e, evs))
        self.dead = keep
        inherit = _compress(inherit)
        self.n += 1
        th = self.nc.alloc_sbuf_tensor_at(f"{name}{self.n}", [128, cols], dtype, offset=start)
        bl = [Buf(f"{name}_{i}", inherit) for i in range(nbuf)]
        self.live[name] = (start, end, bl)
        return th.ap(), bl

    def free(self, name):
        s, e, bl = self.live.pop(name)
        evs = []
        for b in bl:
            if b.w is not None:
                evs.append(b.w)
            evs.extend(b.r)
        self.dead.append((s, e, _compress(evs)))


class WStream:
    NSLOT = 4
    SLOT_ELEMS = 2816

    def __init__(self, P, arena, off):
        self.P = P
        self.ap, self.bufs = arena.alloc("wring", off, [self.NSLOT * self.SLOT_ELEMS], BF16, nbuf=self.NSLOT)
        self.sems = [P.new_dsem("w") for _ in range(self.NSLOT)]
        self.tiles = []
        self.issued = 0
        self.cursor = 0

    def plan(self, tag, dram_ap, nelem):
        assert nelem <= self.SLOT_ELEMS
        self.tiles.append((tag, dram_ap, nelem))

    def _issue(self, i):
        tag, src, nelem = self.tiles[i]
        s = i % self.NSLOT
        dst = self.ap[:, s * self.SLOT_ELEMS: s * self.SLOT_ELEMS + nelem]
        self.P.dma("pool", lambda e, dst=dst, src=src: e.dma_start(out=dst, in_=src),
                   [], [self.bufs[s]], self.sems[s])

    def get(self, tag):
        while self.issued < min(len(self.tiles), self.cursor + self.NSLOT):
            self._issue(self.issued)
            self.issued += 1
        i = self.cursor
        assert self.tiles[i][0] == tag, (self.tiles[i][0], tag)
        self.cursor += 1
        s = i % self.NSLOT
        nelem = self.tiles[i][2]
        return self.ap[:, s * self.SLOT_ELEMS: s * self.SLOT_ELEMS + nelem], self.bufs[s]


NKIND = 6
RG = [[0, 1, 2, 3], [4, 5, 6, 7]]
NT = SEQ // 128
C_EPS = 200
C_SUBG = 208
C_MLG = 210
C_GB = 216
C_CONV = 220
C_GSEL = 240
C_ONE = 244
C_LAM = 256
C_I = 768
C_TRI = 896
C_NEG = 1024
C_SEL = 1152
C_SELSW = 1664
NCST = 2176


def build_program(phases, stop=None):
    import math
    nc = bass.Bass("TRN2", target_bir_lowering=False)
    P = Prog(nc)
    ffn_ids = sorted({ph[1] * 2 + ph[2] for ph in phases if ph[0] == "ffn"})
    mix_ids = sorted({ph[1] for ph in phases if ph[0] == "mix"})
    x_in = nc.dram_tensor("xT", [128, NCH * T], F32, kind="ExternalInput").ap()
    w1_d = nc.dram_tensor("w1", [max(1, len(ffn_ids)) * NF, 128, 2 * NCH * 128], F32, kind="ExternalInput").ap()
    w2_d = nc.dram_tensor("w2", [max(1, len(ffn_ids)) * NCH, 128, NF * 128], F32, kind="ExternalInput").ap()
    wm_d = nc.dram_tensor("wm", [max(1, len(mix_ids)) * 17, 128, 2048], F32, kind="ExternalInput").ap()
    wo_d = nc.dram_tensor("wo", [max(1, len(mix_ids)) * NCH, 128, 1024], F32, kind="ExternalInput").ap()
    rope_d = nc.dram_tensor("rope", [128, 2 * T], F32, kind="ExternalInput").ap()
    cst_d = nc.dram_tensor("cst", [128, NCST], F32, kind="ExternalInput").ap()
    out_d = nc.dram_tensor("out", [128, NCH * T if stop is None else 16], F32, kind="ExternalOutput").ap()
    dbg_d = None
    if stop is not None:
        dbg_d = nc.dram_tensor("dbg", [256, SEQ if stop in ("att", "mls") else 2048], BF16, kind="ExternalOutput").ap()
    NCHK = 13
    gin_c = [nc.dram_tensor(f"gin{i}", [256 if i < 12 else 128, T], BF16).ap() for i in range(NCHK)]
    gout_c = [nc.dram_tensor(f"gout{i}", [4 * (256 if i < 12 else 128), T], BF16).ap() for i in range(NCHK)]
    gin2_c = [nc.dram_tensor(f"gin2_{i}", [256, T], BF16).ap() for i in range(4)]
    gout2_c = [nc.dram_tensor(f"gout2_{i}", [4 * 256, T], BF16).ap() for i in range(4)]
    gin_b = [[Buf(f"gin{k}_{h}") for h in range(4)] for k in range(NKIND)]
    ging_b = Buf("ging")
    gout_b = [Buf(f"gout{i}") for i in range(NCHK)]
    gin2_b = [[Buf(f"gin2_{k}_{q}") for q in range(16)] for k in range(2)]
    gout2_b = [Buf(f"gout2_{i}") for i in range(4)]

    def gin_kh(kind, h):
        return gin_c[kind * 2 + h // 2][(h % 2) * 128:(h % 2) * 128 + 128, :]

    def gout_rkh(r, kind, h):
        base = r * 256 + (h % 2) * 128
        return gout_c[kind * 2 + h // 2][base:base + 128, :]

    A = Arena(nc, 16512, 229344)
    xT, xT_b = A.alloc("xT", 0, [NCH * T], F32, nbuf=4)
    xT3 = xT.rearrange("p (c t) -> p c t", c=NCH)
    W = WStream(P, A, 65536)
    CB = 88064
    cst, cst_b = A.alloc("cst", CB, [768], F32)
    cbf, cbf_b = A.alloc("cbf", CB + 3072, [1552], BF16)
    ones_bf, I_bf, tri_bf, neg_bf, sel_bf = (cbf[:, 0:128], cbf[:, 128:256], cbf[:, 256:384],
                                             cbf[:, 384:512], cbf[:, 512:1024])
    ones_b = cbf_b
    sml, sml_b = A.alloc("sml", CB + 3072 + 3104, [128], F32)
    PH = CB + 3072 + 3104 + 512
    gsel_bf = cbf[:, 1024:1026]
    selsw_bf = cbf[:, 1040:1552]
    psum = [nc.alloc_psum_tensor(f"ps{i}", [128, 512], F32).ap() for i in range(8)]
    ps_b = [Buf(f"ps{i}", excl=True) for i in range(8)]
    ld = [P.new_dsem("ld") for _ in range(4)]
    misc = P.new_dsem("misc")
    rope_sem = P.new_dsem("rope")
    ging_sem = P.new_dsem("ging")

    for ph in phases:
        kind = ph[0]
        if kind == "ffn":
            _, l, k = ph
            lf = l * 2 + k
            fi = ffn_ids.index(lf)
            for ps_ in range(2):
                for f in range(NF):
                    W.plan(("w1", lf, ps_, f), w1_d[fi * NF + f], 2 * NCH * 128)
                for m in range(NCH):
                    W.plan(("w2", lf, ps_, m), w2_d[fi * NCH + m], NF * 128)
        else:
            l = ph[1]
            mi = mix_ids.index(l)
            for ti in range(17):
                W.plan(("wm", l, ti), wm_d[mi * 17 + ti], 2048)
            if stop is None:
                for m in range(NCH):
                    W.plan(("wo", l, m), wo_d[mi * NCH + m], 1024)

    for tt in range(4):
        P.dma("sp", lambda e, tt=tt: e.dma_start(out=xT3[:, :, tt * 512:(tt + 1) * 512],
                                                 in_=x_in.rearrange("p (c t) -> p c t", c=NCH)[:, :, tt * 512:(tt + 1) * 512]),
              [], [xT_b[tt]], ld[tt])
    P.dma("sp", lambda e: e.dma_start(out=cst, in_=cst_d[:, 0:768]), [], [cst_b[0]], misc)
    ctmp, ctmp_b = A.alloc("ctmp", PH, [NCST - 768], F32)
    ctmp_sem = P.new_dsem("ctmp")
    P.dma("sp", lambda e: e.dma_start(out=ctmp, in_=cst_d[:, 768:NCST]), [], [ctmp_b[0]], ctmp_sem)
    P.dve(lambda e: e.memset(ones_bf, 1.0), [], [cbf_b[0]])
    P.dve(lambda e: e.tensor_copy(cbf[:, 128:1024], ctmp[:, 0:896]), [ctmp_b[0]], [cbf_b[0]])
    P.dve(lambda e: e.tensor_copy(gsel_bf, cst[:, C_GSEL:C_GSEL + 2]), [cst_b[0]], [cbf_b[0]])
    P.dve(lambda e: e.tensor_copy(selsw_bf, ctmp[:, C_SELSW - 768:C_SELSW - 768 + 512]), [ctmp_b[0]], [cbf_b[0]])
    A.free("ctmp")

    eps_col = cst[:, C_EPS:C_EPS + 1]

    def gain(l, k, c):
        i = (l * 6 + k) * NCH + c
        return cst[:, i:i + 1]

    def gain_half(l, k, c):
        i = 96 + (l * 6 + k) * NCH + c
        return cst[:, i:i + 1]

    cst_half_b = Buf("csthalf")
    P.dve(lambda e: e.tensor_scalar(cst[:, 96:192], cst[:, 0:96], 0.5, None, ALU.mult), [cst_b[0]], [cst_half_b])

    def rms_rstd(dst, src_ps, reads, writes, scale=1.0 / D):
        P.act(lambda e: e.activation(dst, src_ps, AF.Sqrt, bias=eps_col, scale=scale), reads + [cst_b[0]], writes)
        P.dve(lambda e: e.reciprocal(dst, dst), writes, writes)

    def pre_norm(l, kpre, ntt, t0, xn3, xn_b, sq3, sq_bl, rs, rs_b, pbank):
        for tt in range(ntt):
            g = (t0 + tt * 512) // 512
            ts = slice(t0 + tt * 512, t0 + (tt + 1) * 512)
            ls = slice(tt * 512, (tt + 1) * 512)
            P.dve(lambda e, ts=ts: e.tensor_tensor(sq3, xT3[:, :, ts], xT3[:, :, ts], ALU.mult), [xT_b[g]], sq_bl)
            pss, pss_b = psum[pbank], ps_b[pbank]
            for c in range(NCH):
                P.pe(lambda e, c=c, pss=pss: e.matmul(pss, ones_bf, sq3[:, c, :], start=(c == 0), stop=(c == NCH - 1)),
                     [ones_b[0]] + sq_bl, [pss_b])
            rms_rstd(rs, pss, [pss_b], [rs_b])
            for c in range(NCH):
                P.dve(lambda e, c=c, ts=ts, ls=ls: e.scalar_tensor_tensor(
                    xn3[:, c, ls], xT3[:, c, ts], gain(l, kpre, c), rs, ALU.mult, ALU.mult),
                    [xT_b[g], rs_b, cst_b[0]], [xn_b[c * ntt + tt]])

    def ffn_phase(l, k):
        lf = l * 2 + k
        kpre, kpost = (0, 1) if k == 0 else (4, 5)
        xn, xn_b = A.alloc("xn", PH, [NCH * 1024], BF16, nbuf=16)
        xn3 = xn.rearrange("p (c t) -> p c t", c=NCH)
        actT, act_b = A.alloc("actT", PH + 16384, [NF * 1024], BF16, nbuf=NF * 2)
        act3 = actT.rearrange("p (f t) -> p f t", f=NF)
        hT, h_b = A.alloc("hT", PH + 61440, [NCH * 1024], F32, nbuf=16)
        h3 = hT.rearrange("p (c t) -> p c t", c=NCH)
        sq, sq_b = A.alloc("sq", PH + 94208, [NCH * 512], BF16, nbuf=2)
        sq3 = sq.rearrange("p (c t) -> p c t", c=NCH)
        sg, sg_b = A.alloc("sg", PH + 102400, [2 * 512], F32, nbuf=2)
        rs, rs_b = A.alloc("rs", PH + 106496, [2 * 512], F32, nbuf=2)
        tmp, tmp_b = A.alloc("tmp", PH + 110592, [2 * 512], F32, nbuf=2)
        for ps_ in range(2):
            t0 = ps_ * 1024
            pre_norm(l, kpre, 2, t0, xn3, xn_b, sq3, [sq_b[0], sq_b[1]], rs[:, 0:512], rs_b[0], 6)
            it = 0
            for f in range(NF):
                wt, wb = W.get(("w1", lf, ps_, f))
                wt4 = wt.rearrange("p (g c n) -> p g c n", g=2, c=NCH)
                for tt in range(2):
                    ls = slice(tt * 512, (tt + 1) * 512)
                    pg, pg_b = psum[(it % 2) * 2], ps_b[(it % 2) * 2]
                    pu, pu_b = psum[(it % 2) * 2 + 1], ps_b[(it % 2) * 2 + 1]
                    for c in range(NCH):
                        P.pe(lambda e, c=c, pg=pg, wt4=wt4, ls=ls: e.matmul(pg, wt4[:, 0, c, :], xn3[:, c, ls],
                                                                             start=(c == 0), stop=(c == NCH - 1)),
                             [wb, xn_b[c * 2 + tt]], [pg_b])
                    for c in range(NCH):
                        P.pe(lambda e, c=c, pu=pu, wt4=wt4, ls=ls: e.matmul(pu, wt4[:, 1, c, :], xn3[:, c, ls],
                                                                             start=(c == 0), stop=(c == NCH - 1)),
                             [wb, xn_b[c * 2 + tt]], [pu_b])
                    sgl = sg[:, (it % 2) * 512:(it % 2 + 1) * 512]
                    P.act(lambda e, sgl=sgl, pg=pg: e.activation(sgl, pg, AF.Silu), [pg_b], [sg_b[it % 2]])
                    P.dve(lambda e, f=f, ls=ls, sgl=sgl, pu=pu: e.tensor_tensor(act3[:, f, ls], sgl, pu, ALU.mult),
                          [sg_b[it % 2], pu_b], [act_b[f * 2 + tt]])
                    it += 1
            it = 0
            for m in range(NCH):
                wt, wb = W.get(("w2", lf, ps_, m))
                wt3 = wt.rearrange("p (f n) -> p f n", f=NF)
                for tt in range(2):
                    ls = slice(tt * 512, (tt + 1) * 512)
                    ph_, ph_b = psum[4 + it % 2], ps_b[4 + it % 2]
                    for f in range(NF):
                        P.pe(lambda e, f=f, ph_=ph_, wt3=wt3, ls=ls: e.matmul(ph_, wt3[:, f, :], act3[:, f, ls],
                                                                               start=(f == 0), stop=(f == NF - 1)),
                             [wb, act_b[f * 2 + tt]], [ph_b])
                    P.act(lambda e, m=m, ls=ls, ph_=ph_: e.activation(h3[:, m, ls], ph_, AF.Copy), [ph_b], [h_b[m * 2 + tt]])
                    sql = sq[:, (it % 2) * 512:(it % 2 + 1) * 512]
                    P.act(lambda e, sql=sql, ph_=ph_: e.activation(sql, ph_, AF.Square), [ph_b], [sq_b[it % 2]])
                    pss, pss_b = psum[6 + tt], ps_b[6 + tt]
                    P.pe(lambda e, m=m, pss=pss, sql=sql: e.matmul(pss, ones_bf, sql, start=(m == 0), stop=(m == NCH - 1)),
                         [ones_b[0], sq_b[it % 2]], [pss_b])
                    it += 1
            for tt in range(2):
                g = ps_ * 2 + tt
                ts = slice(t0 + tt * 512, t0 + (tt + 1) * 512)
                ls = slice(tt * 512, (tt + 1) * 512)
                rsl = rs[:, tt * 512:(tt + 1) * 512]
                rms_rstd(rsl, psum[6 + tt], [ps_b[6 + tt]], [rs_b[tt]])
                for m in range(NCH):
                    tl = tmp[:, (m % 2) * 512:(m % 2 + 1) * 512]
                    P.dve(lambda e, m=m, ls=ls, tl=tl, rsl=rsl: e.tensor_tensor(tl, h3[:, m, ls], rsl, ALU.mult),
                          [h_b[m * 2 + tt], rs_b[tt]], [tmp_b[m % 2]])
                    P.dve(lambda e, m=m, ts=ts, tl=tl: e.scalar_tensor_tensor(
                        xT3[:, m, ts], tl, gain_half(l, kpost, m), xT3[:, m, ts], ALU.mult, ALU.add),
                        [tmp_b[m % 2], cst_half_b, xT_b[g]], [xT_b[g]])
        for n in ("xn", "actT", "hT", "sq", "sg", "rs", "tmp"):
            A.free(n)

    cnt = {"ev": 0, "ps": 0}

    def evac(dst, src, reads, writes, func=None):
        if func is not None:
            P.act(lambda e: e.activation(dst, src, func), reads, writes)
        else:
            cnt["ev"] += 1
            if cnt["ev"] % 2:
                P.act(lambda e: e.activation(dst, src, AF.Copy), reads, writes)
            else:
                P.dve(lambda e: e.tensor_copy(dst, src), reads, writes)

    def mix_phase(l):
        lam_init = 0.8 - 0.6 * math.exp(-0.3 * l)
        lamv = cst[:, C_LAM + 256 * l: C_LAM + 256 * (l + 1)]
        P.dve(lambda e: e.tensor_tensor(sml[:, 8:72], lamv[:, 0:64], lamv[:, 64:128], ALU.mult), [cst_b[0]], [sml_b[0]])
        P.dve(lambda e: e.reduce_sum(sml[:, 2:3], sml[:, 8:72], axis=mybir.AxisListType.X), [sml_b[0]], [sml_b[0]])
        P.dve(lambda e: e.tensor_tensor(sml[:, 8:72], lamv[:, 128:192], lamv[:, 192:256], ALU.mult), [cst_b[0], sml_b[0]], [sml_b[0]])
        P.dve(lambda e: e.reduce_sum(sml[:, 3:4], sml[:, 8:72], axis=mybir.AxisListType.X), [sml_b[0]], [sml_b[0]])
        P.act(lambda e: e.activation(sml[:, 2:4], sml[:, 2:4], AF.Exp), [sml_b[0]], [sml_b[0]])
        P.dve(lambda e: e.tensor_tensor(sml[:, 0:1], sml[:, 3:4], sml[:, 2:3], ALU.subtract), [sml_b[0]], [sml_b[0]])
        P.dve(lambda e: e.tensor_scalar(sml[:, 0:1], sml[:, 0:1], -lam_init, None, ALU.add), [sml_b[0]], [sml_b[0]])
        P.dve(lambda e: e.tensor_scalar(sml[:, 1:2], cst[:, C_SUBG + l:C_SUBG + l + 1], 1.0 - lam_init, None, ALU.mult),
              [cst_b[0], sml_b[0]], [sml_b[0]])
        neglam, gsub = sml[:, 0:1], sml[:, 1:2]

        xn, xn_b = A.alloc("xn2", PH, [NCH * T], BF16, nbuf=NCH * 4)
        xn3 = xn.rearrange("p (c t) -> p c t", c=NCH)
        rope, rope_b = A.alloc("rope", PH + 32768, [2 * T], F32)
        rope3 = rope.rearrange("p (k t) -> p k t", k=2)
        stg, stg_b = A.alloc("stg", PH + 49152, [4 * T], BF16, nbuf=4)
        sq, sq_b = A.alloc("sq", PH + 65536, [NCH * 512], BF16)
        sq3 = sq.rearrange("p (c t) -> p c t", c=NCH)
        rs, rs_b = A.alloc("rs", PH + 73728, [512], F32)
        t12, t12_b = A.alloc("t12", PH + 75776, [4 * 512], F32, nbuf=4)
        stgg, stgg_b = A.alloc("stgg", PH + 83968, [T], BF16)
        P.dma("sp", lambda e: e.dma_start(out=rope, in_=rope_d), [], [rope_b[0]], rope_sem)
        pre_norm(l, 2, 4, 0, xn3, xn_b, sq3, [sq_b[0]], rs, rs_b[0], 7)
        stg_sem = [P.dsem(("sg", i)) for i in range(4)]
        si = 0
        pc = 0
        for ti in range(17):
            wt, wb = W.get(("wm", l, ti))
            wt4 = wt.rearrange("p (g c n) -> p g c n", g=2, c=NCH)
            if ti < 8:
                kind, h = (0, ti) if ti < 4 else (1, ti - 4)
                slot = si % 4
                si += 1
                sl = stg[:, slot * T:(slot + 1) * T]
                for tt in range(4):
                    ts = slice(tt * 512, (tt + 1) * 512)
                    ba, bb = (tt % 2) * 2, (tt % 2) * 2 + 1
                    for g_, bk in ((0, ba), (1, bb)):
                        for c in range(NCH):
                            P.pe(lambda e, c=c, g_=g_, bk=bk, wt4=wt4, ts=ts: e.matmul(
                                psum[bk], wt4[:, g_, c, :], xn3[:, c, ts], start=(c == 0), stop=(c == NCH - 1)),
                                [wb, xn_b[c * 4 + tt]], [ps_b[bk]])
                    t1 = t12[:, ba * 512:(ba + 1) * 512]
                    t2 = t12[:, bb * 512:(bb + 1) * 512]
                    P.dve(lambda e, t1=t1, ba=ba, ts=ts: e.tensor_tensor(t1, psum[ba], rope3[:, 0, ts], ALU.mult),
                          [ps_b[ba], rope_b[0]], [t12_b[ba]])
                    P.dve(lambda e, t2=t2, bb=bb, ts=ts: e.tensor_tensor(t2, psum[bb], rope3[:, 1, ts], ALU.mult),
                          [ps_b[bb], rope_b[0]], [t12_b[bb]])
                    P.dve(lambda e, t1=t1, t2=t2, sl=sl, ts=ts: e.tensor_tensor(sl[:, ts], t1, t2, ALU.add),
                          [t12_b[ba], t12_b[bb]], [stg_b[slot]])
                P.dma("sp", lambda e, kind=kind, h=h, sl=sl: e.dma_start(out=gin_kh(kind, h), in_=sl),
                      [stg_b[slot]], [gin_b[kind][h]], stg_sem[slot])
            elif ti < 16:
                kind = 2 + (ti - 8) // 2
                for g_ in range(2):
                    h = 2 * ((ti - 8) % 2) + g_
                    slot = si % 4
                    si += 1
                    sl = stg[:, slot * T:(slot + 1) * T]
                    for tt in range(4):
                        ts = slice(tt * 512, (tt + 1) * 512)
                        bk = 4 + pc % 2
                        pc += 1
                        for c in range(NCH):
                            P.pe(lambda e, c=c, g_=g_, bk=bk, wt4=wt4, ts=ts: e.matmul(
                                psum[bk], wt4[:, g_, c, :], xn3[:, c, ts], start=(c == 0), stop=(c == NCH - 1)),
                                [wb, xn_b[c * 4 + tt]], [ps_b[bk]])
                        evac(sl[:, ts], psum[bk], [ps_b[bk]], [stg_b[slot]], AF.Sigmoid if kind == 5 else None)
                    P.dma("sp", lambda e, kind=kind, h=h, sl=sl: e.dma_start(out=gin_kh(kind, h), in_=sl),
                          [stg_b[slot]], [gin_b[kind][h]], stg_sem[slot])
            else:
                slot = si % 4
                si += 1
                sl = stg[:, slot * T:(slot + 1) * T]
                for tt in range(4):
                    ts = slice(tt * 512, (tt + 1) * 512)
                    for c in range(NCH):
                        P.pe(lambda e, c=c, wt4=wt4, ts=ts: e.matmul(
                            psum[6][0:40, :], wt4[:, 0, c, 0:40], xn3[:, c, ts], start=(c == 0), stop=(c == NCH - 1)),
                            [wb, xn_b[c * 4 + tt]], [ps_b[6]])
                    P.act(lambda e, ts=ts, sl=sl: e.activation(sl[0:8, ts], psum[6][0:8, :], AF.Copy), [ps_b[6]], [stg_b[slot]])
                    P.act(lambda e, ts=ts: e.activation(stgg[32:40, ts], psum[6][32:40, :], AF.Copy), [ps_b[6]], [stgg_b[0]])
                    P.dve(lambda e, ts=ts, sl=sl: e.tensor_tensor(sl[32:40, ts], psum[6][32:40, :], stgg[32:40, ts], ALU.subtract),
                          [ps_b[6], stgg_b[0]], [stg_b[slot]])
                P.dma("sp", lambda e, sl=sl: e.dma_start(out=gin_c[12], in_=sl), [stg_b[slot]], [ging_b], stg_sem[slot])
        for n in ("xn2", "rope", "stg", "sq", "rs", "t12", "stgg"):
            A.free(n)

        if stop == "proj":
            return
        for ci_ in range(NCHK):
            rd = [ging_b] if ci_ == 12 else [gin_b[ci_ // 2][2 * (ci_ % 2)], gin_b[ci_ // 2][2 * (ci_ % 2) + 1]]
            order = [2, 3, 0, 1, 4, 5, 6, 7, 8, 9, 10, 11, 12]
            cj = order[ci_]
            rd = [ging_b] if cj == 12 else [gin_b[cj // 2][2 * (cj % 2)], gin_b[cj // 2][2 * (cj % 2) + 1]]
            P.dma("pool", lambda e, cj=cj: e.collective_compute("AllGather", ALU.bypass, replica_groups=RG,
                                                               ins=[gin_c[cj].opt()], outs=[gout_c[cj].opt()]),
                  rd, [gout_b[cj]], P.dsem(("cc1", cj)), inc=1)
        if stop == "gather":
            return
        KT, KT_b = A.alloc("KT", PH, [SEQ], BF16, nbuf=16)
        QT, QT_b = A.alloc("QT", PH + 16384, [SEQ], BF16, nbuf=16)
        VT, VT_b = A.alloc("VT", PH + 32768, [SEQ], BF16, nbuf=16)
        cand, cand_b = A.alloc("cand", PH + 49152, [3 * 2048], BF16, nbuf=3)
        cand_sem = [P.dsem(("cd", i)) for i in range(3)]
        ci = [0]

        def load_cand(kind, r, tt):
            slot = ci[0] % 3
            ci[0] += 1
            cs = cand[:, slot * 2048:(slot + 1) * 2048].rearrange("p (h t) -> p h t", h=4)
            for half in range(2):
                P.dma("sp", lambda e, cs=cs, kind=kind, r=r, tt=tt, half=half: e.dma_start(
                    out=cs[:, 2 * half:2 * half + 2, :],
                    in_=gout_c[kind * 2 + half][r * 256:(r + 1) * 256, tt * 512:(tt + 1) * 512].rearrange("(h p) t -> p h t", h=2)),
                    [gout_b[kind * 2 + half]], [cand_b[slot]], cand_sem[slot])
            return cs, cand_b[slot]

        def select_fm(kind, dst, dst_b):
            for r in range(4):
                for tt in range(4):
                    cs, cb = load_cand(kind, r, tt)
                    bk = cnt["ps"] % 2
                    cnt["ps"] += 1
                    for h in range(4):
                        P.pe(lambda e, h=h, cs=cs, bk=bk: e.matmul(psum[bk], sel_bf[:, h * 128:(h + 1) * 128], cs[:, h, :],
                                                                  start=(h == 0), stop=(h == 3)),
                             [cbf_b[0], cb], [ps_b[bk]])
                    g = r * 4 + tt
                    evac(dst[:, g * 512:(g + 1) * 512], psum[bk], [ps_b[bk]], [dst_b[g]])

        def select_tok(kind, dst, dst_b):
            for r in range(4):
                for tt in range(4):
                    cs, cb = load_cand(kind, r, tt)
                    bk = cnt["ps"] % 2
                    cnt["ps"] += 1
                    for sub in range(4):
                        for h in range(4):
                            P.pe(lambda e, h=h, sub=sub, cs=cs, bk=bk: e.matmul(
                                psum[bk][:, sub * 128:(sub + 1) * 128], cs[:, h, sub * 128:(sub + 1) * 128],
                                sel_bf[:, h * 128:(h + 1) * 128], start=(h == 0), stop=(h == 3)),
                                [cbf_b[0], cb], [ps_b[bk]])
                    g = r * 4 + tt
                    evac(dst[:, g * 512:(g + 1) * 512], psum[bk], [ps_b[bk]], [dst_b[g]])

        select_fm(1, KT, KT_b)
        select_fm(0, QT, QT_b)
        select_tok(2, VT, VT_b)

        pbuf, pbuf_b = A.alloc("pbuf", PH + 61440, [6 * 512], BF16, nbuf=6)
        rr, rr_b = A.alloc("rr", PH + 67584, [2 * 512], F32, nbuf=2)
        oo, oo_b = A.alloc("oo", PH + 71680, [2 * 512], F32, nbuf=2)
        sqa, sqa_b = A.alloc("sqa", PH + 75776, [512], BF16)
        rsa, rsa_b = A.alloc("rsa", PH + 76800, [512], F32)
        yst, yst_b = A.alloc("yst", PH + 78848, [2 * 512], BF16, nbuf=2)
        yst_sem = [P.dsem(("ys", i)) for i in range(2)]
        steps = [(gq, kt) for gq in range(16) for kt in range(4 * gq + 4)]
        nst = len(steps)

        def att_qk(i):
            gq, kt = steps[i]
            q0 = gq * 512
            r = kt - 4 * gq
            off = 0 if r < 0 else 128 * r
            sb_ = 4 + (i % 2) * 2
            pslot = (i % 3) * 2
            for c in range(2):
                P.pe(lambda e, c=c: e.matmul(
                    psum[sb_ + c][:, off:512], KT[c * 64:(c + 1) * 64, kt * 128:(kt + 1) * 128],
                    QT[c * 64:(c + 1) * 64, q0 + off:q0 + 512], start=True, stop=True),
                    [KT_b[kt // 4], QT_b[gq]], [ps_b[sb_ + c]])
            for c in range(2):
                pb_ = pbuf[:, (pslot + c) * 512:(pslot + c + 1) * 512]
                P.act(lambda e, c=c, pb_=pb_: e.activation(
                    pb_[:, off:512], psum[sb_ + c][:, off:512], AF.Exp, scale=0.125),
                    [ps_b[sb_ + c]], [pbuf_b[pslot + c]])
                if r >= 0:
                    P.dve(lambda e, pb_=pb_: e.memset(pb_[64:128, off:off + 64], 0.0),
                          [pbuf_b[pslot + c]], [pbuf_b[pslot + c]])

        def att_pv(i):
            gq, kt = steps[i]
            q0 = gq * 512
            nkt = 4 * gq + 4
            r = kt - 4 * gq
            off = 0 if r < 0 else 128 * r
            pslot = (i % 3) * 2
            for c in range(2):
                pb_ = pbuf[:, (pslot + c) * 512:(pslot + c + 1) * 512]
                P.pe(lambda e, c=c, pb_=pb_: e.matmul(
                    psum[c][:, off:512], VT[:, kt * 128:(kt + 1) * 128], pb_[:, off:512],
                    start=(kt == 0), stop=(kt == nkt - 1)),
                    [VT_b[kt // 4], pbuf_b[pslot + c]], [ps_b[c]])
                P.pe(lambda e, c=c, pb_=pb_: e.matmul(
                    psum[2 + c][:, off:512], ones_bf, pb_[:, off:512],
                    start=(kt == 0), stop=(kt == nkt - 1)),
                    [cbf_b[0], pbuf_b[pslot + c]], [ps_b[2 + c]])
            if kt != nkt - 1:
                return
            for c in range(2):
                rl = rr[:, c * 512:(c + 1) * 512]
                ol = oo[:, c * 512:(c + 1) * 512]
                P.dve(lambda e, c=c, rl=rl: e.reciprocal(rl, psum[2 + c]), [ps_b[2 + c]], [rr_b[c]])
                P.dve(lambda e, c=c, rl=rl, ol=ol: e.tensor_tensor(ol, psum[c], rl, ALU.mult), [ps_b[c], rr_b[c]], [oo_b[c]])
            o1, o2 = oo[:, 0:512], oo[:, 512:1024]
            P.dve(lambda e: e.scalar_tensor_tensor(o1, o2, neglam, o1, ALU.mult, ALU.add), [oo_b[0], oo_b[1], sml_b[0]], [oo_b[0]])
            P.act(lambda e: e.activation(sqa, o1, AF.Square), [oo_b[0]], [sqa_b[0]])
            sb_ = 4 + (i % 2) * 2
            P.pe(lambda e: e.matmul(psum[sb_], ones_bf, sqa, start=True, stop=True), [cbf_b[0], sqa_b[0]], [ps_b[sb_]])
            rms_rstd(rsa, psum[sb_], [ps_b[sb_]], [rsa_b[0]], scale=1.0 / 128)
            ys = gq % 2
            yl = yst[:, ys * 512:(ys + 1) * 512]
            P.dve(lambda e: e.scalar_tensor_tensor(yl, o1, gsub, rsa, ALU.mult, ALU.mult),
                  [oo_b[0], rsa_b[0], sml_b[0]], [yst_b[ys]])
            P.dma("sp", lambda e: e.dma_start(out=gin2_c[q0 // T][0:128, q0 % T:q0 % T + 512], in_=yl),
                  [yst_b[ys]], [gin2_b[0][gq]], yst_sem[ys])

        for i in range(nst + 1):
            if i < nst:
                att_qk(i)
            if i >= 1:
                att_pv(i - 1)
        for n in ("KT", "QT", "VT", "pbuf", "rr", "oo", "sqa", "rsa", "yst"):
            A.free(n)
        A.free("cand")
        if stop == "att":
            return

        NPQ = 8208
        PQK, PQK_b = A.alloc("PQK", PH, [NPQ], BF16, nbuf=17)
        PKQ, PKQ_b = A.alloc("PKQ", PH + 16416, [NPQ], BF16, nbuf=17)
        MV, MV_b = A.alloc("MV", PH + 32832, [64 * 129], BF16, nbuf=17)
        MV3 = MV.rearrange("p (t d) -> p t d", t=64)
        MO, MO_b = A.alloc("MO", PH + 49344, [SEQ], BF16, nbuf=16)
        cand, cand_b = A.alloc("cand", PH + 65728, [3 * 2048], BF16, nbuf=3)
        GG, GG_b = A.alloc("GG", PH + 78016, [2 * 2048], BF16, nbuf=2)
        SO = PH + 86208
        gts, gts_b = A.alloc("gts", SO, [128], F32)
        gw, gw_b = A.alloc("gw", SO + 512, [6 * 64], F32)
        gwb, gwb_b = A.alloc("gwb", SO + 2048, [2 * 64], BF16)
        lfb, lfb_b = A.alloc("lfb", SO + 2304, [2 * 256], BF16, nbuf=2)
        dex, dex_b = A.alloc("dex", SO + 3328, [2 * 128], F32, nbuf=2)
        wTt, wT_b = A.alloc("wT", SO + 4352, [2 * 128], BF16, nbuf=2)
        ebt, eb_b = A.alloc("eb", SO + 4864, [2 * 128], F32, nbuf=2)
        qtt, qt_b = A.alloc("qt", SO + 5888, [2 * 128], BF16, nbuf=2)
        sct, sc_b = A.alloc("sc", SO + 6400, [2 * 8], F32, nbuf=2)
        ktt, kt_b = A.alloc("kt", SO + 6464, [2 * 64], BF16, nbuf=2)
        Cs, Cs_b = A.alloc("Cs", SO + 6720, [136], F32)
        Cbf, Cbf_b = A.alloc("Cbf", SO + 7264, [2 * 256], BF16, nbuf=2)
        rdn, rdn_b = A.alloc("rdn", SO + 8288, [2 * 128], F32, nbuf=2)
        acc, acc_b = A.alloc("acc", SO + 9312, [2 * 512], F32, nbuf=2)
        aqk, aqk_b = A.alloc("aqk", SO + 13408, [2 * 512], F32, nbuf=2)
        qkb, qkb_b = A.alloc("qkb", SO + 17504, [2 * 512], BF16, nbuf=2)
        hTm, hTm_b = A.alloc("hTm", SO + 19552, [512], F32)
        sqm, sqm_b = A.alloc("sqm", SO + 21600, [512], BF16)
        rsm, rsm_b = A.alloc("rsm", SO + 22624, [512], F32)
        ysm, ysm_b = A.alloc("ysm", SO + 24672, [2 * 512], BF16, nbuf=2)
        cand_sem = [P.dsem(("cd", i)) for i in range(3)]
        ci[0] = 0

        def load_cand2(kind, r, tt):
            slot = ci[0] % 3
            ci[0] += 1
            cs = cand[:, slot * 2048:(slot + 1) * 2048].rearrange("p (h t) -> p h t", h=4)
            for half in range(2):
                P.dma("sp", lambda e, cs=cs, kind=kind, r=r, tt=tt, half=half: e.dma_start(
                    out=cs[:, 2 * half:2 * half + 2, :],
                    in_=gout_c[kind * 2 + half][r * 256:(r + 1) * 256, tt * 512:(tt + 1) * 512].rearrange("(h p) t -> p h t", h=2)),
                    [gout_b[kind * 2 + half]], [cand_b[slot]], cand_sem[slot])
            return cs, cand_b[slot]

        P.dve(lambda e: e.memset(PQK[:, 0:3], 0.0), [], [PQK_b[16]])
        P.dve(lambda e: e.memset(PKQ[:, 0:3], 0.0), [], [PKQ_b[16]])
        swap_bf = None
        for r in range(4):
            for tt in range(4):
                g = r * 4 + tt
                cs, cb = load_cand2(3, r, tt)
                for dst, dstb, selm in ((PQK, PQK_b, 0), (PKQ, PKQ_b, 1)):
                    bk = cnt["ps"] % 2
                    cnt["ps"] += 1
                    for h in range(4):
                        lhs = sel_bf[:, h * 128:(h + 1) * 128] if selm == 0 else selsw_bf[:, h * 128:(h + 1) * 128]
                        P.pe(lambda e, h=h, cs=cs, bk=bk, lhs=lhs: e.matmul(psum[bk], lhs, cs[:, h, :], start=(h == 0), stop=(h == 3)),
                             [cbf_b[0], cb], [ps_b[bk]])
                    evac(dst[:, 3 + g * 512: 3 + (g + 1) * 512], psum[bk], [ps_b[bk]], [dstb[g]])
        for r in range(4):
            for tt in range(4):
                g = r * 4 + tt
                cs, cb = load_cand2(5, r, tt)
                bk = cnt["ps"] % 2
                cnt["ps"] += 1
                for h in range(4):
                    P.pe(lambda e, h=h, cs=cs, bk=bk: e.matmul(psum[bk], sel_bf[:, h * 128:(h + 1) * 128], cs[:, h, :],
                                                              start=(h == 0), stop=(h == 3)), [cbf_b[0], cb], [ps_b[bk]])
                evac(MO[:, g * 512:(g + 1) * 512], psum[bk], [ps_b[bk]], [MO_b[g]])
        P.dve(lambda e: e.memset(MV3[:, :, 128:129], 1.0), [], [MV_b[16]])
        for r in range(4):
            for tt in range(4):
                g = r * 4 + tt
                cs, cb = load_cand2(4, r, tt)
                bk = cnt["ps"] % 2
                cnt["ps"] += 1
                for sub in range(4):
                    for h in range(4):
                        P.pe(lambda e, h=h, sub=sub, cs=cs, bk=bk: e.matmul(
                            psum[bk][:, sub * 128:(sub + 1) * 128], cs[:, h, sub * 128:(sub + 1) * 128],
                            sel_bf[:, h * 128:(h + 1) * 128], start=(h == 0), stop=(h == 3)), [cbf_b[0], cb], [ps_b[bk]])
                evac(MV3[:, g * 4:(g + 1) * 4, 0:128], psum[bk].rearrange("p (t d) -> p t d", t=4), [ps_b[bk]], [MV_b[g]])
        gg_sem = [P.dsem(("gg", i)) for i in range(2)]
        P.dve(lambda e: e.memset(GG, 0.0), [], [GG_b[0], GG_b[1]])
        for r in range(4):
            sl_ = r % 2
            ggl = GG[:, sl_ * 2048:(sl_ + 1) * 2048]
            P.dma("sp", lambda e, ggl=ggl, r=r: e.dma_start(out=ggl[0:40, :], in_=gout_c[12][r * 128:r * 128 + 40, :]),
                  [gout_b[12]], [GG_b[sl_]], gg_sem[sl_])
            for t in range(16):
                T_ = r * 16 + t
                P.pe(lambda e, T_=T_, t=t, ggl=ggl: e.matmul(psum[3][:, 2 * T_:2 * T_ + 2], ggl[0:8, t * 128:(t + 1) * 128],
                                                            gsel_bf[0:8, :], start=True, stop=False), [GG_b[sl_], cbf_b[0]], [ps_b[3]])
                P.pe(lambda e, T_=T_, t=t, ggl=ggl: e.matmul(psum[3][:, 2 * T_:2 * T_ + 2], ggl[32:40, t * 128:(t + 1) * 128],
                                                            gsel_bf[32:40, :], start=False, stop=True), [GG_b[sl_], cbf_b[0]], [ps_b[3]])
        g3v = gts.rearrange("p (t k) -> p t k", k=2)
        p3v = psum[3][:, 0:128].rearrange("p (t k) -> p t k", k=2)
        for k_ in range(2):
            P.dve(lambda e, k_=k_: e.tensor_scalar(g3v[:, :, k_:k_ + 1], p3v[:, :, k_:k_ + 1],
                                                    cst[:, C_GB + 2 * l + k_:C_GB + 2 * l + k_ + 1], None, ALU.add),
                  [ps_b[3], cst_b[0]], [gts_b[0]])
        e1, lf, lfhf, lfl, ccv = (gw[:, i * 64:(i + 1) * 64] for i in range(5))
        lfh_bf, lfl_bf = gwb[:, 0:64], gwb[:, 64:128]
        gfv = g3v[:, :, 1:2].rearrange("p t k -> p (t k)")
        giv = g3v[:, :, 0:1].rearrange("p t k -> p (t k)")
        P.act(lambda e: e.activation(e1, gfv, AF.Exp, scale=-1.0), [gts_b[0]], [gw_b[0]])
        P.act(lambda e: e.activation(lf, e1, AF.Ln, bias=cst[:, C_ONE:C_ONE + 1], scale=1.0), [gw_b[0], cst_b[0]], [gw_b[0]])
        P.dve(lambda e: e.tensor_scalar(lf, lf, -1.0, None, ALU.mult), [gw_b[0]], [gw_b[0]])
        P.dve(lambda e: e.tensor_copy(lfh_bf, lf), [gw_b[0]], [gwb_b[0]])
        P.dve(lambda e: e.tensor_copy(lfhf, lfh_bf), [gwb_b[0]], [gw_b[0]])
        P.dve(lambda e: e.tensor_tensor(lfl, lf, lfhf, ALU.subtract), [gw_b[0]], [gw_b[0]])
        P.dve(lambda e: e.tensor_copy(lfl_bf, lfl), [gw_b[0]], [gwb_b[0]])
        P.pe(lambda e: e.matmul(psum[2][:, 0:64], tri_bf, lfh_bf, start=True, stop=False), [cbf_b[0], gwb_b[0]], [ps_b[2]])
        P.pe(lambda e: e.matmul(psum[2][:, 0:64], tri_bf, lfl_bf, start=False, stop=True), [cbf_b[0], gwb_b[0]], [ps_b[2]])
        P.dve(lambda e: e.tensor_tensor(ccv, giv, psum[2][:, 0:64], ALU.subtract), [gts_b[0], ps_b[2]], [gw_b[0]])

        P.dve(lambda e: e.memset(Cs, 0.0), [], [Cs_b[0]])
        P.dve(lambda e: e.memset(Cbf, 0.0), [], [Cbf_b[0], Cbf_b[1]])
        ysm_sem = [P.dsem(("ym", i)) for i in range(2)]
        mlg = cst[:, C_MLG + l:C_MLG + l + 1]
        q32 = aqk[0:64, 0:512]
        qb_, kb_ = qkb[0:64, 0:512], qkb[0:64, 512:1024]

        def ml_conv(G):
            g0 = G * 512
            for v, (src, srcb) in enumerate(((PQK, PQK_b), (PKQ, PKQ_b))):
                c0 = C_CONV + 5 * (2 * l + v)
                al = acc[:, v * 512:(v + 1) * 512]
                rdl = [srcb[G], srcb[G - 1] if G > 0 else srcb[16]]
                P.dve(lambda e: e.tensor_scalar(al, src[:, g0 + 3:g0 + 3 + 512], cst[:, c0 + 3:c0 + 4],
                                                cst[:, c0 + 4:c0 + 5], ALU.mult, ALU.add),
                      rdl + [cst_b[0]], [acc_b[v]])
                for k_ in range(3):
                    P.dve(lambda e, k_=k_: e.scalar_tensor_tensor(
                        al, src[:, g0 + k_:g0 + k_ + 512], cst[:, c0 + k_:c0 + k_ + 1], al, ALU.mult, ALU.add),
                        rdl + [cst_b[0], acc_b[v]], [acc_b[v]])
                aql = aqk[:, v * 512:(v + 1) * 512]
                P.act(lambda e: e.activation(aql, al, AF.Silu), [acc_b[v]], [aqk_b[v]])
                qkl = qkb[:, v * 512:(v + 1) * 512]
                P.dve(lambda e: e.tensor_copy(qkl[0:64, :], aql[0:64, :]), [aqk_b[v]], [qkb_b[v]])

        def ml_A(T_):
            G, tt = divmod(T_, 4)
            if tt == 0:
                ml_conv(G)
            cs_ = slice(tt * 128, (tt + 1) * 128)
            s2 = T_ % 2
            b0, b1, b2 = (0, 1, 2) if s2 == 0 else (4, 5, 6)
            lfbl = lfb[:, s2 * 256:(s2 + 1) * 256]
            P.dve(lambda e: e.tensor_scalar(lfbl[:, 0:128], ones_bf, lfhf[:, T_:T_ + 1], None, ALU.mult),
                  [cbf_b[0], gw_b[0]], [lfb_b[s2]])
            P.dve(lambda e: e.tensor_scalar(lfbl[:, 128:256], ones_bf, lfl[:, T_:T_ + 1], None, ALU.mult),
                  [cbf_b[0], gw_b[0]], [lfb_b[s2]])
            pb0, pb1 = psum[b0][:, 0:128], psum[b0][:, 128:256]
            P.pe(lambda e: e.matmul(pb0, lfbl[:, 0:128], tri_bf, start=True, stop=False), [lfb_b[s2], cbf_b[0]], [ps_b[b0]])
            P.pe(lambda e: e.matmul(pb0, lfbl[:, 128:256], tri_bf, start=False, stop=True), [lfb_b[s2], cbf_b[0]], [ps_b[b0]])
            P.pe(lambda e: e.matmul(pb1, lfbl[:, 0:128], tri_bf, start=True, stop=False), [lfb_b[s2], cbf_b[0]], [ps_b[b0]])
            P.pe(lambda e: e.matmul(pb1, lfbl[:, 128:256], tri_bf, start=False, stop=False), [lfb_b[s2], cbf_b[0]], [ps_b[b0]])
            P.pe(lambda e: e.matmul(pb1, I_bf, neg_bf, start=False, stop=True), [cbf_b[0]], [ps_b[b0]])
            dxl = dex[:, s2 * 128:(s2 + 1) * 128]
            ebl = ebt[0:64, s2 * 128:(s2 + 1) * 128]
            scl = sct[:, s2 * 8:(s2 + 1) * 8]
            P.act(lambda e: e.activation(dxl, pb1, AF.Exp, bias=ccv[:, T_:T_ + 1], scale=1.0),
                  [ps_b[b0], gw_b[0]], [dex_b[s2]])
            P.act(lambda e: e.activation(ebl, pb0[0:64, :], AF.Exp), [ps_b[b0]], [eb_b[s2]])
            P.act(lambda e: e.activation(scl[:, 0:1], pb0[:, 127:128], AF.Exp, bias=ccv[:, T_:T_ + 1], scale=1.0),
                  [ps_b[b0], gw_b[0]], [sc_b[s2]])
            P.act(lambda e: e.activation(scl[0:64, 1:2], pb0[0:64, 127:128], AF.Exp), [ps_b[b0]], [sc_b[s2]])
            P.pe(lambda e: e.matmul(psum[b1][:, 0:128], kb_[:, cs_], qb_[:, cs_], start=True, stop=True),
                 [qkb_b[0], qkb_b[1]], [ps_b[b1]])
            P.pe(lambda e: e.matmul(psum[b1][:, 128:192], kb_[:, cs_], I_bf[0:64, 0:64], start=True, stop=True),
                 [qkb_b[1], cbf_b[0]], [ps_b[b1]])
            wl = wTt[:, s2 * 128:(s2 + 1) * 128]
            ql = qtt[0:64, s2 * 128:(s2 + 1) * 128]
            kl = ktt[:, s2 * 64:(s2 + 1) * 64]
            P.dve(lambda e: e.scalar_tensor_tensor(wl, psum[b1][:, 0:128], 0.125, dxl, ALU.mult, ALU.mult),
                  [ps_b[b1], dex_b[s2]], [wT_b[s2]])
            P.dve(lambda e: e.scalar_tensor_tensor(ql, q32[:, cs_], 0.125, ebl, ALU.mult, ALU.mult),
                  [aqk_b[0], eb_b[s2]], [qt_b[s2]])
            P.dve(lambda e: e.tensor_scalar(kl, psum[b1][:, 128:192], scl[:, 0:1], None, ALU.mult),
                  [ps_b[b1], sc_b[s2]], [kt_b[s2]])

        def ml_B(T_):
            G, tt = divmod(T_, 4)
            g0 = G * 512
            gs = G % 2
            cs_ = slice(tt * 128, (tt + 1) * 128)
            s2 = T_ % 2
            b0, b1, b2 = (0, 1, 2) if s2 == 0 else (4, 5, 6)
            scl = sct[:, s2 * 8:(s2 + 1) * 8]
            wl = wTt[:, s2 * 128:(s2 + 1) * 128]
            ql = qtt[0:64, s2 * 128:(s2 + 1) * 128]
            kl = ktt[:, s2 * 64:(s2 + 1) * 64]
            cprev = Cbf[0:64, s2 * 256:(s2 + 1) * 256]
            cnext = Cbf[0:64, (1 - s2) * 256:(2 - s2) * 256]
            P.pe(lambda e: e.matmul(psum[b2][:, 0:128], MV3[:, T_, 0:128], wl, start=True, stop=False),
                 [MV_b[T_ // 4], MV_b[16], wT_b[s2]], [ps_b[b2]])
            P.pe(lambda e: e.matmul(psum[b2][:, 0:128], cprev[:, 0:128], ql, start=False, stop=True),
                 [Cbf_b[s2], qt_b[s2]], [ps_b[b2]])
            P.pe(lambda e: e.matmul(psum[b2][:, 128:256], ones_bf, wl, start=True, stop=False), [cbf_b[0], wT_b[s2]], [ps_b[b2]])
            P.pe(lambda e: e.matmul(psum[b2][:, 128:256], cprev[:, 128:256], ql, start=False, stop=True),
                 [Cbf_b[s2], qt_b[s2]], [ps_b[b2]])
            P.pe(lambda e: e.matmul(psum[b1][0:64, 256:385], kl, MV3[:, T_, :], start=True, stop=True),
                 [kt_b[s2], MV_b[T_ // 4], MV_b[16]], [ps_b[b1]])
            P.dve(lambda e: e.scalar_tensor_tensor(Cs[0:64, 0:129], Cs[0:64, 0:129], scl[0:64, 1:2], psum[b1][0:64, 256:385],
                                                   ALU.mult, ALU.add), [Cs_b[0], sc_b[s2], ps_b[b1]], [Cs_b[0]])
            P.dve(lambda e: e.tensor_copy(cnext[:, 0:128], Cs[0:64, 0:128]), [Cs_b[0]], [Cbf_b[1 - s2]])
            P.dve(lambda e: e.tensor_scalar(cnext[:, 128:256], ones_bf[0:64, :], Cs[0:64, 128:129], None, ALU.mult),
                  [Cs_b[0], cbf_b[0]], [Cbf_b[1 - s2]])
            rl = rdn[:, s2 * 128:(s2 + 1) * 128]
            P.act(lambda e: e.activation(rl, psum[b2][:, 128:256], AF.Abs), [ps_b[b2]], [rdn_b[s2]])
            P.dve(lambda e: e.tensor_scalar(rl, rl, 1.0, None, ALU.max), [rdn_b[s2]], [rdn_b[s2]])
            P.dve(lambda e: e.reciprocal(rl, rl), [rdn_b[s2]], [rdn_b[s2]])
            P.dve(lambda e: e.tensor_tensor(hTm[:, cs_], psum[b2][:, 0:128], rl, ALU.mult),
                  [ps_b[b2], rdn_b[s2]], [hTm_b[0]])
            if tt != 3:
                return
            P.act(lambda e: e.activation(sqm, hTm, AF.Square), [hTm_b[0]], [sqm_b[0]])
            bs = 3 if gs == 0 else 7
            P.pe(lambda e: e.matmul(psum[bs], ones_bf, sqm, start=True, stop=True), [cbf_b[0], sqm_b[0]], [ps_b[bs]])
            rms_rstd(rsm, psum[bs], [ps_b[bs]], [rsm_b[0]], scale=1.0 / 128)
            yl = ysm[:, gs * 512:(gs + 1) * 512]
            P.dve(lambda e: e.scalar_tensor_tensor(hTm, hTm, mlg, rsm, ALU.mult, ALU.mult), [hTm_b[0], rsm_b[0], cst_b[0]], [hTm_b[0]])
            P.dve(lambda e: e.tensor_tensor(yl, hTm, MO[:, g0:g0 + 512], ALU.mult), [hTm_b[0], MO_b[G]], [ysm_b[gs]])
            P.dma("sp", lambda e: e.dma_start(out=gin2_c[g0 // T][128:256, g0 % T:g0 % T + 512], in_=yl),
                  [ysm_b[gs]], [gin2_b[1][G]], ysm_sem[gs])

        for T_ in range(65):
            if T_ < 64:
                ml_A(T_)
            if T_ >= 1:
                ml_B(T_ - 1)
        for n in ("PQK", "PKQ", "MV", "MO", "cand", "GG", "gts", "gw", "gwb", "lfb", "dex", "wT", "eb", "qt", "sc", "kt",
                  "Cs", "Cbf", "rdn", "acc", "aqk", "qkb", "hTm", "sqm", "rsm", "ysm"):
            A.free(n)
        if stop == "mls":
            return

        for b_ in range(4):
            rd = [gin2_b[k_][b_ * 4 + i] for k_ in range(2) for i in range(4)]
            P.dma("pool", lambda e, b_=b_: e.collective_compute("AllGather", ALU.bypass, replica_groups=RG,
                                                               ins=[gin2_c[b_].opt()], outs=[gout2_c[b_].opt()]),
                  rd, [gout2_b[b_]], P.dsem(("cc2", b_)), inc=1)

        yT, yT_b = A.alloc("yT", PH, [NCH * T], BF16, nbuf=NCH * 4)
        yT3 = yT.rearrange("p (c t) -> p c t", c=NCH)
        hO, hO_b = A.alloc("hO", PH + 32768, [NCH * T], F32, nbuf=NCH * 4)
        hO3 = hO.rearrange("p (c t) -> p c t", c=NCH)
        cand, cand_b = A.alloc("cand", PH + 98304, [3 * 2048], BF16, nbuf=3)
        sqo, sqo_b = A.alloc("sqo", PH + 110592, [2 * 512], BF16, nbuf=2)
        rso, rso_b = A.alloc("rso", PH + 112640, [512], F32)
        cand_sem = [P.dsem(("cd", i)) for i in range(3)]
        ci[0] = 0
        for c8 in range(NCH):
            kind_, r = divmod(c8, 4)
            for tt in range(4):
                slot = ci[0] % 3
                ci[0] += 1
                cs = cand[:, slot * 2048:(slot + 1) * 2048].rearrange("p (h t) -> p h t", h=4)
                for b_ in range(4):
                    P.dma("sp", lambda e, cs=cs, b_=b_, r=r, kind_=kind_, tt=tt: e.dma_start(
                        out=cs[:, b_, :], in_=gout2_c[b_][r * 256 + kind_ * 128:r * 256 + kind_ * 128 + 128, tt * 512:(tt + 1) * 512]),
                        [gout2_b[b_]], [cand_b[slot]], cand_sem[slot])
                bk = cnt["ps"] % 2
                cnt["ps"] += 1
                for b_ in range(4):
                    P.pe(lambda e, b_=b_, cs=cs, bk=bk: e.matmul(psum[bk], sel_bf[:, b_ * 128:(b_ + 1) * 128], cs[:, b_, :],
                                                                start=(b_ == 0), stop=(b_ == 3)), [cbf_b[0], cand_b[slot]], [ps_b[bk]])
                evac(yT3[:, c8, tt * 512:(tt + 1) * 512], psum[bk], [ps_b[bk]], [yT_b[c8 * 4 + tt]])
        it = 0
        for m in range(NCH):
            wt, wb = W.get(("wo", l, m))
            wt3 = wt.rearrange("p (c n) -> p c n", c=NCH)
            for tt in range(4):
                ts = slice(tt * 512, (tt + 1) * 512)
                bk = 2 + it % 2
                for c in range(NCH):
                    P.pe(lambda e, c=c, bk=bk, wt3=wt3, ts=ts: e.matmul(psum[bk], wt3[:, c, :], yT3[:, c, ts],
                                                                         start=(c == 0), stop=(c == NCH - 1)),
                         [wb, yT_b[c * 4 + tt]], [ps_b[bk]])
                P.act(lambda e, m=m, ts=ts, bk=bk: e.activation(hO3[:, m, ts], psum[bk], AF.Copy), [ps_b[bk]], [hO_b[m * 4 + tt]])
                sql = sqo[:, (it % 2) * 512:(it % 2 + 1) * 512]
                P.act(lambda e, sql=sql, bk=bk: e.activation(sql, psum[bk], AF.Square), [ps_b[bk]], [sqo_b[it % 2]])
                P.pe(lambda e, m=m, tt=tt, sql=sql: e.matmul(psum[4 + tt], ones_bf, sql, start=(m == 0), stop=(m == NCH - 1)),
                     [cbf_b[0], sqo_b[it % 2]], [ps_b[4 + tt]])
                it += 1
        for tt in range(4):
            ts = slice(tt * 512, (tt + 1) * 512)
            rms_rstd(rso, psum[4 + tt], [ps_b[4 + tt]], [rso_b[0]])
            for m in range(NCH):
                P.dve(lambda e, m=m, ts=ts: e.tensor_tensor(hO3[:, m, ts], hO3[:, m, ts], rso, ALU.mult),
                      [hO_b[m * 4 + tt], rso_b[0]], [hO_b[m * 4 + tt]])
                P.dve(lambda e, m=m, ts=ts: e.scalar_tensor_tensor(xT3[:, m, ts], hO3[:, m, ts], gain(l, 3, m), xT3[:, m, ts],
                                                                   ALU.mult, ALU.add),
                      [hO_b[m * 4 + tt], cst_b[0], xT_b[tt]], [xT_b[tt]])
        for n in ("yT", "hO", "cand", "sqo", "rso"):
            A.free(n)

    for ph in phases:
        if ph[0] == "ffn":
            ffn_phase(ph[1], ph[2])
        else:
            mix_phase(ph[1])

    st = P.new_dsem("st")
    evs = []
    if stop == "att":
        for b_ in range(4):
            evs.append(P.dma("sp", lambda e, b_=b_: e.dma_start(out=dbg_d[0:128, b_ * T:(b_ + 1) * T], in_=gin2_c[b_][0:128, :]),
                             [b for b in gin2_b[0]], [], st))
    if stop == "mls":
        for b_ in range(4):
            evs.append(P.dma("sp", lambda e, b_=b_: e.dma_start(out=dbg_d[:, b_ * T:(b_ + 1) * T], in_=gin2_c[b_]),
                             [b for k_ in range(2) for b in gin2_b[k_]], [], st))
    if stop == "proj":
        evs.append(P.dma("sp", lambda e: e.dma_start(out=dbg_d[0:128, 0:T], in_=gin_kh(1, 1)), [gin_b[1][1]], [], st))
        evs.append(P.dma("sp", lambda e: e.dma_start(out=dbg_d[128:256, 0:T], in_=gin_kh(0, 2)), [gin_b[0][2]], [], st))
    if stop == "gather":
        for r in range(4):
            evs.append(P.dma("sp", lambda e, r=r: e.dma_start(out=dbg_d[0:128, r * 512:(r + 1) * 512], in_=gout_rkh(r, 1, 1)[:, 0:512]),
                             gout_b, [], st))
            evs.append(P.dma("sp", lambda e, r=r: e.dma_start(out=dbg_d[128:256, r * 512:(r + 1) * 512], in_=gout_rkh(r, 0, 2)[:, 0:512]),
                             gout_b, [], st))
    if stop is not None:
        evs.append(P.dma("sp", lambda e: e.dma_start(out=out_d, in_=xT[:, 0:16]), [xT_b[0]], [], st))
    for tt in range(4 if stop is None else 0):
        evs.append(P.dma("sp", lambda e, tt=tt: e.dma_start(
            out=out_d.rearrange("p (c t) -> p c t", c=NCH)[:, :, tt * 512:(tt + 1) * 512],
            in_=xT3[:, :, tt * 512:(tt + 1) * 512]), [xT_b[tt]], [], st))
    P.fence("sp", evs)
    P.emit()
    return nc, P


def _mix_tile_cols():
    tiles = []
    rng = np.arange(128)
    perm = (rng // 64) * 64 + ((rng % 64) + 32) % 64
    for base in (0, 512):
        for h in range(4):
            tiles.append((base + h * 128 + rng, base + h * 128 + perm))
    for h0 in (0, 2):
        tiles.append((1024 + h0 * 128 + rng, 1024 + (h0 + 1) * 128 + rng))
    r64 = np.arange(64)
    for h0 in (0, 2):
        tiles.append(tuple(np.concatenate([1536 + h * 64 + r64, 1792 + h * 64 + r64]) for h in (h0, h0 + 1)))
    for base in (2048, 2560):
        for h0 in (0, 2):
            tiles.append((base + h0 * 128 + rng, base + (h0 + 1) * 128 + rng))
    gcols = np.full(128, -1)
    gcols[:8] = 3072 + np.arange(8)
    gcols[32:40] = 3072 + np.arange(8)
    tiles.append((gcols, np.full(128, -1)))
    return tiles


def _prep_shared(inp, ffn_ids, mix_ids):
    f = np.float32
    out = {}
    w_in = np.asarray(inp["ffn_w_in"], f).reshape(DEPTH * 2, D, 2 * DFF)
    w_out = np.asarray(inp["ffn_w_out"], f).reshape(DEPTH * 2, DFF, D)
    if ffn_ids:
        wi = w_in[ffn_ids]
        n = len(ffn_ids)
        w1 = wi.reshape(n, NCH, 128, 2, NF, 128).transpose(0, 4, 2, 3, 1, 5)
        out["w1"] = np.ascontiguousarray(w1).reshape(n * NF, 128, 2 * NCH * 128)
        wo_ = w_out[ffn_ids]
        w2 = wo_.reshape(n, NF, 128, NCH, 128).transpose(0, 3, 2, 1, 4)
        out["w2"] = np.ascontiguousarray(w2).reshape(n * NCH, 128, NF * 128)
    else:
        out["w1"] = np.zeros((NF, 128, 2 * NCH * 128), f)
        out["w2"] = np.zeros((NCH, 128, NF * 128), f)
    if mix_ids:
        tiles = _mix_tile_cols()
        wm = np.zeros((len(mix_ids), 17, 128, 2, NCH, 128), f)
        wo = np.zeros((len(mix_ids), NCH, 128, NCH, 128), f)
        for i, l in enumerate(mix_ids):
            Wl = np.asarray(inp["mix_w_in"][l], f)
            Wp = np.concatenate([Wl, np.zeros((D, 1), f)], axis=1)
            for ti, groups in enumerate(tiles):
                for g, cols in enumerate(groups):
                    blk = Wp[:, cols]
                    wm[i, ti, :, g] = blk.reshape(NCH, 128, 128).transpose(1, 0, 2)
            Wo = np.asarray(inp["mix_w_out"][l], f)
            wo[i] = Wo.reshape(NCH, 128, NCH, 128).transpose(2, 1, 0, 3)
        out["wm"] = wm.reshape(len(mix_ids) * 17, 128, 2048)
        out["wo"] = wo.reshape(len(mix_ids) * NCH, 128, 1024)
    else:
        out["wm"] = np.zeros((17, 128, 2048), f)
        out["wo"] = np.zeros((NCH, 128, 1024), f)
    return out


def _prep_core(inp, core):
    f = np.float32
    b, j = divmod(core, 4)
    cst = np.zeros((128, NCST), f)
    g = np.asarray(inp["norm_gains"], f)
    cst[:, 0:96] = g.reshape(DEPTH * 6, NCH, 128).transpose(2, 0, 1).reshape(128, 96)
    cst[:, C_EPS] = EPS
    cst[:, C_ONE] = 1.0
    cw = np.asarray(inp["ml_conv_w"], f)
    cb = np.asarray(inp["ml_conv_b"], f)
    gb = np.asarray(inp["ml_gate_b"], f)
    for l in range(DEPTH):
        cst[:, C_SUBG + l] = np.asarray(inp["da_subln_g"][l], f)
        cst[:, C_MLG + l] = np.asarray(inp["ml_norm_g"][l], f)
        cst[:, C_GB + 2 * l] = gb[l, j]
        cst[:, C_GB + 2 * l + 1] = gb[l, 4 + j]
        qi = j * 64 + np.arange(64)
        ki = 256 + j * 64 + np.arange(64)
        for v, idx in enumerate((np.concatenate([qi, ki]), np.concatenate([ki, qi]))):
            c0 = C_CONV + 5 * (2 * l + v)
            cst[:, c0:c0 + 4] = cw[l][:, idx].T
            cst[:, c0 + 4] = cb[l][idx]
        cst[:, C_LAM + 256 * l: C_LAM + 256 * (l + 1)] = np.asarray(inp["da_lambda"][l], f).reshape(1, 256)
    cst[j, C_GSEL] = 1.0
    cst[4 + j, C_GSEL + 1] = 1.0
    cst[32 + j, C_GSEL] = 1.0
    cst[32 + 4 + j, C_GSEL + 1] = 1.0
    sw = np.zeros((128, 128), f)
    sw[(np.arange(128) + 64) % 128, np.arange(128)] = 1.0
    cst[:, C_SELSW + j * 128: C_SELSW + (j + 1) * 128] = sw
    cst[:, C_I:C_I + 128] = np.eye(128, dtype=f)
    ii = np.arange(128)
    cst[:, C_TRI:C_TRI + 128] = (ii[:, None] <= ii[None, :]).astype(f)
    cst[:, C_NEG:C_NEG + 128] = np.where(ii[:, None] > ii[None, :], -30000.0, 0.0).astype(f)
    cst[:, C_SEL + j * 128: C_SEL + (j + 1) * 128] = np.eye(128, dtype=f)
    pos = (j * T + np.arange(T)).astype(np.float64)
    inv = 10000.0 ** (-np.arange(0, 64, 2, dtype=np.float64) / 64.0)
    d = np.arange(128) % 64
    ang = pos[None, :] * inv[d % 32][:, None]
    sgn = np.where(d < 32, -1.0, 1.0)[:, None]
    rope = np.concatenate([np.cos(ang), sgn * np.sin(ang)], axis=1).astype(f)
    return {"cst": cst, "rope": rope}


def _shard_x(x):
    xs = []
    for c in range(8):
        b, q = divmod(c, 4)
        xc = np.asarray(x[b, q * T:(q + 1) * T, :], np.float32)
        xs.append(np.ascontiguousarray(xc.reshape(T, NCH, 128).transpose(2, 1, 0)).reshape(128, NCH * T))
    return xs


def _unshard(outs):
    y = np.zeros((2, SEQ, D), np.float32)
    for c in range(8):
        b, q = divmod(c, 4)
        o = np.asarray(outs[c], np.float32).reshape(128, NCH, T).transpose(2, 1, 0).reshape(T, D)
        y[b, q * T:(q + 1) * T, :] = o
    return y


ALL_PHASES = [("ffn", 0, 0), ("mix", 0), ("ffn", 0, 1), ("ffn", 1, 0), ("mix", 1), ("ffn", 1, 1)]


def run_phases(inputs, phases, stop=None):
    ffn_ids = sorted({ph[1] * 2 + ph[2] for ph in phases if ph[0] == "ffn"})
    mix_ids = sorted({ph[1] for ph in phases if ph[0] == "mix"})
    shared = _prep_shared(inputs, ffn_ids, mix_ids)
    xs = _shard_x(inputs["x"])
    nc, P = build_program(phases, stop=stop)
    in_maps = [dict(shared, xT=xs[c], **_prep_core(inputs, c)) for c in range(8)]
    res = run_bass_kernel_spmd(nc, in_maps, core_ids=list(range(8)))
    if stop is not None:
        return None, [np.asarray(r["dbg"]) for r in res.results]
    return _unshard([r["out"] for r in res.results])


def kernel(**inputs):
    return run_phases(inputs, ALL_PHASES)
```

```python
import numpy as np
import concourse.bass as bass
import concourse.mybir as mybir
from concourse.bass_utils import run_bass_kernel_spmd

F32 = mybir.dt.float32
BF16 = mybir.dt.bfloat16
ALU = mybir.AluOpType
AF = mybir.ActivationFunctionType

D = 1024
NCH = 8
DFF = 2816
NF = 22
DEPTH = 2
T = 2048
SEQ = 8192
EPS = 1e-6

ENGS = ("pe", "act", "dve", "pool", "sp")
EPOCH = 3000


class Buf:
    __slots__ = ("name", "w", "r", "excl")

    def __init__(self, name, inherit=None, excl=False):
        self.name = name
        self.w = None
        self.r = list(inherit) if inherit else []
        self.excl = excl


class Ev:
    __slots__ = ("key", "val", "clock", "op")

    def __init__(self, key, val, clock, op):
        self.key, self.val, self.clock, self.op = key, val, clock, op


class DSem:
    def __init__(self, sem):
        self.sem = sem
        self.count = 0


class Op:
    __slots__ = ("eng", "fn", "waits", "ev", "signal", "sig_no", "dsem", "inc")


class _Rec:
    def __init__(self):
        self.call = None

    def __getattr__(self, name):
        def f(*a, **k):
            assert self.call is None
            self.call = (name, a, k)
            return None
        return f


def _compress(events):
    best = {}
    for ev in events:
        cur = best.get(ev.key)
        if cur is None or cur.val < ev.val:
            best[ev.key] = ev
    return list(best.values())


class Prog:
    def __init__(self, nc):
        self.nc = nc
        self.ops = {e: [] for e in ENGS}
        self.known = {e: {} for e in ENGS}
        self.nwaits = 0
        self._nsem = 0

    def new_dsem(self, name="d"):
        self._nsem += 1
        return DSem(self.nc.alloc_semaphore(name=f"{name}_{self._nsem}"))

    def dsem(self, key):
        if not hasattr(self, "_cache"):
            self._cache = {}
        if key not in self._cache:
            self._cache[key] = self.new_dsem(str(key))
        return self._cache[key]

    def _add(self, eng, fn, reads, writes, dsem=None, inc=16, extra=()):
        op = Op()
        if fn is not None:
            rec = _Rec()
            fn(rec)
            assert rec.call is not None
            fn = rec.call
        op.eng, op.fn, op.dsem, op.signal, op.sig_no, op.inc = eng, fn, dsem, False, 0, inc
        deps = {}
        for b in reads:
            if b.w is not None:
                deps[id(b.w)] = b.w
            if b.excl:
                for r in b.r:
                    if r.key != eng:
                        deps[id(r)] = r
        for b in writes:
            if b.w is not None:
                deps[id(b.w)] = b.w
            for r in b.r:
                deps[id(r)] = r
        for ev in extra:
            deps[id(ev)] = ev
        known = self.known[eng]
        waits = {}
        for ev in deps.values():
            if eng == "pe" and ev.key == "pe":
                continue
            if known.get(ev.key, 0) >= ev.val:
                continue
            cur = waits.get(ev.key)
            if cur is None or cur.val < ev.val:
                waits[ev.key] = ev
        if waits:
            known = dict(known)
            for ev in waits.values():
                for k, v in ev.clock.items():
                    if known.get(k, 0) < v:
                        known[k] = v
                if known.get(ev.key, 0) < ev.val:
                    known[ev.key] = ev.val
            self.known[eng] = known
            self.nwaits += len(waits)
        op.waits = list(waits.values())
        for ev in op.waits:
            if ev.op.dsem is None:
                ev.op.signal = True
        self.ops[eng].append(op)
        idx = len(self.ops[eng])
        if dsem is None:
            ev = Ev(eng, idx, known, op)
        else:
            dsem.count += inc
            ev = Ev(dsem, dsem.count, known, op)
        op.ev = ev
        for b in reads:
            b.r.append(ev)
        for b in writes:
            b.w = ev
            b.r = []
        return ev

    def pe(self, fn, reads, writes):
        return self._add("pe", fn, reads, writes)

    def act(self, fn, reads, writes):
        return self._add("act", fn, reads, writes)

    def dve(self, fn, reads, writes):
        return self._add("dve", fn, reads, writes)

    def dma(self, queue, fn, reads, writes, dsem, inc=16, extra=()):
        return self._add(queue, fn, reads, writes, dsem=dsem, inc=inc, extra=extra)

    def fence(self, eng, events):
        return self._add(eng, None, [], [], extra=events)

    def emit(self):
        nc = self.nc
        esems = {}
        for e in ENGS:
            n = 0
            for op in self.ops[e]:
                if op.signal:
                    n += 1
                    op.sig_no = n
            nep = max(1, (n + EPOCH - 1) // EPOCH)
            esems[e] = [nc.alloc_semaphore(name=f"prog_{e}_{i}") for i in range(nep)]

        def sem_of(ev):
            if isinstance(ev.key, DSem):
                return ev.key.sem, ev.val
            n = ev.op.sig_no
            assert n > 0
            return esems[ev.key][(n - 1) // EPOCH], (n - 1) % EPOCH + 1

        def run(eng_name, eng):
            for op in self.ops[eng_name]:
                for ev in op.waits:
                    s, v = sem_of(ev)
                    eng.wait_ge(s, v)
                if op.fn is None:
                    if op.signal:
                        s, v = sem_of(op.ev)
                        eng.nop().then_inc(s, 1)
                    continue
                name, a, k = op.fn
                ins = getattr(eng, name)(*a, **k)
                if op.dsem is not None:
                    ins.then_inc(op.dsem.sem, op.inc)
                elif op.signal:
                    s, v = sem_of(op.ev)
                    ins.then_inc(s, 1)

        with nc.Block() as block:
            @block.tensor
            def _(e):
                run("pe", e)

            @block.scalar
            def _(e):
                run("act", e)

            @block.vector
            def _(e):
                run("dve", e)

            @block.gpsimd
            def _(e):
                run("pool", e)

            @block.sync
            def _(e):
                run("sp", e)


class Arena:
    def __init__(self, nc, base, top):
        self.nc, self.base, self.top = nc, base, top
        self.live = {}
        self.dead = []
        self.n = 0

    def alloc(self, name, off, free_shape, dtype, nbuf=1):
        esz = 4 if dtype == F32 else 2
        cols = int(np.prod(free_shape))
        start = self.base + off
        end = start + cols * esz
        assert end <= self.top, (name, end, self.top)
        assert start % 32 == 0, (name, start)
        for k, (s, e, _) in self.live.items():
            assert e <= start or s >= end, f"arena overlap {name} vs {k}"
        inherit = []
        keep = []
        for (s, e, evs) in self.dead:
            if not (e <= start or s >= end):
                inherit.extend(evs)
                if s >= start and e <= end:
                    continue
            keep.append((s, e, evs))
        self.dead = keep
        inherit = _compress(inherit)
        self.n += 1
        th = self.nc.alloc_sbuf_tensor_at(f"{name}{self.n}", [128, cols], dtype, offset=start)
        bl = [Buf(f"{name}_{i}", inherit) for i in range(nbuf)]
        self.live[name] = (start, end, bl)
        return th.ap(), bl

    def free(self, name):
        s, e, bl = self.live.pop(name)
        evs = []
        for b in bl:
            if b.w is not None:
                evs.append(b.w)
            evs.extend(b.r)
        self.dead.append((s, e, _compress(evs)))


class WStream:
    NSLOT = 4
    SLOT_ELEMS = 2816

    def __init__(self, P, arena, off):
        self.P = P
        self.ap, self.bufs = arena.alloc("wring", off, [self.NSLOT * self.SLOT_ELEMS], BF16, nbuf=self.NSLOT)
        self.sems = [P.new_dsem("w") for _ in range(self.NSLOT)]
        self.tiles = []
        self.issued = 0
        self.cursor = 0

    def plan(self, tag, dram_ap, nelem):
        assert nelem <= self.SLOT_ELEMS
        self.tiles.append((tag, dram_ap, nelem))

    def _issue(self, i):
        tag, src, nelem = self.tiles[i]
        s = i % self.NSLOT
        dst = self.ap[:, s * self.SLOT_ELEMS: s * self.SLOT_ELEMS + nelem]
        self.P.dma("pool", lambda e, dst=dst, src=src: e.dma_start(out=dst, in_=src),
                   [], [self.bufs[s]], self.sems[s])

    def get(self, tag):
        while self.issued < min(len(self.tiles), self.cursor + self.NSLOT):
            self._issue(self.issued)
            self.issued += 1
        i = self.cursor
        assert self.tiles[i][0] == tag, (self.tiles[i][0], tag)
        self.cursor += 1
        s = i % self.NSLOT
        nelem = self.tiles[i][2]
        return self.ap[:, s * self.SLOT_ELEMS: s * self.SLOT_ELEMS + nelem], self.bufs[s]


NKIND = 6
RG = [[0, 1, 2, 3], [4, 5, 6, 7]]
NT = SEQ // 128
C_EPS = 200
C_SUBG = 208
C_MLG = 210
C_GB = 216
C_CONV = 220
C_GSEL = 240
C_ONE = 244
C_LAM = 256
C_I = 768
C_TRI = 896
C_NEG = 1024
C_SEL = 1152
C_SELSW = 1664
NCST = 2176


def build_program(phases, stop=None):
    import math
    nc = bass.Bass("TRN2", target_bir_lowering=False)
    P = Prog(nc)
    ffn_ids = sorted({ph[1] * 2 + ph[2] for ph in phases if ph[0] == "ffn"})
    mix_ids = sorted({ph[1] for ph in phases if ph[0] == "mix"})
    x_in = nc.dram_tensor("xT", [128, NCH * T], F32, kind="ExternalInput").ap()
    w1_d = nc.dram_tensor("w1", [max(1, len(ffn_ids)) * NF, 128, 2 * NCH * 128], F32, kind="ExternalInput").ap()
    w2_d = nc.dram_tensor("w2", [max(1, len(ffn_ids)) * NCH, 128, NF * 128], F32, kind="ExternalInput").ap()
    wm_d = nc.dram_tensor("wm", [max(1, len(mix_ids)) * 17, 128, 2048], F32, kind="ExternalInput").ap()
    wo_d = nc.dram_tensor("wo", [max(1, len(mix_ids)) * NCH, 128, 1024], F32, kind="ExternalInput").ap()
    rope_d = nc.dram_tensor("rope", [128, 2 * T], F32, kind="ExternalInput").ap()
    cst_d = nc.dram_tensor("cst", [128, NCST], F32, kind="ExternalInput").ap()
    out_d = nc.dram_tensor("out", [128, NCH * T if stop is None else 16], F32, kind="ExternalOutput").ap()
    dbg_d = None
    if stop is not None:
        dbg_d = nc.dram_tensor("dbg", [256, SEQ if stop in ("att", "mls") else 2048], BF16, kind="ExternalOutput").ap()
    NCHK = 13
    gin_c = [nc.dram_tensor(f"gin{i}", [256 if i < 12 else 128, T], BF16).ap() for i in range(NCHK)]
    gout_c = [nc.dram_tensor(f"gout{i}", [4 * (256 if i < 12 else 128), T], BF16).ap() for i in range(NCHK)]
    gin2_c = [nc.dram_tensor(f"gin2_{i}", [256, T], BF16).ap() for i in range(4)]
    gout2_c = [nc.dram_tensor(f"gout2_{i}", [4 * 256, T], BF16).ap() for i in range(4)]
    gin_b = [[Buf(f"gin{k}_{h}") for h in range(4)] for k in range(NKIND)]
    ging_b = Buf("ging")
    gout_b = [Buf(f"gout{i}") for i in range(NCHK)]
    gin2_b = [[Buf(f"gin2_{k}_{q}") for q in range(16)] for k in range(2)]
    gout2_b = [Buf(f"gout2_{i}") for i in range(4)]

    def gin_kh(kind, h):
        return gin_c[kind * 2 + h // 2][(h % 2) * 128:(h % 2) * 128 + 128, :]

    def gout_rkh(r, kind, h):
        base = r * 256 + (h % 2) * 128
        return gout_c[kind * 2 + h // 2][base:base + 128, :]

    A = Arena(nc, 16512, 229344)
    xT, xT_b = A.alloc("xT", 0, [NCH * T], F32, nbuf=4)
    xT3 = xT.rearrange("p (c t) -> p c t", c=NCH)
    W = WStream(P, A, 65536)
    CB = 88064
    cst, cst_b = A.alloc("cst", CB, [768], F32)
    cbf, cbf_b = A.alloc("cbf", CB + 3072, [1552], BF16)
    ones_bf, I_bf, tri_bf, neg_bf, sel_bf = (cbf[:, 0:128], cbf[:, 128:256], cbf[:, 256:384],
                                             cbf[:, 384:512], cbf[:, 512:1024])
    ones_b = cbf_b
    sml, sml_b = A.alloc("sml", CB + 3072 + 3104, [128], F32)
    PH = CB + 3072 + 3104 + 512
    gsel_bf = cbf[:, 1024:1026]
    selsw_bf = cbf[:, 1040:1552]
    psum = [nc.alloc_psum_tensor(f"ps{i}", [128, 512], F32).ap() for i in range(8)]
    ps_b = [Buf(f"ps{i}", excl=True) for i in range(8)]
    ld = [P.new_dsem("ld") for _ in range(4)]
    misc = P.new_dsem("misc")
    rope_sem = P.new_dsem("rope")
    ging_sem = P.new_dsem("ging")

    for ph in phases:
        kind = ph[0]
        if kind == "ffn":
            _, l, k = ph
            lf = l * 2 + k
            fi = ffn_ids.index(lf)
            for ps_ in range(2):
                for f in range(NF):
                    W.plan(("w1", lf, ps_, f), w1_d[fi * NF + f], 2 * NCH * 128)
                for m in range(NCH):
                    W.plan(("w2", lf, ps_, m), w2_d[fi * NCH + m], NF * 128)
        else:
            l = ph[1]
            mi = mix_ids.index(l)
            for ti in range(17):
                W.plan(("wm", l, ti), wm_d[mi * 17 + ti], 2048)
            if stop is None:
                for m in range(NCH):
                    W.plan(("wo", l, m), wo_d[mi * NCH + m], 1024)

    for tt in range(4):
        P.dma("sp", lambda e, tt=tt: e.dma_start(out=xT3[:, :, tt * 512:(tt + 1) * 512],
                                                 in_=x_in.rearrange("p (c t) -> p c t", c=NCH)[:, :, tt * 512:(tt + 1) * 512]),
              [], [xT_b[tt]], ld[tt])
    P.dma("sp", lambda e: e.dma_start(out=cst, in_=cst_d[:, 0:768]), [], [cst_b[0]], misc)
    ctmp, ctmp_b = A.alloc("ctmp", PH, [NCST - 768], F32)
    ctmp_sem = P.new_dsem("ctmp")
    P.dma("sp", lambda e: e.dma_start(out=ctmp, in_=cst_d[:, 768:NCST]), [], [ctmp_b[0]], ctmp_sem)
    P.dve(lambda e: e.memset(ones_bf, 1.0), [], [cbf_b[0]])
    P.dve(lambda e: e.tensor_copy(cbf[:, 128:1024], ctmp[:, 0:896]), [ctmp_b[0]], [cbf_b[0]])
    P.dve(lambda e: e.tensor_copy(gsel_bf, cst[:, C_GSEL:C_GSEL + 2]), [cst_b[0]], [cbf_b[0]])
    P.dve(lambda e: e.tensor_copy(selsw_bf, ctmp[:, C_SELSW - 768:C_SELSW - 768 + 512]), [ctmp_b[0]], [cbf_b[0]])
    A.free("ctmp")

    eps_col = cst[:, C_EPS:C_EPS + 1]

    def gain(l, k, c):
        i = (l * 6 + k) * NCH + c
        return cst[:, i:i + 1]

    def gain_half(l, k, c):
        i = 96 + (l * 6 + k) * NCH + c
        return cst[:, i:i + 1]

    cst_half_b = Buf("csthalf")
    P.dve(lambda e: e.tensor_scalar(cst[:, 96:192], cst[:, 0:96], 0.5, None, ALU.mult), [cst_b[0]], [cst_half_b])

    def rms_rstd(dst, src_ps, reads, writes, scale=1.0 / D):
        P.act(lambda e: e.activation(dst, src_ps, AF.Sqrt, bias=eps_col, scale=scale), reads + [cst_b[0]], writes)
        P.dve(lambda e: e.reciprocal(dst, dst), writes, writes)

    def pre_norm(l, kpre, ntt, t0, xn3, xn_b, sq3, sq_bl, rs, rs_b, pbank):
        for tt in range(ntt):
            g = (t0 + tt * 512) // 512
            ts = slice(t0 + tt * 512, t0 + (tt + 1) * 512)
            ls = slice(tt * 512, (tt + 1) * 512)
            P.dve(lambda e, ts=ts: e.tensor_tensor(sq3, xT3[:, :, ts], xT3[:, :, ts], ALU.mult), [xT_b[g]], sq_bl)
            pss, pss_b = psum[pbank], ps_b[pbank]
            for c in range(NCH):
                P.pe(lambda e, c=c, pss=pss: e.matmul(pss, ones_bf, sq3[:, c, :], start=(c == 0), stop=(c == NCH - 1)),
                     [ones_b[0]] + sq_bl, [pss_b])
            rms_rstd(rs, pss, [pss_b], [rs_b])
            for c in range(NCH):
                P.dve(lambda e, c=c, ts=ts, ls=ls: e.scalar_tensor_tensor(
                    xn3[:, c, ls], xT3[:, c, ts], gain(l, kpre, c), rs, ALU.mult, ALU.mult),
                    [xT_b[g], rs_b, cst_b[0]], [xn_b[c * ntt + tt]])

    def ffn_phase(l, k):
        lf = l * 2 + k
        kpre, kpost = (0, 1) if k == 0 else (4, 5)
        xn, xn_b = A.alloc("xn", PH, [NCH * 1024], BF16, nbuf=16)
        xn3 = xn.rearrange("p (c t) -> p c t", c=NCH)
        actT, act_b = A.alloc("actT", PH + 16384, [NF * 1024], BF16, nbuf=NF * 2)
        act3 = actT.rearrange("p (f t) -> p f t", f=NF)
        hT, h_b = A.alloc("hT", PH + 61440, [NCH * 1024], F32, nbuf=16)
        h3 = hT.rearrange("p (c t) -> p c t", c=NCH)
        sq, sq_b = A.alloc("sq", PH + 94208, [NCH * 512], BF16, nbuf=2)
        sq3 = sq.rearrange("p (c t) -> p c t", c=NCH)
        sg, sg_b = A.alloc("sg", PH + 102400, [2 * 512], F32, nbuf=2)
        rs, rs_b = A.alloc("rs", PH + 106496, [2 * 512], F32, nbuf=2)
        tmp, tmp_b = A.alloc("tmp", PH + 110592, [2 * 512], F32, nbuf=2)
        for ps_ in range(2):
            t0 = ps_ * 1024
            pre_norm(l, kpre, 2, t0, xn3, xn_b, sq3, [sq_b[0], sq_b[1]], rs[:, 0:512], rs_b[0], 6)
            it = 0
            for f in range(NF):
                wt, wb = W.get(("w1", lf, ps_, f))
                wt4 = wt.rearrange("p (g c n) -> p g c n", g=2, c=NCH)
                for tt in range(2):
                    ls = slice(tt * 512, (tt + 1) * 512)
                    pg, pg_b = psum[(it % 2) * 2], ps_b[(it % 2) * 2]
                    pu, pu_b = psum[(it % 2) * 2 + 1], ps_b[(it % 2) * 2 + 1]
                    for c in range(NCH):
                        P.pe(lambda e, c=c, pg=pg, wt4=wt4, ls=ls: e.matmul(pg, wt4[:, 0, c, :], xn3[:, c, ls],
                                                                             start=(c == 0), stop=(c == NCH - 1)),
                             [wb, xn_b[c * 2 + tt]], [pg_b])
                    for c in range(NCH):
                        P.pe(lambda e, c=c, pu=pu, wt4=wt4, ls=ls: e.matmul(pu, wt4[:, 1, c, :], xn3[:, c, ls],
                                                                             start=(c == 0), stop=(c == NCH - 1)),
                             [wb, xn_b[c * 2 + tt]], [pu_b])
                    sgl = sg[:, (it % 2) * 512:(it % 2 + 1) * 512]
                    P.act(lambda e, sgl=sgl, pg=pg: e.activation(sgl, pg, AF.Silu), [pg_b], [sg_b[it % 2]])
                    P.dve(lambda e, f=f, ls=ls, sgl=sgl, pu=pu: e.tensor_tensor(act3[:, f, ls], sgl, pu, ALU.mult),
                          [sg_b[it % 2], pu_b], [act_b[f * 2 + tt]])
                    it += 1
            it = 0
            for m in range(NCH):
                wt, wb = W.get(("w2", lf, ps_, m))
                wt3 = wt.rearrange("p (f n) -> p f n", f=NF)
                for tt in range(2):
                    ls = slice(tt * 512, (tt + 1) * 512)
                    ph_, ph_b = psum[4 + it % 2], ps_b[4 + it % 2]
                    for f in range(NF):
                        P.pe(lambda e, f=f, ph_=ph_, wt3=wt3, ls=ls: e.matmul(ph_, wt3[:, f, :], act3[:, f, ls],
                                                                               start=(f == 0), stop=(f == NF - 1)),
                             [wb, act_b[f * 2 + tt]], [ph_b])
                    P.act(lambda e, m=m, ls=ls, ph_=ph_: e.activation(h3[:, m, ls], ph_, AF.Copy), [ph_b], [h_b[m * 2 + tt]])
                    sql = sq[:, (it % 2) * 512:(it % 2 + 1) * 512]
                    P.act(lambda e, sql=sql, ph_=ph_: e.activation(sql, ph_, AF.Square), [ph_b], [sq_b[it % 2]])
                    pss, pss_b = psum[6 + tt], ps_b[6 + tt]
                    P.pe(lambda e, m=m, pss=pss, sql=sql: e.matmul(pss, ones_bf, sql, start=(m == 0), stop=(m == NCH - 1)),
                         [ones_b[0], sq_b[it % 2]], [pss_b])
                    it += 1
            for tt in range(2):
                g = ps_ * 2 + tt
                ts = slice(t0 + tt * 512, t0 + (tt + 1) * 512)
                ls = slice(tt * 512, (tt + 1) * 512)
                rsl = rs[:, tt * 512:(tt + 1) * 512]
                rms_rstd(rsl, psum[6 + tt], [ps_b[6 + tt]], [rs_b[tt]])
                for m in range(NCH):
                    tl = tmp[:, (m % 2) * 512:(m % 2 + 1) * 512]
                    P.dve(lambda e, m=m, ls=ls, tl=tl, rsl=rsl: e.tensor_tensor(tl, h3[:, m, ls], rsl, ALU.mult),
                          [h_b[m * 2 + tt], rs_b[tt]], [tmp_b[m % 2]])
                    P.dve(lambda e, m=m, ts=ts, tl=tl: e.scalar_tensor_tensor(
                        xT3[:, m, ts], tl, gain_half(l, kpost, m), xT3[:, m, ts], ALU.mult, ALU.add),
                        [tmp_b[m % 2], cst_half_b, xT_b[g]], [xT_b[g]])
        for n in ("xn", "actT", "hT", "sq", "sg", "rs", "tmp"):
            A.free(n)

    cnt = {"ev": 0, "ps": 0}

    def evac(dst, src, reads, writes, func=None):
        if func is not None:
            P.act(lambda e: e.activation(dst, src, func), reads, writes)
        else:
            cnt["ev"] += 1
            if cnt["ev"] % 2:
                P.act(lambda e: e.activation(dst, src, AF.Copy), reads, writes)
            else:
                P.dve(lambda e: e.tensor_copy(dst, src), reads, writes)

    def mix_phase(l):
        lam_init = 0.8 - 0.6 * math.exp(-0.3 * l)
        lamv = cst[:, C_LAM + 256 * l: C_LAM + 256 * (l + 1)]
        P.dve(lambda e: e.tensor_tensor(sml[:, 8:72], lamv[:, 0:64], lamv[:, 64:128], ALU.mult), [cst_b[0]], [sml_b[0]])
        P.dve(lambda e: e.reduce_sum(sml[:, 2:3], sml[:, 8:72], axis=mybir.AxisListType.X), [sml_b[0]], [sml_b[0]])
        P.dve(lambda e: e.tensor_tensor(sml[:, 8:72], lamv[:, 128:192], lamv[:, 192:256], ALU.mult), [cst_b[0], sml_b[0]], [sml_b[0]])
        P.dve(lambda e: e.reduce_sum(sml[:, 3:4], sml[:, 8:72], axis=mybir.AxisListType.X), [sml_b[0]], [sml_b[0]])
        P.act(lambda e: e.activation(sml[:, 2:4], sml[:, 2:4], AF.Exp), [sml_b[0]], [sml_b[0]])
        P.dve(lambda e: e.tensor_tensor(sml[:, 0:1], sml[:, 3:4], sml[:, 2:3], ALU.subtract), [sml_b[0]], [sml_b[0]])
        P.dve(lambda e: e.tensor_scalar(sml[:, 0:1], sml[:, 0:1], -lam_init, None, ALU.add), [sml_b[0]], [sml_b[0]])
        P.dve(lambda e: e.tensor_scalar(sml[:, 1:2], cst[:, C_SUBG + l:C_SUBG + l + 1], 1.0 - lam_init, None, ALU.mult),
              [cst_b[0], sml_b[0]], [sml_b[0]])
        neglam, gsub = sml[:, 0:1], sml[:, 1:2]

        xn, xn_b = A.alloc("xn2", PH, [NCH * T], BF16, nbuf=NCH * 4)
        xn3 = xn.rearrange("p (c t) -> p c t", c=NCH)
        rope, rope_b = A.alloc("rope", PH + 32768, [2 * T], F32)
        rope3 = rope.rearrange("p (k t) -> p k t", k=2)
        stg, stg_b = A.alloc("stg", PH + 49152, [4 * T], BF16, nbuf=4)
        sq, sq_b = A.alloc("sq", PH + 65536, [NCH * 512], BF16)
        sq3 = sq.rearrange("p (c t) -> p c t", c=NCH)
        rs, rs_b = A.alloc("rs", PH + 73728, [512], F32)
        t12, t12_b = A.alloc("t12", PH + 75776, [4 * 512], F32, nbuf=4)
        stgg, stgg_b = A.alloc("stgg", PH + 83968, [T], BF16)
        P.dma("sp", lambda e: e.dma_start(out=rope, in_=rope_d), [], [rope_b[0]], rope_sem)
        pre_norm(l, 2, 4, 0, xn3, xn_b, sq3, [sq_b[0]], rs, rs_b[0], 7)
        stg_sem = [P.dsem(("sg", i)) for i in range(4)]
        si = 0
        pc = 0
        for ti in range(17):
            wt, wb = W.get(("wm", l, ti))
            wt4 = wt.rearrange("p (g c n) -> p g c n", g=2, c=NCH)
            if ti < 8:
                kind, h = (0, ti) if ti < 4 else (1, ti - 4)
                slot = si % 4
                si += 1
                sl = stg[:, slot * T:(slot + 1) * T]
                for tt in range(4):
                    ts = slice(tt * 512, (tt + 1) * 512)
                    ba, bb = (tt % 2) * 2, (tt % 2) * 2 + 1
                    for g_, bk in ((0, ba), (1, bb)):
                        for c in range(NCH):
                            P.pe(lambda e, c=c, g_=g_, bk=bk, wt4=wt4, ts=ts: e.matmul(
                                psum[bk], wt4[:, g_, c, :], xn3[:, c, ts], start=(c == 0), stop=(c == NCH - 1)),
                                [wb, xn_b[c * 4 + tt]], [ps_b[bk]])
                    t1 = t12[:, ba * 512:(ba + 1) * 512]
                    t2 = t12[:, bb * 512:(bb + 1) * 512]
                    P.dve(lambda e, t1=t1, ba=ba, ts=ts: e.tensor_tensor(t1, psum[ba], rope3[:, 0, ts], ALU.mult),
                          [ps_b[ba], rope_b[0]], [t12_b[ba]])
                    P.dve(lambda e, t2=t2, bb=bb, ts=ts: e.tensor_tensor(t2, psum[bb], rope3[:, 1, ts], ALU.mult),
                          [ps_b[bb], rope_b[0]], [t12_b[bb]])
                    P.dve(lambda e, t1=t1, t2=t2, sl=sl, ts=ts: e.tensor_tensor(sl[:, ts], t1, t2, ALU.add),
                          [t12_b[ba], t12_b[bb]], [stg_b[slot]])
                P.dma("sp", lambda e, kind=kind, h=h, sl=sl: e.dma_start(out=gin_kh(kind, h), in_=sl),
                      [stg_b[slot]], [gin_b[kind][h]], stg_sem[slot])
            elif ti < 16:
                kind = 2 + (ti - 8) // 2
                for g_ in range(2):
                    h = 2 * ((ti - 8) % 2) + g_
                    slot = si % 4
                    si += 1
                    sl = stg[:, slot * T:(slot + 1) * T]
                    for tt in range(4):
                        ts = slice(tt * 512, (tt + 1) * 512)
                        bk = 4 + pc % 2
                        pc += 1
                        for c in range(NCH):
                            P.pe(lambda e, c=c, g_=g_, bk=bk, wt4=wt4, ts=ts: e.matmul(
                                psum[bk], wt4[:, g_, c, :], xn3[:, c, ts], start=(c == 0), stop=(c == NCH - 1)),
                                [wb, xn_b[c * 4 + tt]], [ps_b[bk]])
                        evac(sl[:, ts], psum[bk], [ps_b[bk]], [stg_b[slot]], AF.Sigmoid if kind == 5 else None)
                    P.dma("sp", lambda e, kind=kind, h=h, sl=sl: e.dma_start(out=gin_kh(kind, h), in_=sl),
                          [stg_b[slot]], [gin_b[kind][h]], stg_sem[slot])
            else:
                slot = si % 4
                si += 1
                sl = stg[:, slot * T:(slot + 1) * T]
                for tt in range(4):
                    ts = slice(tt * 512, (tt + 1) * 512)
                    for c in range(NCH):
                        P.pe(lambda e, c=c, wt4=wt4, ts=ts: e.matmul(
                            psum[6][0:40, :], wt4[:, 0, c, 0:40], xn3[:, c, ts], start=(c == 0), stop=(c == NCH - 1)),
                            [wb, xn_b[c * 4 + tt]], [ps_b[6]])
                    P.act(lambda e, ts=ts, sl=sl: e.activation(sl[0:8, ts], psum[6][0:8, :], AF.Copy), [ps_b[6]], [stg_b[slot]])
                    P.act(lambda e, ts=ts: e.activation(stgg[32:40, ts], psum[6][32:40, :], AF.Copy), [ps_b[6]], [stgg_b[0]])
                    P.dve(lambda e, ts=ts, sl=sl: e.tensor_tensor(sl[32:40, ts], psum[6][32:40, :], stgg[32:40, ts], ALU.subtract),
                          [ps_b[6], stgg_b[0]], [stg_b[slot]])
                P.dma("sp", lambda e, sl=sl: e.dma_start(out=gin_c[12], in_=sl), [stg_b[slot]], [ging_b], stg_sem[slot])
        for n in ("xn2", "rope", "stg", "sq", "rs", "t12", "stgg"):
            A.free(n)

        if stop == "proj":
            return
        for ci_ in range(NCHK):
            rd = [ging_b] if ci_ == 12 else [gin_b[ci_ // 2][2 * (ci_ % 2)], gin_b[ci_ // 2][2 * (ci_ % 2) + 1]]
            order = [2, 3, 0, 1, 4, 5, 6, 7, 8, 9, 10, 11, 12]
            cj = order[ci_]
            rd = [ging_b] if cj == 12 else [gin_b[cj // 2][2 * (cj % 2)], gin_b[cj // 2][2 * (cj % 2) + 1]]
            P.dma("pool", lambda e, cj=cj: e.collective_compute("AllGather", ALU.bypass, replica_groups=RG,
                                                               ins=[gin_c[cj].opt()], outs=[gout_c[cj].opt()]),
                  rd, [gout_b[cj]], P.dsem(("cc1", cj)), inc=1)
        if stop == "gather":
            return
        KT, KT_b = A.alloc("KT", PH, [SEQ], BF16, nbuf=16)
        QT, QT_b = A.alloc("QT", PH + 16384, [SEQ], BF16, nbuf=16)
        VT, VT_b = A.alloc("VT", PH + 32768, [SEQ], BF16, nbuf=16)
        cand, cand_b = A.alloc("cand", PH + 49152, [3 * 2048], BF16, nbuf=3)
        cand_sem = [P.dsem(("cd", i)) for i in range(3)]
        ci = [0]

        def load_cand(kind, r, tt):
            slot = ci[0] % 3
            ci[0] += 1
            cs = cand[:, slot * 2048:(slot + 1) * 2048].rearrange("p (h t) -> p h t", h=4)
            for half in range(2):
                P.dma("sp", lambda e, cs=cs, kind=kind, r=r, tt=tt, half=half: e.dma_start(
                    out=cs[:, 2 * half:2 * half + 2, :],
                    in_=gout_c[kind * 2 + half][r * 256:(r + 1) * 256, tt * 512:(tt + 1) * 512].rearrange("(h p) t -> p h t", h=2)),
                    [gout_b[kind * 2 + half]], [cand_b[slot]], cand_sem[slot])
            return cs, cand_b[slot]

        def select_fm(kind, dst, dst_b):
            for r in range(4):
                for tt in range(4):
                    cs, cb = load_cand(kind, r, tt)
                    bk = cnt["ps"] % 2
                    cnt["ps"] += 1
                    for h in range(4):
                        P.pe(lambda e, h=h, cs=cs, bk=bk: e.matmul(psum[bk], sel_bf[:, h * 128:(h + 1) * 128], cs[:, h, :],
                                                                  start=(h == 0), stop=(h == 3)),
                             [cbf_b[0], cb], [ps_b[bk]])
                    g = r * 4 + tt
                    evac(dst[:, g * 512:(g + 1) * 512], psum[bk], [ps_b[bk]], [dst_b[g]])

        def select_tok(kind, dst, dst_b):
            for r in range(4):
                for tt in range(4):
                    cs, cb = load_cand(kind, r, tt)
                    bk = cnt["ps"] % 2
                    cnt["ps"] += 1
                    for sub in range(4):
                        for h in range(4):
                            P.pe(lambda e, h=h, sub=sub, cs=cs, bk=bk: e.matmul(
                                psum[bk][:, sub * 128:(sub + 1) * 128], cs[:, h, sub * 128:(sub + 1) * 128],
                                sel_bf[:, h * 128:(h + 1) * 128], start=(h == 0), stop=(h == 3)),
                                [cbf_b[0], cb], [ps_b[bk]])
                    g = r * 4 + tt
                    evac(dst[:, g * 512:(g + 1) * 512], psum[bk], [ps_b[bk]], [dst_b[g]])

        select_fm(1, KT, KT_b)
        select_fm(0, QT, QT_b)
        select_tok(2, VT, VT_b)

        pbuf, pbuf_b = A.alloc("pbuf", PH + 61440, [6 * 512], BF16, nbuf=6)
        rr, rr_b = A.alloc("rr", PH + 67584, [2 * 512], F32, nbuf=2)
        oo, oo_b = A.alloc("oo", PH + 71680, [2 * 512], F32, nbuf=2)
        sqa, sqa_b = A.alloc("sqa", PH + 75776, [512], BF16)
        rsa, rsa_b = A.alloc("rsa", PH + 76800, [512], F32)
        yst, yst_b = A.alloc("yst", PH + 78848, [2 * 512], BF16, nbuf=2)
        yst_sem = [P.dsem(("ys", i)) for i in range(2)]
        steps = [(gq, kt) for gq in range(16) for kt in range(4 * gq + 4)]
        nst = len(steps)

        def att_qk(i):
            gq, kt = steps[i]
            q0 = gq * 512
            r = kt - 4 * gq
            off = 0 if r < 0 else 128 * r
            sb_ = 4 + (i % 2) * 2
            pslot = (i % 3) * 2
            for c in range(2):
                P.pe(lambda e, c=c: e.matmul(
                    psum[sb_ + c][:, off:512], KT[c * 64:(c + 1) * 64, kt * 128:(kt + 1) * 128],
                    QT[c * 64:(c + 1) * 64, q0 + off:q0 + 512], start=True, stop=True),
                    [KT_b[kt // 4], QT_b[gq]], [ps_b[sb_ + c]])
            for c in range(2):
                pb_ = pbuf[:, (pslot + c) * 512:(pslot + c + 1) * 512]
                P.act(lambda e, c=c, pb_=pb_: e.activation(
                    pb_[:, off:512], psum[sb_ + c][:, off:512], AF.Exp, scale=0.125),
                    [ps_b[sb_ + c]], [pbuf_b[pslot + c]])
                if r >= 0:
                    P.dve(lambda e, pb_=pb_: e.memset(pb_[64:128, off:off + 64], 0.0),
                          [pbuf_b[pslot + c]], [pbuf_b[pslot + c]])

        def att_pv(i):
            gq, kt = steps[i]
            q0 = gq * 512
            nkt = 4 * gq + 4
            r = kt - 4 * gq
            off = 0 if r < 0 else 128 * r
            pslot = (i % 3) * 2
            for c in range(2):
                pb_ = pbuf[:, (pslot + c) * 512:(pslot + c + 1) * 512]
                P.pe(lambda e, c=c, pb_=pb_: e.matmul(
                    psum[c][:, off:512], VT[:, kt * 128:(kt + 1) * 128], pb_[:, off:512],
                    start=(kt == 0), stop=(kt == nkt - 1)),
                    [VT_b[kt // 4], pbuf_b[pslot + c]], [ps_b[c]])
                P.pe(lambda e, c=c, pb_=pb_: e.matmul(
                    psum[2 + c][:, off:512], ones_bf, pb_[:, off:512],
                    start=(kt == 0), stop=(kt == nkt - 1)),
                    [cbf_b[0], pbuf_b[pslot + c]], [ps_b[2 + c]])
            if kt != nkt - 1:
                return
            for c in range(2):
                rl = rr[:, c * 512:(c + 1) * 512]
                ol = oo[:, c * 512:(c + 1) * 512]
                P.dve(lambda e, c=c, rl=rl: e.reciprocal(rl, psum[2 + c]), [ps_b[2 + c]], [rr_b[c]])
                P.dve(lambda e, c=c, rl=rl, ol=ol: e.tensor_tensor(ol, psum[c], rl, ALU.mult), [ps_b[c], rr_b[c]], [oo_b[c]])
            o1, o2 = oo[:, 0:512], oo[:, 512:1024]
            P.dve(lambda e: e.scalar_tensor_tensor(o1, o2, neglam, o1, ALU.mult, ALU.add), [oo_b[0], oo_b[1], sml_b[0]], [oo_b[0]])
            P.act(lambda e: e.activation(sqa, o1, AF.Square), [oo_b[0]], [sqa_b[0]])
            sb_ = 4 + (i % 2) * 2
            P.pe(lambda e: e.matmul(psum[sb_], ones_bf, sqa, start=True, stop=True), [cbf_b[0], sqa_b[0]], [ps_b[sb_]])
            rms_rstd(rsa, psum[sb_], [ps_b[sb_]], [rsa_b[0]], scale=1.0 / 128)
            ys = gq % 2
            yl = yst[:, ys * 512:(ys + 1) * 512]
            P.dve(lambda e: e.scalar_tensor_tensor(yl, o1, gsub, rsa, ALU.mult, ALU.mult),
                  [oo_b[0], rsa_b[0], sml_b[0]], [yst_b[ys]])
            P.dma("sp", lambda e: e.dma_start(out=gin2_c[q0 // T][0:128, q0 % T:q0 % T + 512], in_=yl),
                  [yst_b[ys]], [gin2_b[0][gq]], yst_sem[ys])

        for i in range(nst + 1):
            if i < nst:
                att_qk(i)
            if i >= 1:
                att_pv(i - 1)
        for n in ("KT", "QT", "VT", "pbuf", "rr", "oo", "sqa", "rsa", "yst"):
            A.free(n)
        A.free("cand")
        if stop == "att":
            return

        NPQ = 8208
        PQK, PQK_b = A.alloc("PQK", PH, [NPQ], BF16, nbuf=17)
        PKQ, PKQ_b = A.alloc("PKQ", PH + 16416, [NPQ], BF16, nbuf=17)
        MV, MV_b = A.alloc("MV", PH + 32832, [64 * 129], BF16, nbuf=17)
        MV3 = MV.rearrange("p (t d) -> p t d", t=64)
        MO, MO_b = A.alloc("MO", PH + 49344, [SEQ], BF16, nbuf=16)
        cand, cand_b = A.alloc("cand", PH + 65728, [3 * 2048], BF16, nbuf=3)
        GG, GG_b = A.alloc("GG", PH + 78016, [2 * 2048], BF16, nbuf=2)
        SO = PH + 86208
        gts, gts_b = A.alloc("gts", SO, [128], F32)
        gw, gw_b = A.alloc("gw", SO + 512, [6 * 64], F32)
        gwb, gwb_b = A.alloc("gwb", SO + 2048, [2 * 64], BF16)
        lfb, lfb_b = A.alloc("lfb", SO + 2304, [2 * 256], BF16, nbuf=2)
        dex, dex_b = A.alloc("dex", SO + 3328, [2 * 128], F32, nbuf=2)
        wTt, wT_b = A.alloc("wT", SO + 4352, [2 * 128], BF16, nbuf=2)
        ebt, eb_b = A.alloc("eb", SO + 4864, [2 * 128], F32, nbuf=2)
        qtt, qt_b = A.alloc("qt", SO + 5888, [2 * 128], BF16, nbuf=2)
        sct, sc_b = A.alloc("sc", SO + 6400, [2 * 8], F32, nbuf=2)
        ktt, kt_b = A.alloc("kt", SO + 6464, [2 * 64], BF16, nbuf=2)
        Cs, Cs_b = A.alloc("Cs", SO + 6720, [136], F32)
        Cbf, Cbf_b = A.alloc("Cbf", SO + 7264, [2 * 256], BF16, nbuf=2)
        rdn, rdn_b = A.alloc("rdn", SO + 8288, [2 * 128], F32, nbuf=2)
        acc, acc_b = A.alloc("acc", SO + 9312, [2 * 512], F32, nbuf=2)
        aqk, aqk_b = A.alloc("aqk", SO + 13408, [2 * 512], F32, nbuf=2)
        qkb, qkb_b = A.alloc("qkb", SO + 17504, [2 * 512], BF16, nbuf=2)
        hTm, hTm_b = A.alloc("hTm", SO + 19552, [512], F32)
        sqm, sqm_b = A.alloc("sqm", SO + 21600, [512], BF16)
        rsm, rsm_b = A.alloc("rsm", SO + 22624, [512], F32)
        ysm, ysm_b = A.alloc("ysm", SO + 24672, [2 * 512], BF16, nbuf=2)
        cand_sem = [P.dsem(("cd", i)) for i in range(3)]
        ci[0] = 0

        def load_cand2(kind, r, tt):
            slot = ci[0] % 3
            ci[0] += 1
            cs = cand[:, slot * 2048:(slot + 1) * 2048].rearrange("p (h t) -> p h t", h=4)
            for half in range(2):
                P.dma("sp", lambda e, cs=cs, kind=kind, r=r, tt=tt, half=half: e.dma_start(
                    out=cs[:, 2 * half:2 * half + 2, :],
                    in_=gout_c[kind * 2 + half][r * 256:(r + 1) * 256, tt * 512:(tt + 1) * 512].rearrange("(h p) t -> p h t", h=2)),
                    [gout_b[kind * 2 + half]], [cand_b[slot]], cand_sem[slot])
            return cs, cand_b[slot]

        P.dve(lambda e: e.memset(PQK[:, 0:3], 0.0), [], [PQK_b[16]])
        P.dve(lambda e: e.memset(PKQ[:, 0:3], 0.0), [], [PKQ_b[16]])
        swap_bf = None
        for r in range(4):
            for tt in range(4):
                g = r * 4 + tt
                cs, cb = load_cand2(3, r, tt)
                for dst, dstb, selm in ((PQK, PQK_b, 0), (PKQ, PKQ_b, 1)):
                    bk = cnt["ps"] % 2
                    cnt["ps"] += 1
                    for h in range(4):
                        lhs = sel_bf[:, h * 128:(h + 1) * 128] if selm == 0 else selsw_bf[:, h * 128:(h + 1) * 128]
                        P.pe(lambda e, h=h, cs=cs, bk=bk, lhs=lhs: e.matmul(psum[bk], lhs, cs[:, h, :], start=(h == 0), stop=(h == 3)),
                             [cbf_b[0], cb], [ps_b[bk]])
                    evac(dst[:, 3 + g * 512: 3 + (g + 1) * 512], psum[bk], [ps_b[bk]], [dstb[g]])
        for r in range(4):
            for tt in range(4):
                g = r * 4 + tt
                cs, cb = load_cand2(5, r, tt)
                bk = cnt["ps"] % 2
                cnt["ps"] += 1
                for h in range(4):
                    P.pe(lambda e, h=h, cs=cs, bk=bk: e.matmul(psum[bk], sel_bf[:, h * 128:(h + 1) * 128], cs[:, h, :],
                                                              start=(h == 0), stop=(h == 3)), [cbf_b[0], cb], [ps_b[bk]])
                evac(MO[:, g * 512:(g + 1) * 512], psum[bk], [ps_b[bk]], [MO_b[g]])
        P.dve(lambda e: e.memset(MV3[:, :, 128:129], 1.0), [], [MV_b[16]])
        for r in range(4):
            for tt in range(4):
                g = r * 4 + tt
                cs, cb = load_cand2(4, r, tt)
                bk = cnt["ps"] % 2
                cnt["ps"] += 1
                for sub in range(4):
                    for h in range(4):
                        P.pe(lambda e, h=h, sub=sub, cs=cs, bk=bk: e.matmul(
                            psum[bk][:, sub * 128:(sub + 1) * 128], cs[:, h, sub * 128:(sub + 1) * 128],
                            sel_bf[:, h * 128:(h + 1) * 128], start=(h == 0), stop=(h == 3)), [cbf_b[0], cb], [ps_b[bk]])
                evac(MV3[:, g * 4:(g + 1) * 4, 0:128], psum[bk].rearrange("p (t d) -> p t d", t=4), [ps_b[bk]], [MV_b[g]])
        gg_sem = [P.dsem(("gg", i)) for i in range(2)]
        P.dve(lambda e: e.memset(GG, 0.0), [], [GG_b[0], GG_b[1]])
        for r in range(4):
            sl_ = r % 2
            ggl = GG[:, sl_ * 2048:(sl_ + 1) * 2048]
            P.dma("sp", lambda e, ggl=ggl, r=r: e.dma_start(out=ggl[0:40, :], in_=gout_c[12][r * 128:r * 128 + 40, :]),
                  [gout_b[12]], [GG_b[sl_]], gg_sem[sl_])
            for t in range(16):
                T_ = r * 16 + t
                P.pe(lambda e, T_=T_, t=t, ggl=ggl: e.matmul(psum[3][:, 2 * T_:2 * T_ + 2], ggl[0:8, t * 128:(t + 1) * 128],
                                                            gsel_bf[0:8, :], start=True, stop=False), [GG_b[sl_], cbf_b[0]], [ps_b[3]])
                P.pe(lambda e, T_=T_, t=t, ggl=ggl: e.matmul(psum[3][:, 2 * T_:2 * T_ + 2], ggl[32:40, t * 128:(t + 1) * 128],
                                                            gsel_bf[32:40, :], start=False, stop=True), [GG_b[sl_], cbf_b[0]], [ps_b[3]])
        g3v = gts.rearrange("p (t k) -> p t k", k=2)
        p3v = psum[3][:, 0:128].rearrange("p (t k) -> p t k", k=2)
        for k_ in range(2):
            P.dve(lambda e, k_=k_: e.tensor_scalar(g3v[:, :, k_:k_ + 1], p3v[:, :, k_:k_ + 1],
                                                    cst[:, C_GB + 2 * l + k_:C_GB + 2 * l + k_ + 1], None, ALU.add),
                  [ps_b[3], cst_b[0]], [gts_b[0]])
        e1, lf, lfhf, lfl, ccv = (gw[:, i * 64:(i + 1) * 64] for i in range(5))
        lfh_bf, lfl_bf = gwb[:, 0:64], gwb[:, 64:128]
        gfv = g3v[:, :, 1:2].rearrange("p t k -> p (t k)")
        giv = g3v[:, :, 0:1].rearrange("p t k -> p (t k)")
        P.act(lambda e: e.activation(e1, gfv, AF.Exp, scale=-1.0), [gts_b[0]], [gw_b[0]])
        P.act(lambda e: e.activation(lf, e1, AF.Ln, bias=cst[:, C_ONE:C_ONE + 1], scale=1.0), [gw_b[0], cst_b[0]], [gw_b[0]])
        P.dve(lambda e: e.tensor_scalar(lf, lf, -1.0, None, ALU.mult), [gw_b[0]], [gw_b[0]])
        P.dve(lambda e: e.tensor_copy(lfh_bf, lf), [gw_b[0]], [gwb_b[0]])
        P.dve(lambda e: e.tensor_copy(lfhf, lfh_bf), [gwb_b[0]], [gw_b[0]])
        P.dve(lambda e: e.tensor_tensor(lfl, lf, lfhf, ALU.subtract), [gw_b[0]], [gw_b[0]])
        P.dve(lambda e: e.tensor_copy(lfl_bf, lfl), [gw_b[0]], [gwb_b[0]])
        P.pe(lambda e: e.matmul(psum[2][:, 0:64], tri_bf, lfh_bf, start=True, stop=False), [cbf_b[0], gwb_b[0]], [ps_b[2]])
        P.pe(lambda e: e.matmul(psum[2][:, 0:64], tri_bf, lfl_bf, start=False, stop=True), [cbf_b[0], gwb_b[0]], [ps_b[2]])
        P.dve(lambda e: e.tensor_tensor(ccv, giv, psum[2][:, 0:64], ALU.subtract), [gts_b[0], ps_b[2]], [gw_b[0]])

        P.dve(lambda e: e.memset(Cs, 0.0), [], [Cs_b[0]])
        P.dve(lambda e: e.memset(Cbf, 0.0), [], [Cbf_b[0], Cbf_b[1]])
        ysm_sem = [P.dsem(("ym", i)) for i in range(2)]
        mlg = cst[:, C_MLG + l:C_MLG + l + 1]
        q32 = aqk[0:64, 0:512]
        qb_, kb_ = qkb[0:64, 0:512], qkb[0:64, 512:1024]

        def ml_conv(G):
            g0 = G * 512
            for v, (src, srcb) in enumerate(((PQK, PQK_b), (PKQ, PKQ_b))):
                c0 = C_CONV + 5 * (2 * l + v)
                al = acc[:, v * 512:(v + 1) * 512]
                rdl = [srcb[G], srcb[G - 1] if G > 0 else srcb[16]]
                P.dve(lambda e: e.tensor_scalar(al, src[:, g0 + 3:g0 + 3 + 512], cst[:, c0 + 3:c0 + 4],
                                                cst[:, c0 + 4:c0 + 5], ALU.mult, ALU.add),
                      rdl + [cst_b[0]], [acc_b[v]])
                for k_ in range(3):
                    P.dve(lambda e, k_=k_: e.scalar_tensor_tensor(
                        al, src[:, g0 + k_:g0 + k_ + 512], cst[:, c0 + k_:c0 + k_ + 1], al, ALU.mult, ALU.add),
                        rdl + [cst_b[0], acc_b[v]], [acc_b[v]])
                aql = aqk[:, v * 512:(v + 1) * 512]
                P.act(lambda e: e.activation(aql, al, AF.Silu), [acc_b[v]], [aqk_b[v]])
                qkl = qkb[:, v * 512:(v + 1) * 512]
                P.dve(lambda e: e.tensor_copy(qkl[0:64, :], aql[0:64, :]), [aqk_b[v]], [qkb_b[v]])

        def ml_A(T_):
            G, tt = divmod(T_, 4)
            if tt == 0:
                ml_conv(G)
            cs_ = slice(tt * 128, (tt + 1) * 128)
            s2 = T_ % 2
            b0, b1, b2 = (0, 1, 2) if s2 == 0 else (4, 5, 6)
            lfbl = lfb[:, s2 * 256:(s2 + 1) * 256]
            P.act(lambda e: e.activation(lfbl[:, 0:128], ones_bf, AF.Copy, scale=lfhf[:, T_:T_ + 1]),
                  [cbf_b[0], gw_b[0]], [lfb_b[s2]])
            P.act(lambda e: e.activation(lfbl[:, 128:256], ones_bf, AF.Copy, scale=lfl[:, T_:T_ + 1]),
                  [cbf_b[0], gw_b[0]], [lfb_b[s2]])
            pb0, pb1 = psum[b0][:, 0:128], psum[b0][:, 128:256]
            P.pe(lambda e: e.matmul(pb0, lfbl[:, 0:128], tri_bf, start=True, stop=False), [lfb_b[s2], cbf_b[0]], [ps_b[b0]])
            P.pe(lambda e: e.matmul(pb0, lfbl[:, 128:256], tri_bf, start=False, stop=True), [lfb_b[s2], cbf_b[0]], [ps_b[b0]])
            P.pe(lambda e: e.matmul(pb1, lfbl[:, 0:128], tri_bf, start=True, stop=False), [lfb_b[s2], cbf_b[0]], [ps_b[b0]])
            P.pe(lambda e: e.matmul(pb1, lfbl[:, 128:256], tri_bf, start=False, stop=False), [lfb_b[s2], cbf_b[0]], [ps_b[b0]])
            P.pe(lambda e: e.matmul(pb1, I_bf, neg_bf, start=False, stop=True), [cbf_b[0]], [ps_b[b0]])
            dxl = dex[:, s2 * 128:(s2 + 1) * 128]
            ebl = ebt[0:64, s2 * 128:(s2 + 1) * 128]
            scl = sct[:, s2 * 8:(s2 + 1) * 8]
            P.act(lambda e: e.activation(dxl, pb1, AF.Exp, bias=ccv[:, T_:T_ + 1], scale=1.0),
                  [ps_b[b0], gw_b[0]], [dex_b[s2]])
            P.act(lambda e: e.activation(ebl, pb0[0:64, :], AF.Exp), [ps_b[b0]], [eb_b[s2]])
            P.act(lambda e: e.activation(scl[:, 0:1], pb0[:, 127:128], AF.Exp, bias=ccv[:, T_:T_ + 1], scale=1.0),
                  [ps_b[b0], gw_b[0]], [sc_b[s2]])
            P.act(lambda e: e.activation(scl[0:64, 1:2], pb0[0:64, 127:128], AF.Exp), [ps_b[b0]], [sc_b[s2]])
            P.pe(lambda e: e.matmul(psum[b1][:, 0:128], kb_[:, cs_], qb_[:, cs_], start=True, stop=True),
                 [qkb_b[0], qkb_b[1]], [ps_b[b1]])
            P.pe(lambda e: e.matmul(psum[b1][:, 128:192], kb_[:, cs_], I_bf[0:64, 0:64], start=True, stop=True),
                 [qkb_b[1], cbf_b[0]], [ps_b[b1]])
            wl = wTt[:, s2 * 128:(s2 + 1) * 128]
            ql = qtt[0:64, s2 * 128:(s2 + 1) * 128]
            kl = ktt[:, s2 * 64:(s2 + 1) * 64]
            P.dve(lambda e: e.scalar_tensor_tensor(wl, psum[b1][:, 0:128], 0.125, dxl, ALU.mult, ALU.mult),
                  [ps_b[b1], dex_b[s2]], [wT_b[s2]])
            P.dve(lambda e: e.scalar_tensor_tensor(ql, q32[:, cs_], 0.125, ebl, ALU.mult, ALU.mult),
                  [aqk_b[0], eb_b[s2]], [qt_b[s2]])
            P.dve(lambda e: e.tensor_scalar(kl, psum[b1][:, 128:192], scl[:, 0:1], None, ALU.mult),
                  [ps_b[b1], sc_b[s2]], [kt_b[s2]])

        def ml_B(T_):
            G, tt = divmod(T_, 4)
            g0 = G * 512
            gs = G % 2
            cs_ = slice(tt * 128, (tt + 1) * 128)
            s2 = T_ % 2
            b0, b1, b2 = (0, 1, 2) if s2 == 0 else (4, 5, 6)
            scl = sct[:, s2 * 8:(s2 + 1) * 8]
            wl = wTt[:, s2 * 128:(s2 + 1) * 128]
            ql = qtt[0:64, s2 * 128:(s2 + 1) * 128]
            kl = ktt[:, s2 * 64:(s2 + 1) * 64]
            cprev = Cbf[0:64, s2 * 256:(s2 + 1) * 256]
            cnext = Cbf[0:64, (1 - s2) * 256:(2 - s2) * 256]
            P.pe(lambda e: e.matmul(psum[b2][:, 0:128], MV3[:, T_, 0:128], wl, start=True, stop=False),
                 [MV_b[T_ // 4], MV_b[16], wT_b[s2]], [ps_b[b2]])
            P.pe(lambda e: e.matmul(psum[b2][:, 0:128], cprev[:, 0:128], ql, start=False, stop=True),
                 [Cbf_b[s2], qt_b[s2]], [ps_b[b2]])
            P.pe(lambda e: e.matmul(psum[b2][:, 128:256], ones_bf, wl, start=True, stop=False), [cbf_b[0], wT_b[s2]], [ps_b[b2]])
            P.pe(lambda e: e.matmul(psum[b2][:, 128:256], cprev[:, 128:256], ql, start=False, stop=True),
                 [Cbf_b[s2], qt_b[s2]], [ps_b[b2]])
            P.pe(lambda e: e.matmul(psum[b1][0:64, 256:385], kl, MV3[:, T_, :], start=True, stop=True),
                 [kt_b[s2], MV_b[T_ // 4], MV_b[16]], [ps_b[b1]])
            P.dve(lambda e: e.scalar_tensor_tensor(Cs[0:64, 0:129], Cs[0:64, 0:129], scl[0:64, 1:2], psum[b1][0:64, 256:385],
                                                   ALU.mult, ALU.add), [Cs_b[0], sc_b[s2], ps_b[b1]], [Cs_b[0]])
            P.act(lambda e: e.activation(cnext[:, 0:128], Cs[0:64, 0:128], AF.Copy), [Cs_b[0]], [Cbf_b[1 - s2]])
            P.dve(lambda e: e.tensor_scalar(cnext[:, 128:256], ones_bf[0:64, :], Cs[0:64, 128:129], None, ALU.mult),
                  [Cs_b[0], cbf_b[0]], [Cbf_b[1 - s2]])
            rl = rdn[:, s2 * 128:(s2 + 1) * 128]
            P.act(lambda e: e.activation(rl, psum[b2][:, 128:256], AF.Abs), [ps_b[b2]], [rdn_b[s2]])
            P.dve(lambda e: e.tensor_scalar(rl, rl, 1.0, None, ALU.max), [rdn_b[s2]], [rdn_b[s2]])
            P.dve(lambda e: e.reciprocal(rl, rl), [rdn_b[s2]], [rdn_b[s2]])
            P.dve(lambda e: e.tensor_tensor(hTm[:, cs_], psum[b2][:, 0:128], rl, ALU.mult),
                  [ps_b[b2], rdn_b[s2]], [hTm_b[0]])
            if tt != 3:
                return
            P.act(lambda e: e.activation(sqm, hTm, AF.Square), [hTm_b[0]], [sqm_b[0]])
            bs = 3 if gs == 0 else 7
            P.pe(lambda e: e.matmul(psum[bs], ones_bf, sqm, start=True, stop=True), [cbf_b[0], sqm_b[0]], [ps_b[bs]])
            rms_rstd(rsm, psum[bs], [ps_b[bs]], [rsm_b[0]], scale=1.0 / 128)
            yl = ysm[:, gs * 512:(gs + 1) * 512]
            P.dve(lambda e: e.scalar_tensor_tensor(hTm, hTm, mlg, rsm, ALU.mult, ALU.mult), [hTm_b[0], rsm_b[0], cst_b[0]], [hTm_b[0]])
            P.dve(lambda e: e.tensor_tensor(yl, hTm, MO[:, g0:g0 + 512], ALU.mult), [hTm_b[0], MO_b[G]], [ysm_b[gs]])
            P.dma("sp", lambda e: e.dma_start(out=gin2_c[g0 // T][128:256, g0 % T:g0 % T + 512], in_=yl),
                  [ysm_b[gs]], [gin2_b[1][G]], ysm_sem[gs])

        for T_ in range(65):
            if T_ < 64:
                ml_A(T_)
            if T_ >= 1:
                ml_B(T_ - 1)
        for n in ("PQK", "PKQ", "MV", "MO", "cand", "GG", "gts", "gw", "gwb", "lfb", "dex", "wT", "eb", "qt", "sc", "kt",
                  "Cs", "Cbf", "rdn", "acc", "aqk", "qkb", "hTm", "sqm", "rsm", "ysm"):
            A.free(n)
        if stop == "mls":
            return

        for b_ in range(4):
            rd = [gin2_b[k_][b_ * 4 + i] for k_ in range(2) for i in range(4)]
            P.dma("pool", lambda e, b_=b_: e.collective_compute("AllGather", ALU.bypass, replica_groups=RG,
                                                               ins=[gin2_c[b_].opt()], outs=[gout2_c[b_].opt()]),
                  rd, [gout2_b[b_]], P.dsem(("cc2", b_)), inc=1)

        yT, yT_b = A.alloc("yT", PH, [NCH * T], BF16, nbuf=NCH * 4)
        yT3 = yT.rearrange("p (c t) -> p c t", c=NCH)
        hO, hO_b = A.alloc("hO", PH + 32768, [NCH * T], F32, nbuf=NCH * 4)
        hO3 = hO.rearrange("p (c t) -> p c t", c=NCH)
        cand, cand_b = A.alloc("cand", PH + 98304, [3 * 2048], BF16, nbuf=3)
        sqo, sqo_b = A.alloc("sqo", PH + 110592, [2 * 512], BF16, nbuf=2)
        rso, rso_b = A.alloc("rso", PH + 112640, [512], F32)
        cand_sem = [P.dsem(("cd", i)) for i in range(3)]
        ci[0] = 0
        for c8 in range(NCH):
            kind_, r = divmod(c8, 4)
            for tt in range(4):
                slot = ci[0] % 3
                ci[0] += 1
                cs = cand[:, slot * 2048:(slot + 1) * 2048].rearrange("p (h t) -> p h t", h=4)
                for b_ in range(4):
                    P.dma("sp", lambda e, cs=cs, b_=b_, r=r, kind_=kind_, tt=tt: e.dma_start(
                        out=cs[:, b_, :], in_=gout2_c[b_][r * 256 + kind_ * 128:r * 256 + kind_ * 128 + 128, tt * 512:(tt + 1) * 512]),
                        [gout2_b[b_]], [cand_b[slot]], cand_sem[slot])
                bk = cnt["ps"] % 2
                cnt["ps"] += 1
                for b_ in range(4):
                    P.pe(lambda e, b_=b_, cs=cs, bk=bk: e.matmul(psum[bk], sel_bf[:, b_ * 128:(b_ + 1) * 128], cs[:, b_, :],
                                                                start=(b_ == 0), stop=(b_ == 3)), [cbf_b[0], cand_b[slot]], [ps_b[bk]])
                evac(yT3[:, c8, tt * 512:(tt + 1) * 512], psum[bk], [ps_b[bk]], [yT_b[c8 * 4 + tt]])
        it = 0
        for m in range(NCH):
            wt, wb = W.get(("wo", l, m))
            wt3 = wt.rearrange("p (c n) -> p c n", c=NCH)
            for tt in range(4):
                ts = slice(tt * 512, (tt + 1) * 512)
                bk = 2 + it % 2
                for c in range(NCH):
                    P.pe(lambda e, c=c, bk=bk, wt3=wt3, ts=ts: e.matmul(psum[bk], wt3[:, c, :], yT3[:, c, ts],
                                                                         start=(c == 0), stop=(c == NCH - 1)),
                         [wb, yT_b[c * 4 + tt]], [ps_b[bk]])
                P.act(lambda e, m=m, ts=ts, bk=bk: e.activation(hO3[:, m, ts], psum[bk], AF.Copy), [ps_b[bk]], [hO_b[m * 4 + tt]])
                sql = sqo[:, (it % 2) * 512:(it % 2 + 1) * 512]
                P.act(lambda e, sql=sql, bk=bk: e.activation(sql, psum[bk], AF.Square), [ps_b[bk]], [sqo_b[it % 2]])
                P.pe(lambda e, m=m, tt=tt, sql=sql: e.matmul(psum[4 + tt], ones_bf, sql, start=(m == 0), stop=(m == NCH - 1)),
                     [cbf_b[0], sqo_b[it % 2]], [ps_b[4 + tt]])
                it += 1
        for tt in range(4):
            ts = slice(tt * 512, (tt + 1) * 512)
            rms_rstd(rso, psum[4 + tt], [ps_b[4 + tt]], [rso_b[0]])
            for m in range(NCH):
                P.dve(lambda e, m=m, ts=ts: e.tensor_tensor(hO3[:, m, ts], hO3[:, m, ts], rso, ALU.mult),
                      [hO_b[m * 4 + tt], rso_b[0]], [hO_b[m * 4 + tt]])
                P.dve(lambda e, m=m, ts=ts: e.scalar_tensor_tensor(xT3[:, m, ts], hO3[:, m, ts], gain(l, 3, m), xT3[:, m, ts],
                                                                   ALU.mult, ALU.add),
                      [hO_b[m * 4 + tt], cst_b[0], xT_b[tt]], [xT_b[tt]])
        for n in ("yT", "hO", "cand", "sqo", "rso"):
            A.free(n)

    for ph in phases:
        if ph[0] == "ffn":
            ffn_phase(ph[1], ph[2])
        else:
            mix_phase(ph[1])

    st = P.new_dsem("st")
    evs = []
    if stop == "att":
        for b_ in range(4):
            evs.append(P.dma("sp", lambda e, b_=b_: e.dma_start(out=dbg_d[0:128, b_ * T:(b_ + 1) * T], in_=gin2_c[b_][0:128, :]),
                             [b for b in gin2_b[0]], [], st))
    if stop == "mls":
        for b_ in range(4):
            evs.append(P.dma("sp", lambda e, b_=b_: e.dma_start(out=dbg_d[:, b_ * T:(b_ + 1) * T], in_=gin2_c[b_]),
                             [b for k_ in range(2) for b in gin2_b[k_]], [], st))
    if stop == "proj":
        evs.append(P.dma("sp", lambda e: e.dma_start(out=dbg_d[0:128, 0:T], in_=gin_kh(1, 1)), [gin_b[1][1]], [], st))
        evs.append(P.dma("sp", lambda e: e.dma_start(out=dbg_d[128:256, 0:T], in_=gin_kh(0, 2)), [gin_b[0][2]], [], st))
    if stop == "gather":
        for r in range(4):
            evs.append(P.dma("sp", lambda e, r=r: e.dma_start(out=dbg_d[0:128, r * 512:(r + 1) * 512], in_=gout_rkh(r, 1, 1)[:, 0:512]),
                             gout_b, [], st))
            evs.append(P.dma("sp", lambda e, r=r: e.dma_start(out=dbg_d[128:256, r * 512:(r + 1) * 512], in_=gout_rkh(r, 0, 2)[:, 0:512]),
                             gout_b, [], st))
    if stop is not None:
        evs.append(P.dma("sp", lambda e: e.dma_start(out=out_d, in_=xT[:, 0:16]), [xT_b[0]], [], st))
    for tt in range(4 if stop is None else 0):
        evs.append(P.dma("sp", lambda e, tt=tt: e.dma_start(
            out=out_d.rearrange("p (c t) -> p c t", c=NCH)[:, :, tt * 512:(tt + 1) * 512],
            in_=xT3[:, :, tt * 512:(tt + 1) * 512]), [xT_b[tt]], [], st))
    P.fence("sp", evs)
    P.emit()
    return nc, P


def _mix_tile_cols():
    tiles = []
    rng = np.arange(128)
    perm = (rng // 64) * 64 + ((rng % 64) + 32) % 64
    for base in (0, 512):
        for h in range(4):
            tiles.append((base + h * 128 + rng, base + h * 128 + perm))
    for h0 in (0, 2):
        tiles.append((1024 + h0 * 128 + rng, 1024 + (h0 + 1) * 128 + rng))
    r64 = np.arange(64)
    for h0 in (0, 2):
        tiles.append(tuple(np.concatenate([1536 + h * 64 + r64, 1792 + h * 64 + r64]) for h in (h0, h0 + 1)))
    for base in (2048, 2560):
        for h0 in (0, 2):
            tiles.append((base + h0 * 128 + rng, base + (h0 + 1) * 128 + rng))
    gcols = np.full(128, -1)
    gcols[:8] = 3072 + np.arange(8)
    gcols[32:40] = 3072 + np.arange(8)
    tiles.append((gcols, np.full(128, -1)))
    return tiles


def _prep_shared(inp, ffn_ids, mix_ids):
    f = np.float32
    out = {}
    w_in = np.asarray(inp["ffn_w_in"], f).reshape(DEPTH * 2, D, 2 * DFF)
    w_out = np.asarray(inp["ffn_w_out"], f).reshape(DEPTH * 2, DFF, D)
    if ffn_ids:
        wi = w_in[ffn_ids]
        n = len(ffn_ids)
        w1 = wi.reshape(n, NCH, 128, 2, NF, 128).transpose(0, 4, 2, 3, 1, 5)
        out["w1"] = np.ascontiguousarray(w1).reshape(n * NF, 128, 2 * NCH * 128)
        wo_ = w_out[ffn_ids]
        w2 = wo_.reshape(n, NF, 128, NCH, 128).transpose(0, 3, 2, 1, 4)
        out["w2"] = np.ascontiguousarray(w2).reshape(n * NCH, 128, NF * 128)
    else:
        out["w1"] = np.zeros((NF, 128, 2 * NCH * 128), f)
        out["w2"] = np.zeros((NCH, 128, NF * 128), f)
    if mix_ids:
        tiles = _mix_tile_cols()
        wm = np.zeros((len(mix_ids), 17, 128, 2, NCH, 128), f)
        wo = np.zeros((len(mix_ids), NCH, 128, NCH, 128), f)
        for i, l in enumerate(mix_ids):
            Wl = np.asarray(inp["mix_w_in"][l], f)
            Wp = np.concatenate([Wl, np.zeros((D, 1), f)], axis=1)
            for ti, groups in enumerate(tiles):
                for g, cols in enumerate(groups):
                    blk = Wp[:, cols]
                    wm[i, ti, :, g] = blk.reshape(NCH, 128, 128).transpose(1, 0, 2)
            Wo = np.asarray(inp["mix_w_out"][l], f)
            wo[i] = Wo.reshape(NCH, 128, NCH, 128).transpose(2, 1, 0, 3)
        out["wm"] = wm.reshape(len(mix_ids) * 17, 128, 2048)
        out["wo"] = wo.reshape(len(mix_ids) * NCH, 128, 1024)
    else:
        out["wm"] = np.zeros((17, 128, 2048), f)
        out["wo"] = np.zeros((NCH, 128, 1024), f)
    return out


def _prep_core(inp, core):
    f = np.float32
    b, j = divmod(core, 4)
    cst = np.zeros((128, NCST), f)
    g = np.asarray(inp["norm_gains"], f)
    cst[:, 0:96] = g.reshape(DEPTH * 6, NCH, 128).transpose(2, 0, 1).reshape(128, 96)
    cst[:, C_EPS] = EPS
    cst[:, C_ONE] = 1.0
    cw = np.asarray(inp["ml_conv_w"], f)
    cb = np.asarray(inp["ml_conv_b"], f)
    gb = np.asarray(inp["ml_gate_b"], f)
    for l in range(DEPTH):
        cst[:, C_SUBG + l] = np.asarray(inp["da_subln_g"][l], f)
        cst[:, C_MLG + l] = np.asarray(inp["ml_norm_g"][l], f)
        cst[:, C_GB + 2 * l] = gb[l, j]
        cst[:, C_GB + 2 * l + 1] = gb[l, 4 + j]
        qi = j * 64 + np.arange(64)
        ki = 256 + j * 64 + np.arange(64)
        for v, idx in enumerate((np.concatenate([qi, ki]), np.concatenate([ki, qi]))):
            c0 = C_CONV + 5 * (2 * l + v)
            cst[:, c0:c0 + 4] = cw[l][:, idx].T
            cst[:, c0 + 4] = cb[l][idx]
        cst[:, C_LAM + 256 * l: C_LAM + 256 * (l + 1)] = np.asarray(inp["da_lambda"][l], f).reshape(1, 256)
    cst[j, C_GSEL] = 1.0
    cst[4 + j, C_GSEL + 1] = 1.0
    cst[32 + j, C_GSEL] = 1.0
    cst[32 + 4 + j, C_GSEL + 1] = 1.0
    sw = np.zeros((128, 128), f)
    sw[(np.arange(128) + 64) % 128, np.arange(128)] = 1.0
    cst[:, C_SELSW + j * 128: C_SELSW + (j + 1) * 128] = sw
    cst[:, C_I:C_I + 128] = np.eye(128, dtype=f)
    ii = np.arange(128)
    cst[:, C_TRI:C_TRI + 128] = (ii[:, None] <= ii[None, :]).astype(f)
    cst[:, C_NEG:C_NEG + 128] = np.where(ii[:, None] > ii[None, :], -30000.0, 0.0).astype(f)
    cst[:, C_SEL + j * 128: C_SEL + (j + 1) * 128] = np.eye(128, dtype=f)
    pos = (j * T + np.arange(T)).astype(np.float64)
    inv = 10000.0 ** (-np.arange(0, 64, 2, dtype=np.float64) / 64.0)
    d = np.arange(128) % 64
    ang = pos[None, :] * inv[d % 32][:, None]
    sgn = np.where(d < 32, -1.0, 1.0)[:, None]
    rope = np.concatenate([np.cos(ang), sgn * np.sin(ang)], axis=1).astype(f)
    return {"cst": cst, "rope": rope}


def _shard_x(x):
    xs = []
    for c in range(8):
        b, q = divmod(c, 4)
        xc = np.asarray(x[b, q * T:(q + 1) * T, :], np.float32)
        xs.append(np.ascontiguousarray(xc.reshape(T, NCH, 128).transpose(2, 1, 0)).reshape(128, NCH * T))
    return xs


def _unshard(outs):
    y = np.zeros((2, SEQ, D), np.float32)
    for c in range(8):
        b, q = divmod(c, 4)
        o = np.asarray(outs[c], np.float32).reshape(128, NCH, T).transpose(2, 1, 0).reshape(T, D)
        y[b, q * T:(q + 1) * T, :] = o
    return y


ALL_PHASES = [("ffn", 0, 0), ("mix", 0), ("ffn", 0, 1), ("ffn", 1, 0), ("mix", 1), ("ffn", 1, 1)]


def run_phases(inputs, phases, stop=None):
    ffn_ids = sorted({ph[1] * 2 + ph[2] for ph in phases if ph[0] == "ffn"})
    mix_ids = sorted({ph[1] for ph in phases if ph[0] == "mix"})
    shared = _prep_shared(inputs, ffn_ids, mix_ids)
    xs = _shard_x(inputs["x"])
    nc, P = build_program(phases, stop=stop)
    in_maps = [dict(shared, xT=xs[c], **_prep_core(inputs, c)) for c in range(8)]
    res = run_bass_kernel_spmd(nc, in_maps, core_ids=list(range(8)))
    if stop is not None:
        return None, [np.asarray(r["dbg"]) for r in res.results]
    return _unshard([r["out"] for r in res.results])


def kernel(**inputs):
    return run_phases(inputs, ALL_PHASES)
```

```python
import numpy as np
import concourse.bass as bass
import concourse.mybir as mybir
from concourse.bass_utils import run_bass_kernel_spmd

F32 = mybir.dt.float32
BF16 = mybir.dt.bfloat16
ALU = mybir.AluOpType
AF = mybir.ActivationFunctionType

D = 1024
NCH = 8
DFF = 2816
NF = 22
DEPTH = 2
T = 2048
SEQ = 8192
EPS = 1e-6

ENGS = ("pe", "act", "dve", "pool", "sp")
EPOCH = 3000


class Buf:
    __slots__ = ("name", "w", "r", "excl")

    def __init__(self, name, inherit=None, excl=False):
        self.name = name
        self.w = None
        self.r = list(inherit) if inherit else []
        self.excl = excl


class Ev:
    __slots__ = ("key", "val", "clock", "op")

    def __init__(self, key, val, clock, op):
        self.key, self.val, self.clock, self.op = key, val, clock, op


class DSem:
    def __init__(self, sem):
        self.sem = sem
        self.count = 0


class Op:
    __slots__ = ("eng", "fn", "waits", "ev", "signal", "sig_no", "dsem", "inc")


class _Rec:
    def __init__(self):
        self.call = None

    def __getattr__(self, name):
        def f(*a, **k):
            assert self.call is None
            self.call = (name, a, k)
            return None
        return f


def _compress(events):
    best = {}
    for ev in events:
        cur = best.get(ev.key)
        if cur is None or cur.val < ev.val:
            best[ev.key] = ev
    return list(best.values())


class Prog:
    def __init__(self, nc):
        self.nc = nc
        self.ops = {e: [] for e in ENGS}
        self.known = {e: {} for e in ENGS}
        self.nwaits = 0
        self._nsem = 0

    def new_dsem(self, name="d"):
        self._nsem += 1
        return DSem(self.nc.alloc_semaphore(name=f"{name}_{self._nsem}"))

    def dsem(self, key):
        if not hasattr(self, "_cache"):
            self._cache = {}
        if key not in self._cache:
            self._cache[key] = self.new_dsem(str(key))
        return self._cache[key]

    def _add(self, eng, fn, reads, writes, dsem=None, inc=16, extra=()):
        op = Op()
        if fn is not None:
            rec = _Rec()
            fn(rec)
            assert rec.call is not None
            fn = rec.call
        op.eng, op.fn, op.dsem, op.signal, op.sig_no, op.inc = eng, fn, dsem, False, 0, inc
        deps = {}
        for b in reads:
            if b.w is not None:
                deps[id(b.w)] = b.w
            if b.excl:
                for r in b.r:
                    if r.key != eng:
                        deps[id(r)] = r
        for b in writes:
            if b.w is not None:
                deps[id(b.w)] = b.w
            for r in b.r:
                deps[id(r)] = r
        for ev in extra:
            deps[id(ev)] = ev
        known = self.known[eng]
        waits = {}
        for ev in deps.values():
            if eng == "pe" and ev.key == "pe":
                continue
            if known.get(ev.key, 0) >= ev.val:
                continue
            cur = waits.get(ev.key)
            if cur is None or cur.val < ev.val:
                waits[ev.key] = ev
        if waits:
            known = dict(known)
            for ev in waits.values():
                for k, v in ev.clock.items():
                    if known.get(k, 0) < v:
                        known[k] = v
                if known.get(ev.key, 0) < ev.val:
                    known[ev.key] = ev.val
            self.known[eng] = known
            self.nwaits += len(waits)
        op.waits = list(waits.values())
        for ev in op.waits:
            if ev.op.dsem is None:
                ev.op.signal = True
        self.ops[eng].append(op)
        idx = len(self.ops[eng])
        if dsem is None:
            ev = Ev(eng, idx, known, op)
        else:
            dsem.count += inc
            ev = Ev(dsem, dsem.count, known, op)
        op.ev = ev
        for b in reads:
            b.r.append(ev)
        for b in writes:
            b.w = ev
            b.r = []
        return ev

    def pe(self, fn, reads, writes):
        return self._add("pe", fn, reads, writes)

    def act(self, fn, reads, writes):
        return self._add("act", fn, reads, writes)

    def dve(self, fn, reads, writes):
        return self._add("dve", fn, reads, writes)

    def dma(self, queue, fn, reads, writes, dsem, inc=16, extra=()):
        return self._add(queue, fn, reads, writes, dsem=dsem, inc=inc, extra=extra)

    def fence(self, eng, events):
        return self._add(eng, None, [], [], extra=events)

    def emit(self):
        nc = self.nc
        esems = {}
        for e in ENGS:
            n = 0
            for op in self.ops[e]:
                if op.signal:
                    n += 1
                    op.sig_no = n
            nep = max(1, (n + EPOCH - 1) // EPOCH)
            esems[e] = [nc.alloc_semaphore(name=f"prog_{e}_{i}") for i in range(nep)]

        def sem_of(ev):
            if isinstance(ev.key, DSem):
                return ev.key.sem, ev.val
            n = ev.op.sig_no
            assert n > 0
            return esems[ev.key][(n - 1) // EPOCH], (n - 1) % EPOCH + 1

        def run(eng_name, eng):
            for op in self.ops[eng_name]:
                for ev in op.waits:
                    s, v = sem_of(ev)
                    eng.wait_ge(s, v)
                if op.fn is None:
                    if op.signal:
                        s, v = sem_of(op.ev)
                        eng.nop().then_inc(s, 1)
                    continue
                name, a, k = op.fn
                ins = getattr(eng, name)(*a, **k)
                if op.dsem is not None:
                    ins.then_inc(op.dsem.sem, op.inc)
                elif op.signal:
                    s, v = sem_of(op.ev)
                    ins.then_inc(s, 1)

        with nc.Block() as block:
            @block.tensor
            def _(e):
                run("pe", e)

            @block.scalar
            def _(e):
                run("act", e)

            @block.vector
            def _(e):
                run("dve", e)

            @block.gpsimd
            def _(e):
                run("pool", e)

            @block.sync
            def _(e):
                run("sp", e)


class Arena:
    def __init__(self, nc, base, top):
        self.nc, self.base, self.top = nc, base, top
        self.live = {}
        self.dead = []
        self.n = 0

    def alloc(self, name, off, free_shape, dtype, nbuf=1):
        esz = 4 if dtype == F32 else 2
        cols = int(np.prod(free_shape))
        start = self.base + off
        end = start + cols * esz
        assert end <= self.top, (name, end, self.top)
        assert start % 32 == 0, (name, start)
        for k, (s, e, _) in self.live.items():
            assert e <= start or s >= end, f"arena overlap {name} vs {k}"
        inherit = []
        keep = []
        for (s, e, evs) in self.dead:
            if not (e <= start or s >= end):
                inherit.extend(evs)
                if s >= start and e <= end:
                    continue
            keep.append((s, e, evs))
        self.dead = keep
        inherit = _compress(inherit)
        self.n += 1
        th = self.nc.alloc_sbuf_tensor_at(f"{name}{self.n}", [128, cols], dtype, offset=start)
        bl = [Buf(f"{name}_{i}", inherit) for i in range(nbuf)]
        self.live[name] = (start, end, bl)
        return th.ap(), bl

    def free(self, name):
        s, e, bl = self.live.pop(name)
        evs = []
        for b in bl:
            if b.w is not None:
                evs.append(b.w)
            evs.extend(b.r)
        self.dead.append((s, e, _compress(evs)))


class WStream:
    NSLOT = 4
    SLOT_ELEMS = 2816

    def __init__(self, P, arena, off):
        self.P = P
        self.ap, self.bufs = arena.alloc("wring", off, [self.NSLOT * self.SLOT_ELEMS], BF16, nbuf=self.NSLOT)
        self.sems = [P.new_dsem("w") for _ in range(self.NSLOT)]
        self.tiles = []
        self.issued = 0
        self.cursor = 0

    def plan(self, tag, dram_ap, nelem):
        assert nelem <= self.SLOT_ELEMS
        self.tiles.append((tag, dram_ap, nelem))

    def _issue(self, i):
        tag, src, nelem = self.tiles[i]
        s = i % self.NSLOT
        dst = self.ap[:, s * self.SLOT_ELEMS: s * self.SLOT_ELEMS + nelem]
        self.P.dma("pool", lambda e, dst=dst, src=src: e.dma_start(out=dst, in_=src),
                   [], [self.bufs[s]], self.sems[s])

    def get(self, tag):
        while self.issued < min(len(self.tiles), self.cursor + self.NSLOT):
            self._issue(self.issued)
            self.issued += 1
        i = self.cursor
        assert self.tiles[i][0] == tag, (self.tiles[i][0], tag)
        self.cursor += 1
        s = i % self.NSLOT
        nelem = self.tiles[i][2]
        return self.ap[:, s * self.SLOT_ELEMS: s * self.SLOT_ELEMS + nelem], self.bufs[s]


NKIND = 6
RG = [[0, 1, 2, 3], [4, 5, 6, 7]]
NT = SEQ // 128
C_EPS = 200
C_SUBG = 208
C_MLG = 210
C_GB = 216
C_CONV = 220
C_GSEL = 240
C_ONE = 244
C_LAM = 256
C_I = 768
C_TRI = 896
C_NEG = 1024
C_SEL = 1152
C_SELSW = 1664
NCST = 2176


def build_program(phases, stop=None):
    import math
    nc = bass.Bass("TRN2", target_bir_lowering=False)
    P = Prog(nc)
    ffn_ids = sorted({ph[1] * 2 + ph[2] for ph in phases if ph[0] == "ffn"})
    mix_ids = sorted({ph[1] for ph in phases if ph[0] == "mix"})
    x_in = nc.dram_tensor("xT", [128, NCH * T], F32, kind="ExternalInput").ap()
    w1_d = nc.dram_tensor("w1", [max(1, len(ffn_ids)) * NF, 128, 2 * NCH * 128], F32, kind="ExternalInput").ap()
    w2_d = nc.dram_tensor("w2", [max(1, len(ffn_ids)) * NCH, 128, NF * 128], F32, kind="ExternalInput").ap()
    wm_d = nc.dram_tensor("wm", [max(1, len(mix_ids)) * 17, 128, 2048], F32, kind="ExternalInput").ap()
    wo_d = nc.dram_tensor("wo", [max(1, len(mix_ids)) * NCH, 128, 1024], F32, kind="ExternalInput").ap()
    rope_d = nc.dram_tensor("rope", [128, 2 * T], F32, kind="ExternalInput").ap()
    cst_d = nc.dram_tensor("cst", [128, NCST], F32, kind="ExternalInput").ap()
    out_d = nc.dram_tensor("out", [128, NCH * T if stop is None else 16], F32, kind="ExternalOutput").ap()
    dbg_d = None
    if stop is not None:
        dbg_d = nc.dram_tensor("dbg", [256, SEQ if stop in ("att", "mls") else 2048], BF16, kind="ExternalOutput").ap()
    NCHK = 13
    gin_c = [nc.dram_tensor(f"gin{i}", [256 if i < 12 else 128, T], BF16).ap() for i in range(NCHK)]
    gout_c = [nc.dram_tensor(f"gout{i}", [4 * (256 if i < 12 else 128), T], BF16).ap() for i in range(NCHK)]
    gin2_c = [nc.dram_tensor(f"gin2_{i}", [256, T], BF16).ap() for i in range(4)]
    gout2_c = [nc.dram_tensor(f"gout2_{i}", [4 * 256, T], BF16).ap() for i in range(4)]
    gin_b = [[Buf(f"gin{k}_{h}") for h in range(4)] for k in range(NKIND)]
    ging_b = Buf("ging")
    gout_b = [Buf(f"gout{i}") for i in range(NCHK)]
    gin2_b = [[Buf(f"gin2_{k}_{q}") for q in range(16)] for k in range(2)]
    gout2_b = [Buf(f"gout2_{i}") for i in range(4)]

    def gin_kh(kind, h):
        return gin_c[kind * 2 + h // 2][(h % 2) * 128:(h % 2) * 128 + 128, :]

    def gout_rkh(r, kind, h):
        base = r * 256 + (h % 2) * 128
        return gout_c[kind * 2 + h // 2][base:base + 128, :]

    A = Arena(nc, 16512, 229344)
    xT, xT_b = A.alloc("xT", 0, [NCH * T], F32, nbuf=4)
    xT3 = xT.rearrange("p (c t) -> p c t", c=NCH)
    W = WStream(P, A, 65536)
    CB = 88064
    cst, cst_b = A.alloc("cst", CB, [768], F32)
    cbf, cbf_b = A.alloc("cbf", CB + 3072, [1552], BF16)
    ones_bf, I_bf, tri_bf, neg_bf, sel_bf = (cbf[:, 0:128], cbf[:, 128:256], cbf[:, 256:384],
                                             cbf[:, 384:512], cbf[:, 512:1024])
    ones_b = cbf_b
    sml, sml_b = A.alloc("sml", CB + 3072 + 3104, [128], F32)
    PH = CB + 3072 + 3104 + 512
    gsel_bf = cbf[:, 1024:1026]
    selsw_bf = cbf[:, 1040:1552]
    psum = [nc.alloc_psum_tensor(f"ps{i}", [128, 512], F32).ap() for i in range(8)]
    ps_b = [Buf(f"ps{i}", excl=True) for i in range(8)]
    ld = [P.new_dsem("ld") for _ in range(4)]
    misc = P.new_dsem("misc")
    rope_sem = P.new_dsem("rope")
    ging_sem = P.new_dsem("ging")

    for ph in phases:
        kind = ph[0]
        if kind == "ffn":
            _, l, k = ph
            lf = l * 2 + k
            fi = ffn_ids.index(lf)
            for ps_ in range(2):
                for f in range(NF):
                    W.plan(("w1", lf, ps_, f), w1_d[fi * NF + f], 2 * NCH * 128)
                for m in range(NCH):
                    W.plan(("w2", lf, ps_, m), w2_d[fi * NCH + m], NF * 128)
        else:
            l = ph[1]
            mi = mix_ids.index(l)
            for ti in range(17):
                W.plan(("wm", l, ti), wm_d[mi * 17 + ti], 2048)
            if stop is None:
                for m in range(NCH):
                    W.plan(("wo", l, m), wo_d[mi * NCH + m], 1024)

    for tt in range(4):
        P.dma("sp", lambda e, tt=tt: e.dma_start(out=xT3[:, :, tt * 512:(tt + 1) * 512],
                                                 in_=x_in.rearrange("p (c t) -> p c t", c=NCH)[:, :, tt * 512:(tt + 1) * 512]),
              [], [xT_b[tt]], ld[tt])
    P.dma("sp", lambda e: e.dma_start(out=cst, in_=cst_d[:, 0:768]), [], [cst_b[0]], misc)
    ctmp, ctmp_b = A.alloc("ctmp", PH, [NCST - 768], F32)
    ctmp_sem = P.new_dsem("ctmp")
    P.dma("sp", lambda e: e.dma_start(out=ctmp, in_=cst_d[:, 768:NCST]), [], [ctmp_b[0]], ctmp_sem)
    P.dve(lambda e: e.memset(ones_bf, 1.0), [], [cbf_b[0]])
    P.dve(lambda e: e.tensor_copy(cbf[:, 128:1024], ctmp[:, 0:896]), [ctmp_b[0]], [cbf_b[0]])
    P.dve(lambda e: e.tensor_copy(gsel_bf, cst[:, C_GSEL:C_GSEL + 2]), [cst_b[0]], [cbf_b[0]])
    P.dve(lambda e: e.tensor_copy(selsw_bf, ctmp[:, C_SELSW - 768:C_SELSW - 768 + 512]), [ctmp_b[0]], [cbf_b[0]])
    A.free("ctmp")

    eps_col = cst[:, C_EPS:C_EPS + 1]

    def gain(l, k, c):
        i = (l * 6 + k) * NCH + c
        return cst[:, i:i + 1]

    def gain_half(l, k, c):
        i = 96 + (l * 6 + k) * NCH + c
        return cst[:, i:i + 1]

    cst_half_b = Buf("csthalf")
    P.dve(lambda e: e.tensor_scalar(cst[:, 96:192], cst[:, 0:96], 0.5, None, ALU.mult), [cst_b[0]], [cst_half_b])

    def rms_rstd(dst, src_ps, reads, writes, scale=1.0 / D):
        P.act(lambda e: e.activation(dst, src_ps, AF.Sqrt, bias=eps_col, scale=scale), reads + [cst_b[0]], writes)
        P.dve(lambda e: e.reciprocal(dst, dst), writes, writes)

    def pre_norm(l, kpre, ntt, t0, xn3, xn_b, sq3, sq_bl, rs, rs_b, pbank):
        for tt in range(ntt):
            g = (t0 + tt * 512) // 512
            ts = slice(t0 + tt * 512, t0 + (tt + 1) * 512)
            ls = slice(tt * 512, (tt + 1) * 512)
            P.dve(lambda e, ts=ts: e.tensor_tensor(sq3, xT3[:, :, ts], xT3[:, :, ts], ALU.mult), [xT_b[g]], sq_bl)
            pss, pss_b = psum[pbank], ps_b[pbank]
            for c in range(NCH):
                P.pe(lambda e, c=c, pss=pss: e.matmul(pss, ones_bf, sq3[:, c, :], start=(c == 0), stop=(c == NCH - 1)),
                     [ones_b[0]] + sq_bl, [pss_b])
            rms_rstd(rs, pss, [pss_b], [rs_b])
            for c in range(NCH):
                P.dve(lambda e, c=c, ts=ts, ls=ls: e.scalar_tensor_tensor(
                    xn3[:, c, ls], xT3[:, c, ts], gain(l, kpre, c), rs, ALU.mult, ALU.mult),
                    [xT_b[g], rs_b, cst_b[0]], [xn_b[c * ntt + tt]])

    def ffn_phase(l, k):
        lf = l * 2 + k
        kpre, kpost = (0, 1) if k == 0 else (4, 5)
        xn, xn_b = A.alloc("xn", PH, [NCH * 1024], BF16, nbuf=16)
        xn3 = xn.rearrange("p (c t) -> p c t", c=NCH)
        actT, act_b = A.alloc("actT", PH + 16384, [NF * 1024], BF16, nbuf=NF * 2)
        act3 = actT.rearrange("p (f t) -> p f t", f=NF)
        hT, h_b = A.alloc("hT", PH + 61440, [NCH * 1024], F32, nbuf=16)
        h3 = hT.rearrange("p (c t) -> p c t", c=NCH)
        sq, sq_b = A.alloc("sq", PH + 94208, [NCH * 512], BF16, nbuf=2)
        sq3 = sq.rearrange("p (c t) -> p c t", c=NCH)
        sg, sg_b = A.alloc("sg", PH + 102400, [2 * 512], F32, nbuf=2)
        rs, rs_b = A.alloc("rs", PH + 106496, [2 * 512], F32, nbuf=2)
        tmp, tmp_b = A.alloc("tmp", PH + 110592, [2 * 512], F32, nbuf=2)
        for ps_ in range(2):
            t0 = ps_ * 1024
            pre_norm(l, kpre, 2, t0, xn3, xn_b, sq3, [sq_b[0], sq_b[1]], rs[:, 0:512], rs_b[0], 6)
            it = 0
            for f in range(NF):
                wt, wb = W.get(("w1", lf, ps_, f))
                wt4 = wt.rearrange("p (g c n) -> p g c n", g=2, c=NCH)
                for tt in range(2):
                    ls = slice(tt * 512, (tt + 1) * 512)
                    pg, pg_b = psum[(it % 2) * 2], ps_b[(it % 2) * 2]
                    pu, pu_b = psum[(it % 2) * 2 + 1], ps_b[(it % 2) * 2 + 1]
                    for c in range(NCH):
                        P.pe(lambda e, c=c, pg=pg, wt4=wt4, ls=ls: e.matmul(pg, wt4[:, 0, c, :], xn3[:, c, ls],
                                                                             start=(c == 0), stop=(c == NCH - 1)),
                             [wb, xn_b[c * 2 + tt]], [pg_b])
                    for c in range(NCH):
                        P.pe(lambda e, c=c, pu=pu, wt4=wt4, ls=ls: e.matmul(pu, wt4[:, 1, c, :], xn3[:, c, ls],
                                                                             start=(c == 0), stop=(c == NCH - 1)),
                             [wb, xn_b[c * 2 + tt]], [pu_b])
                    sgl = sg[:, (it % 2) * 512:(it % 2 + 1) * 512]
                    P.act(lambda e, sgl=sgl, pg=pg: e.activation(sgl, pg, AF.Silu), [pg_b], [sg_b[it % 2]])
                    P.dve(lambda e, f=f, ls=ls, sgl=sgl, pu=pu: e.tensor_tensor(act3[:, f, ls], sgl, pu, ALU.mult),
                          [sg_b[it % 2], pu_b], [act_b[f * 2 + tt]])
                    it += 1
            it = 0
            for m in range(NCH):
                wt, wb = W.get(("w2", lf, ps_, m))
                wt3 = wt.rearrange("p (f n) -> p f n", f=NF)
                for tt in range(2):
                    ls = slice(tt * 512, (tt + 1) * 512)
                    ph_, ph_b = psum[4 + it % 2], ps_b[4 + it % 2]
                    for f in range(NF):
                        P.pe(lambda e, f=f, ph_=ph_, wt3=wt3, ls=ls: e.matmul(ph_, wt3[:, f, :], act3[:, f, ls],
                                                                               start=(f == 0), stop=(f == NF - 1)),
                             [wb, act_b[f * 2 + tt]], [ph_b])
                    P.act(lambda e, m=m, ls=ls, ph_=ph_: e.activation(h3[:, m, ls], ph_, AF.Copy), [ph_b], [h_b[m * 2 + tt]])
                    sql = sq[:, (it % 2) * 512:(it % 2 + 1) * 512]
                    P.act(lambda e, sql=sql, ph_=ph_: e.activation(sql, ph_, AF.Square), [ph_b], [sq_b[it % 2]])
                    pss, pss_b = psum[6 + tt], ps_b[6 + tt]
                    P.pe(lambda e, m=m, pss=pss, sql=sql: e.matmul(pss, ones_bf, sql, start=(m == 0), stop=(m == NCH - 1)),
                         [ones_b[0], sq_b[it % 2]], [pss_b])
                    it += 1
            for tt in range(2):
                g = ps_ * 2 + tt
                ts = slice(t0 + tt * 512, t0 + (tt + 1) * 512)
                ls = slice(tt * 512, (tt + 1) * 512)
                rsl = rs[:, tt * 512:(tt + 1) * 512]
                rms_rstd(rsl, psum[6 + tt], [ps_b[6 + tt]], [rs_b[tt]])
                for m in range(NCH):
                    tl = tmp[:, (m % 2) * 512:(m % 2 + 1) * 512]
                    P.dve(lambda e, m=m, ls=ls, tl=tl, rsl=rsl: e.tensor_tensor(tl, h3[:, m, ls], rsl, ALU.mult),
                          [h_b[m * 2 + tt], rs_b[tt]], [tmp_b[m % 2]])
                    P.dve(lambda e, m=m, ts=ts, tl=tl: e.scalar_tensor_tensor(
                        xT3[:, m, ts], tl, gain_half(l, kpost, m), xT3[:, m, ts], ALU.mult, ALU.add),
                        [tmp_b[m % 2], cst_half_b, xT_b[g]], [xT_b[g]])
        for n in ("xn", "actT", "hT", "sq", "sg", "rs", "tmp"):
            A.free(n)

    cnt = {"ev": 0, "ps": 0}

    def evac(dst, src, reads, writes, func=None):
        if func is not None:
            P.act(lambda e: e.activation(dst, src, func), reads, writes)
        else:
            cnt["ev"] += 1
            if cnt["ev"] % 2:
                P.act(lambda e: e.activation(dst, src, AF.Copy), reads, writes)
            else:
                P.dve(lambda e: e.tensor_copy(dst, src), reads, writes)

    def mix_phase(l):
        lam_init = 0.8 - 0.6 * math.exp(-0.3 * l)
        lamv = cst[:, C_LAM + 256 * l: C_LAM + 256 * (l + 1)]
        P.dve(lambda e: e.tensor_tensor(sml[:, 8:72], lamv[:, 0:64], lamv[:, 64:128], ALU.mult), [cst_b[0]], [sml_b[0]])
        P.dve(lambda e: e.reduce_sum(sml[:, 2:3], sml[:, 8:72], axis=mybir.AxisListType.X), [sml_b[0]], [sml_b[0]])
        P.dve(lambda e: e.tensor_tensor(sml[:, 8:72], lamv[:, 128:192], lamv[:, 192:256], ALU.mult), [cst_b[0], sml_b[0]], [sml_b[0]])
        P.dve(lambda e: e.reduce_sum(sml[:, 3:4], sml[:, 8:72], axis=mybir.AxisListType.X), [sml_b[0]], [sml_b[0]])
        P.act(lambda e: e.activation(sml[:, 2:4], sml[:, 2:4], AF.Exp), [sml_b[0]], [sml_b[0]])
        P.dve(lambda e: e.tensor_tensor(sml[:, 0:1], sml[:, 3:4], sml[:, 2:3], ALU.subtract), [sml_b[0]], [sml_b[0]])
        P.dve(lambda e: e.tensor_scalar(sml[:, 0:1], sml[:, 0:1], -lam_init, None, ALU.add), [sml_b[0]], [sml_b[0]])
        P.dve(lambda e: e.tensor_scalar(sml[:, 1:2], cst[:, C_SUBG + l:C_SUBG + l + 1], 1.0 - lam_init, None, ALU.mult),
              [cst_b[0], sml_b[0]], [sml_b[0]])
        neglam, gsub = sml[:, 0:1], sml[:, 1:2]

        xn, xn_b = A.alloc("xn2", PH, [NCH * T], BF16, nbuf=NCH * 4)
        xn3 = xn.rearrange("p (c t) -> p c t", c=NCH)
        rope, rope_b = A.alloc("rope", PH + 32768, [2 * T], F32)
        rope3 = rope.rearrange("p (k t) -> p k t", k=2)
        stg, stg_b = A.alloc("stg", PH + 49152, [4 * T], BF16, nbuf=4)
        sq, sq_b = A.alloc("sq", PH + 65536, [NCH * 512], BF16)
        sq3 = sq.rearrange("p (c t) -> p c t", c=NCH)
        rs, rs_b = A.alloc("rs", PH + 73728, [512], F32)
        t12, t12_b = A.alloc("t12", PH + 75776, [4 * 512], F32, nbuf=4)
        stgg, stgg_b = A.alloc("stgg", PH + 83968, [T], BF16)
        P.dma("sp", lambda e: e.dma_start(out=rope, in_=rope_d), [], [rope_b[0]], rope_sem)
        pre_norm(l, 2, 4, 0, xn3, xn_b, sq3, [sq_b[0]], rs, rs_b[0], 7)
        def issue_gather(cj):
            rd = [ging_b] if cj == 12 else [gin_b[cj // 2][2 * (cj % 2)], gin_b[cj // 2][2 * (cj % 2) + 1]]
            P.dma("pool", lambda e: e.collective_compute("AllGather", ALU.bypass, replica_groups=RG,
                                                         ins=[gin_c[cj].opt()], outs=[gout_c[cj].opt()]),
                  rd, [gout_b[cj]], P.dsem(("cc1", cj)), inc=1)

        stg_sem = [P.dsem(("sg", i)) for i in range(4)]
        si = 0
        pc = 0
        for ti in range(17):
            wt, wb = W.get(("wm", l, ti))
            wt4 = wt.rearrange("p (g c n) -> p g c n", g=2, c=NCH)
            if ti < 8:
                kind, h = (0, ti) if ti < 4 else (1, ti - 4)
                slot = si % 4
                si += 1
                sl = stg[:, slot * T:(slot + 1) * T]
                for tt in range(4):
                    ts = slice(tt * 512, (tt + 1) * 512)
                    ba, bb = (tt % 2) * 2, (tt % 2) * 2 + 1
                    for g_, bk in ((0, ba), (1, bb)):
                        for c in range(NCH):
                            P.pe(lambda e, c=c, g_=g_, bk=bk, wt4=wt4, ts=ts: e.matmul(
                                psum[bk], wt4[:, g_, c, :], xn3[:, c, ts], start=(c == 0), stop=(c == NCH - 1)),
                                [wb, xn_b[c * 4 + tt]], [ps_b[bk]])
                    t1 = t12[:, ba * 512:(ba + 1) * 512]
                    t2 = t12[:, bb * 512:(bb + 1) * 512]
                    P.dve(lambda e, t1=t1, ba=ba, ts=ts: e.tensor_tensor(t1, psum[ba], rope3[:, 0, ts], ALU.mult),
                          [ps_b[ba], rope_b[0]], [t12_b[ba]])
                    P.dve(lambda e, t2=t2, bb=bb, ts=ts: e.tensor_tensor(t2, psum[bb], rope3[:, 1, ts], ALU.mult),
                          [ps_b[bb], rope_b[0]], [t12_b[bb]])
                    P.dve(lambda e, t1=t1, t2=t2, sl=sl, ts=ts: e.tensor_tensor(sl[:, ts], t1, t2, ALU.add),
                          [t12_b[ba], t12_b[bb]], [stg_b[slot]])
                P.dma("sp", lambda e, kind=kind, h=h, sl=sl: e.dma_start(out=gin_kh(kind, h), in_=sl),
                      [stg_b[slot]], [gin_b[kind][h]], stg_sem[slot])
                if h % 2 == 1:
                    issue_gather(kind * 2 + h // 2)
            elif ti < 16:
                kind = 2 + (ti - 8) // 2
                for g_ in range(2):
                    h = 2 * ((ti - 8) % 2) + g_
                    slot = si % 4
                    si += 1
                    sl = stg[:, slot * T:(slot + 1) * T]
                    for tt in range(4):
                        ts = slice(tt * 512, (tt + 1) * 512)
                        bk = 4 + pc % 2
                        pc += 1
                        for c in range(NCH):
                            P.pe(lambda e, c=c, g_=g_, bk=bk, wt4=wt4, ts=ts: e.matmul(
                                psum[bk], wt4[:, g_, c, :], xn3[:, c, ts], start=(c == 0), stop=(c == NCH - 1)),
                                [wb, xn_b[c * 4 + tt]], [ps_b[bk]])
                        evac(sl[:, ts], psum[bk], [ps_b[bk]], [stg_b[slot]], AF.Sigmoid if kind == 5 else None)
                    P.dma("sp", lambda e, kind=kind, h=h, sl=sl: e.dma_start(out=gin_kh(kind, h), in_=sl),
                          [stg_b[slot]], [gin_b[kind][h]], stg_sem[slot])
                    if h % 2 == 1:
                        issue_gather(kind * 2 + h // 2)
            else:
                slot = si % 4
                si += 1
                sl = stg[:, slot * T:(slot + 1) * T]
                for tt in range(4):
                    ts = slice(tt * 512, (tt + 1) * 512)
                    for c in range(NCH):
                        P.pe(lambda e, c=c, wt4=wt4, ts=ts: e.matmul(
                            psum[6][0:40, :], wt4[:, 0, c, 0:40], xn3[:, c, ts], start=(c == 0), stop=(c == NCH - 1)),
                            [wb, xn_b[c * 4 + tt]], [ps_b[6]])
                    P.act(lambda e, ts=ts, sl=sl: e.activation(sl[0:8, ts], psum[6][0:8, :], AF.Copy), [ps_b[6]], [stg_b[slot]])
                    P.act(lambda e, ts=ts: e.activation(stgg[32:40, ts], psum[6][32:40, :], AF.Copy), [ps_b[6]], [stgg_b[0]])
                    P.dve(lambda e, ts=ts, sl=sl: e.tensor_tensor(sl[32:40, ts], psum[6][32:40, :], stgg[32:40, ts], ALU.subtract),
                          [ps_b[6], stgg_b[0]], [stg_b[slot]])
                P.dma("sp", lambda e, sl=sl: e.dma_start(out=gin_c[12], in_=sl), [stg_b[slot]], [ging_b], stg_sem[slot])
                issue_gather(12)
        for n in ("xn2", "rope", "stg", "sq", "rs", "t12", "stgg"):
            A.free(n)

        if stop == "proj":
            return
        if stop == "gather":
            return
        KT, KT_b = A.alloc("KT", PH, [SEQ], BF16, nbuf=16)
        QT, QT_b = A.alloc("QT", PH + 16384, [SEQ], BF16, nbuf=16)
        VT, VT_b = A.alloc("VT", PH + 32768, [SEQ], BF16, nbuf=16)
        cand, cand_b = A.alloc("cand", PH + 49152, [3 * 2048], BF16, nbuf=3)
        cand_sem = [P.dsem(("cd", i)) for i in range(3)]
        ci = [0]

        def load_cand(kind, r, tt):
            slot = ci[0] % 3
            ci[0] += 1
            cs = cand[:, slot * 2048:(slot + 1) * 2048].rearrange("p (h t) -> p h t", h=4)
            for half in range(2):
                P.dma("sp", lambda e, cs=cs, kind=kind, r=r, tt=tt, half=half: e.dma_start(
                    out=cs[:, 2 * half:2 * half + 2, :],
                    in_=gout_c[kind * 2 + half][r * 256:(r + 1) * 256, tt * 512:(tt + 1) * 512].rearrange("(h p) t -> p h t", h=2)),
                    [gout_b[kind * 2 + half]], [cand_b[slot]], cand_sem[slot])
            return cs, cand_b[slot]

        def select_fm(kind, dst, dst_b):
            for r in range(4):
                for tt in range(4):
                    cs, cb = load_cand(kind, r, tt)
                    bk = cnt["ps"] % 2
                    cnt["ps"] += 1
                    for h in range(4):
                        P.pe(lambda e, h=h, cs=cs, bk=bk: e.matmul(psum[bk], sel_bf[:, h * 128:(h + 1) * 128], cs[:, h, :],
                                                                  start=(h == 0), stop=(h == 3)),
                             [cbf_b[0], cb], [ps_b[bk]])
                    g = r * 4 + tt
                    evac(dst[:, g * 512:(g + 1) * 512], psum[bk], [ps_b[bk]], [dst_b[g]])

        def select_tok(kind, dst, dst_b):
            for r in range(4):
                for tt in range(4):
                    cs, cb = load_cand(kind, r, tt)
                    bk = cnt["ps"] % 2
                    cnt["ps"] += 1
                    for sub in range(4):
                        for h in range(4):
                            P.pe(lambda e, h=h, sub=sub, cs=cs, bk=bk: e.matmul(
                                psum[bk][:, sub * 128:(sub + 1) * 128], cs[:, h, sub * 128:(sub + 1) * 128],
                                sel_bf[:, h * 128:(h + 1) * 128], start=(h == 0), stop=(h == 3)),
                                [cbf_b[0], cb], [ps_b[bk]])
                    g = r * 4 + tt
                    evac(dst[:, g * 512:(g + 1) * 512], psum[bk], [ps_b[bk]], [dst_b[g]])

        select_fm(1, KT, KT_b)
        select_fm(0, QT, QT_b)
        select_tok(2, VT, VT_b)

        pbuf, pbuf_b = A.alloc("pbuf", PH + 61440, [6 * 512], BF16, nbuf=6)
        rr, rr_b = A.alloc("rr", PH + 67584, [2 * 512], F32, nbuf=2)
        oo, oo_b = A.alloc("oo", PH + 71680, [2 * 512], F32, nbuf=2)
        sqa, sqa_b = A.alloc("sqa", PH + 75776, [512], BF16)
        rsa, rsa_b = A.alloc("rsa", PH + 76800, [512], F32)
        yst, yst_b = A.alloc("yst", PH + 78848, [2 * 512], BF16, nbuf=2)
        yst_sem = [P.dsem(("ys", i)) for i in range(2)]
        steps = [(gq, kt) for gq in range(16) for kt in range(4 * gq + 4)]
        nst = len(steps)

        def att_qk(i):
            gq, kt = steps[i]
            q0 = gq * 512
            r = kt - 4 * gq
            off = 0 if r < 0 else 128 * r
            sb_ = 4 + (i % 2) * 2
            pslot = (i % 3) * 2
            for c in range(2):
                P.pe(lambda e, c=c: e.matmul(
                    psum[sb_ + c][:, off:512], KT[c * 64:(c + 1) * 64, kt * 128:(kt + 1) * 128],
                    QT[c * 64:(c + 1) * 64, q0 + off:q0 + 512], start=True, stop=True),
                    [KT_b[kt // 4], QT_b[gq]], [ps_b[sb_ + c]])
            for c in range(2):
                pb_ = pbuf[:, (pslot + c) * 512:(pslot + c + 1) * 512]
                P.act(lambda e, c=c, pb_=pb_: e.activation(
                    pb_[:, off:512], psum[sb_ + c][:, off:512], AF.Exp, scale=0.125),
                    [ps_b[sb_ + c]], [pbuf_b[pslot + c]])
                if r >= 0:
                    P.dve(lambda e, pb_=pb_: e.memset(pb_[64:128, off:off + 64], 0.0),
                          [pbuf_b[pslot + c]], [pbuf_b[pslot + c]])

        def att_pv(i):
            gq, kt = steps[i]
            q0 = gq * 512
            nkt = 4 * gq + 4
            r = kt - 4 * gq
            off = 0 if r < 0 else 128 * r
            pslot = (i % 3) * 2
            for c in range(2):
                pb_ = pbuf[:, (pslot + c) * 512:(pslot + c + 1) * 512]
                P.pe(lambda e, c=c, pb_=pb_: e.matmul(
                    psum[c][:, off:512], VT[:, kt * 128:(kt + 1) * 128], pb_[:, off:512],
                    start=(kt == 0), stop=(kt == nkt - 1)),
                    [VT_b[kt // 4], pbuf_b[pslot + c]], [ps_b[c]])
                P.pe(lambda e, c=c, pb_=pb_: e.matmul(
                    psum[2 + c][:, off:512], ones_bf, pb_[:, off:512],
                    start=(kt == 0), stop=(kt == nkt - 1)),
                    [cbf_b[0], pbuf_b[pslot + c]], [ps_b[2 + c]])
            if kt != nkt - 1:
                return
            for c in range(2):
                rl = rr[:, c * 512:(c + 1) * 512]
                ol = oo[:, c * 512:(c + 1) * 512]
                P.dve(lambda e, c=c, rl=rl: e.reciprocal(rl, psum[2 + c]), [ps_b[2 + c]], [rr_b[c]])
                P.dve(lambda e, c=c, rl=rl, ol=ol: e.tensor_tensor(ol, psum[c], rl, ALU.mult), [ps_b[c], rr_b[c]], [oo_b[c]])
            o1, o2 = oo[:, 0:512], oo[:, 512:1024]
            P.dve(lambda e: e.scalar_tensor_tensor(o1, o2, neglam, o1, ALU.mult, ALU.add), [oo_b[0], oo_b[1], sml_b[0]], [oo_b[0]])
            P.act(lambda e: e.activation(sqa, o1, AF.Square), [oo_b[0]], [sqa_b[0]])
            sb_ = 4 + (i % 2) * 2
            P.pe(lambda e: e.matmul(psum[sb_], ones_bf, sqa, start=True, stop=True), [cbf_b[0], sqa_b[0]], [ps_b[sb_]])
            rms_rstd(rsa, psum[sb_], [ps_b[sb_]], [rsa_b[0]], scale=1.0 / 128)
            ys = gq % 2
            yl = yst[:, ys * 512:(ys + 1) * 512]
            P.dve(lambda e: e.scalar_tensor_tensor(yl, o1, gsub, rsa, ALU.mult, ALU.mult),
                  [oo_b[0], rsa_b[0], sml_b[0]], [yst_b[ys]])
            P.dma("sp", lambda e: e.dma_start(out=gin2_c[q0 // T][0:128, q0 % T:q0 % T + 512], in_=yl),
                  [yst_b[ys]], [gin2_b[0][gq]], yst_sem[ys])

        for i in range(nst + 1):
            if i < nst:
                att_qk(i)
            if i >= 1:
                att_pv(i - 1)
        for n in ("KT", "QT", "VT", "pbuf", "rr", "oo", "sqa", "rsa", "yst"):
            A.free(n)
        A.free("cand")
        if stop == "att":
            return

        NPQ = 8208
        PQK, PQK_b = A.alloc("PQK", PH, [NPQ], BF16, nbuf=17)
        PKQ, PKQ_b = A.alloc("PKQ", PH + 16416, [NPQ], BF16, nbuf=17)
        MV, MV_b = A.alloc("MV", PH + 32832, [64 * 129], BF16, nbuf=17)
        MV3 = MV.rearrange("p (t d) -> p t d", t=64)
        MO, MO_b = A.alloc("MO", PH + 49344, [SEQ], BF16, nbuf=16)
        cand, cand_b = A.alloc("cand", PH + 65728, [3 * 2048], BF16, nbuf=3)
        GG, GG_b = A.alloc("GG", PH + 78016, [2 * 2048], BF16, nbuf=2)
        SO = PH + 86208
        gts, gts_b = A.alloc("gts", SO, [128], F32)
        gw, gw_b = A.alloc("gw", SO + 512, [6 * 64], F32)
        gwb, gwb_b = A.alloc("gwb", SO + 2048, [2 * 64], BF16)
        lfb, lfb_b = A.alloc("lfb", SO + 2304, [2 * 256], BF16, nbuf=2)
        dex, dex_b = A.alloc("dex", SO + 3328, [2 * 128], F32, nbuf=2)
        wTt, wT_b = A.alloc("wT", SO + 4352, [2 * 128], BF16, nbuf=2)
        ebt, eb_b = A.alloc("eb", SO + 4864, [2 * 128], F32, nbuf=2)
        qtt, qt_b = A.alloc("qt", SO + 5888, [2 * 128], BF16, nbuf=2)
        sct, sc_b = A.alloc("sc", SO + 6400, [2 * 8], F32, nbuf=2)
        ktt, kt_b = A.alloc("kt", SO + 6464, [2 * 64], BF16, nbuf=2)
        Cs, Cs_b = A.alloc("Cs", SO + 6720, [136], F32)
        Cbf, Cbf_b = A.alloc("Cbf", SO + 7264, [2 * 256], BF16, nbuf=2)
        rdn, rdn_b = A.alloc("rdn", SO + 8288, [2 * 128], F32, nbuf=2)
        acc, acc_b = A.alloc("acc", SO + 9312, [2 * 512], F32, nbuf=2)
        aqk, aqk_b = A.alloc("aqk", SO + 13408, [2 * 512], F32, nbuf=2)
        qkb, qkb_b = A.alloc("qkb", SO + 17504, [2 * 512], BF16, nbuf=2)
        hTm, hTm_b = A.alloc("hTm", SO + 19552, [512], F32)
        sqm, sqm_b = A.alloc("sqm", SO + 21600, [512], BF16)
        rsm, rsm_b = A.alloc("rsm", SO + 22624, [512], F32)
        ysm, ysm_b = A.alloc("ysm", SO + 24672, [2 * 512], BF16, nbuf=2)
        cand_sem = [P.dsem(("cd", i)) for i in range(3)]
        ci[0] = 0

        def load_cand2(kind, r, tt):
            slot = ci[0] % 3
            ci[0] += 1
            cs = cand[:, slot * 2048:(slot + 1) * 2048].rearrange("p (h t) -> p h t", h=4)
            for half in range(2):
                P.dma("sp", lambda e, cs=cs, kind=kind, r=r, tt=tt, half=half: e.dma_start(
                    out=cs[:, 2 * half:2 * half + 2, :],
                    in_=gout_c[kind * 2 + half][r * 256:(r + 1) * 256, tt * 512:(tt + 1) * 512].rearrange("(h p) t -> p h t", h=2)),
                    [gout_b[kind * 2 + half]], [cand_b[slot]], cand_sem[slot])
            return cs, cand_b[slot]

        P.dve(lambda e: e.memset(PQK[:, 0:3], 0.0), [], [PQK_b[16]])
        P.dve(lambda e: e.memset(PKQ[:, 0:3], 0.0), [], [PKQ_b[16]])
        swap_bf = None
        for r in range(4):
            for tt in range(4):
                g = r * 4 + tt
                cs, cb = load_cand2(3, r, tt)
                for dst, dstb, selm in ((PQK, PQK_b, 0), (PKQ, PKQ_b, 1)):
                    bk = cnt["ps"] % 2
                    cnt["ps"] += 1
                    for h in range(4):
                        lhs = sel_bf[:, h * 128:(h + 1) * 128] if selm == 0 else selsw_bf[:, h * 128:(h + 1) * 128]
                        P.pe(lambda e, h=h, cs=cs, bk=bk, lhs=lhs: e.matmul(psum[bk], lhs, cs[:, h, :], start=(h == 0), stop=(h == 3)),
                             [cbf_b[0], cb], [ps_b[bk]])
                    evac(dst[:, 3 + g * 512: 3 + (g + 1) * 512], psum[bk], [ps_b[bk]], [dstb[g]])
        for r in range(4):
            for tt in range(4):
                g = r * 4 + tt
                cs, cb = load_cand2(5, r, tt)
                bk = cnt["ps"] % 2
                cnt["ps"] += 1
                for h in range(4):
                    P.pe(lambda e, h=h, cs=cs, bk=bk: e.matmul(psum[bk], sel_bf[:, h * 128:(h + 1) * 128], cs[:, h, :],
                                                              start=(h == 0), stop=(h == 3)), [cbf_b[0], cb], [ps_b[bk]])
                evac(MO[:, g * 512:(g + 1) * 512], psum[bk], [ps_b[bk]], [MO_b[g]])
        P.dve(lambda e: e.memset(MV3[:, :, 128:129], 1.0), [], [MV_b[16]])
        for r in range(4):
            for tt in range(4):
                g = r * 4 + tt
                cs, cb = load_cand2(4, r, tt)
                bk = cnt["ps"] % 2
                cnt["ps"] += 1
                for sub in range(4):
                    for h in range(4):
                        P.pe(lambda e, h=h, sub=sub, cs=cs, bk=bk: e.matmul(
                            psum[bk][:, sub * 128:(sub + 1) * 128], cs[:, h, sub * 128:(sub + 1) * 128],
                            sel_bf[:, h * 128:(h + 1) * 128], start=(h == 0), stop=(h == 3)), [cbf_b[0], cb], [ps_b[bk]])
                evac(MV3[:, g * 4:(g + 1) * 4, 0:128], psum[bk].rearrange("p (t d) -> p t d", t=4), [ps_b[bk]], [MV_b[g]])
        gg_sem = [P.dsem(("gg", i)) for i in range(2)]
        P.dve(lambda e: e.memset(GG, 0.0), [], [GG_b[0], GG_b[1]])
        for r in range(4):
            sl_ = r % 2
            ggl = GG[:, sl_ * 2048:(sl_ + 1) * 2048]
            P.dma("sp", lambda e, ggl=ggl, r=r: e.dma_start(out=ggl[0:40, :], in_=gout_c[12][r * 128:r * 128 + 40, :]),
                  [gout_b[12]], [GG_b[sl_]], gg_sem[sl_])
            for t in range(16):
                T_ = r * 16 + t
                P.pe(lambda e, T_=T_, t=t, ggl=ggl: e.matmul(psum[3][:, 2 * T_:2 * T_ + 2], ggl[0:8, t * 128:(t + 1) * 128],
                                                            gsel_bf[0:8, :], start=True, stop=False), [GG_b[sl_], cbf_b[0]], [ps_b[3]])
                P.pe(lambda e, T_=T_, t=t, ggl=ggl: e.matmul(psum[3][:, 2 * T_:2 * T_ + 2], ggl[32:40, t * 128:(t + 1) * 128],
                                                            gsel_bf[32:40, :], start=False, stop=True), [GG_b[sl_], cbf_b[0]], [ps_b[3]])
        g3v = gts.rearrange("p (t k) -> p t k", k=2)
        p3v = psum[3][:, 0:128].rearrange("p (t k) -> p t k", k=2)
        for k_ in range(2):
            P.dve(lambda e, k_=k_: e.tensor_scalar(g3v[:, :, k_:k_ + 1], p3v[:, :, k_:k_ + 1],
                                                    cst[:, C_GB + 2 * l + k_:C_GB + 2 * l + k_ + 1], None, ALU.add),
                  [ps_b[3], cst_b[0]], [gts_b[0]])
        e1, lf, lfhf, lfl, ccv = (gw[:, i * 64:(i + 1) * 64] for i in range(5))
        lfh_bf, lfl_bf = gwb[:, 0:64], gwb[:, 64:128]
        gfv = g3v[:, :, 1:2].rearrange("p t k -> p (t k)")
        giv = g3v[:, :, 0:1].rearrange("p t k -> p (t k)")
        P.act(lambda e: e.activation(e1, gfv, AF.Exp, scale=-1.0), [gts_b[0]], [gw_b[0]])
        P.act(lambda e: e.activation(lf, e1, AF.Ln, bias=cst[:, C_ONE:C_ONE + 1], scale=1.0), [gw_b[0], cst_b[0]], [gw_b[0]])
        P.dve(lambda e: e.tensor_scalar(lf, lf, -1.0, None, ALU.mult), [gw_b[0]], [gw_b[0]])
        P.dve(lambda e: e.tensor_copy(lfh_bf, lf), [gw_b[0]], [gwb_b[0]])
        P.dve(lambda e: e.tensor_copy(lfhf, lfh_bf), [gwb_b[0]], [gw_b[0]])
        P.dve(lambda e: e.tensor_tensor(lfl, lf, lfhf, ALU.subtract), [gw_b[0]], [gw_b[0]])
        P.dve(lambda e: e.tensor_copy(lfl_bf, lfl), [gw_b[0]], [gwb_b[0]])
        P.pe(lambda e: e.matmul(psum[2][:, 0:64], tri_bf, lfh_bf, start=True, stop=False), [cbf_b[0], gwb_b[0]], [ps_b[2]])
        P.pe(lambda e: e.matmul(psum[2][:, 0:64], tri_bf, lfl_bf, start=False, stop=True), [cbf_b[0], gwb_b[0]], [ps_b[2]])
        P.dve(lambda e: e.tensor_tensor(ccv, giv, psum[2][:, 0:64], ALU.subtract), [gts_b[0], ps_b[2]], [gw_b[0]])

        P.dve(lambda e: e.memset(Cs, 0.0), [], [Cs_b[0]])
        P.dve(lambda e: e.memset(Cbf, 0.0), [], [Cbf_b[0], Cbf_b[1]])
        ysm_sem = [P.dsem(("ym", i)) for i in range(2)]
        mlg = cst[:, C_MLG + l:C_MLG + l + 1]
        q32 = aqk[0:64, 0:512]
        qb_, kb_ = qkb[0:64, 0:512], qkb[0:64, 512:1024]

        def ml_conv(G):
            g0 = G * 512
            for v, (src, srcb) in enumerate(((PQK, PQK_b), (PKQ, PKQ_b))):
                c0 = C_CONV + 5 * (2 * l + v)
                al = acc[:, v * 512:(v + 1) * 512]
                rdl = [srcb[G], srcb[G - 1] if G > 0 else srcb[16]]
                P.dve(lambda e: e.tensor_scalar(al, src[:, g0 + 3:g0 + 3 + 512], cst[:, c0 + 3:c0 + 4],
                                                cst[:, c0 + 4:c0 + 5], ALU.mult, ALU.add),
                      rdl + [cst_b[0]], [acc_b[v]])
                for k_ in range(3):
                    P.dve(lambda e, k_=k_: e.scalar_tensor_tensor(
                        al, src[:, g0 + k_:g0 + k_ + 512], cst[:, c0 + k_:c0 + k_ + 1], al, ALU.mult, ALU.add),
                        rdl + [cst_b[0], acc_b[v]], [acc_b[v]])
                aql = aqk[:, v * 512:(v + 1) * 512]
                P.act(lambda e: e.activation(aql, al, AF.Silu), [acc_b[v]], [aqk_b[v]])
                qkl = qkb[:, v * 512:(v + 1) * 512]
                P.dve(lambda e: e.tensor_copy(qkl[0:64, :], aql[0:64, :]), [aqk_b[v]], [qkb_b[v]])

        def ml_A(T_):
            G, tt = divmod(T_, 4)
            if tt == 0:
                ml_conv(G)
            cs_ = slice(tt * 128, (tt + 1) * 128)
            s2 = T_ % 2
            b0, b1, b2 = (0, 1, 2) if s2 == 0 else (4, 5, 6)
            lfbl = lfb[:, s2 * 256:(s2 + 1) * 256]
            P.act(lambda e: e.activation(lfbl[:, 0:128], ones_bf, AF.Copy, scale=lfhf[:, T_:T_ + 1]),
                  [cbf_b[0], gw_b[0]], [lfb_b[s2]])
            P.act(lambda e: e.activation(lfbl[:, 128:256], ones_bf, AF.Copy, scale=lfl[:, T_:T_ + 1]),
                  [cbf_b[0], gw_b[0]], [lfb_b[s2]])
            pb0, pb1 = psum[b0][:, 0:128], psum[b0][:, 128:256]
            P.pe(lambda e: e.matmul(pb0, lfbl[:, 0:128], tri_bf, start=True, stop=False), [lfb_b[s2], cbf_b[0]], [ps_b[b0]])
            P.pe(lambda e: e.matmul(pb0, lfbl[:, 128:256], tri_bf, start=False, stop=True), [lfb_b[s2], cbf_b[0]], [ps_b[b0]])
            P.pe(lambda e: e.matmul(pb1, lfbl[:, 0:128], tri_bf, start=True, stop=False), [lfb_b[s2], cbf_b[0]], [ps_b[b0]])
            P.pe(lambda e: e.matmul(pb1, lfbl[:, 128:256], tri_bf, start=False, stop=False), [lfb_b[s2], cbf_b[0]], [ps_b[b0]])
            P.pe(lambda e: e.matmul(pb1, I_bf, neg_bf, start=False, stop=True), [cbf_b[0]], [ps_b[b0]])
            dxl = dex[:, s2 * 128:(s2 + 1) * 128]
            ebl = ebt[0:64, s2 * 128:(s2 + 1) * 128]
            scl = sct[:, s2 * 8:(s2 + 1) * 8]
            P.act(lambda e: e.activation(dxl, pb1, AF.Exp, bias=ccv[:, T_:T_ + 1], scale=1.0),
                  [ps_b[b0], gw_b[0]], [dex_b[s2]])
            P.act(lambda e: e.activation(ebl, pb0[0:64, :], AF.Exp), [ps_b[b0]], [eb_b[s2]])
            P.act(lambda e: e.activation(scl[:, 0:1], pb0[:, 127:128], AF.Exp, bias=ccv[:, T_:T_ + 1], scale=1.0),
                  [ps_b[b0], gw_b[0]], [sc_b[s2]])
            P.act(lambda e: e.activation(scl[0:64, 1:2], pb0[0:64, 127:128], AF.Exp), [ps_b[b0]], [sc_b[s2]])
            P.pe(lambda e: e.matmul(psum[b1][:, 0:128], kb_[:, cs_], qb_[:, cs_], start=True, stop=True),
                 [qkb_b[0], qkb_b[1]], [ps_b[b1]])
            P.pe(lambda e: e.matmul(psum[b1][:, 128:192], kb_[:, cs_], I_bf[0:64, 0:64], start=True, stop=True),
                 [qkb_b[1], cbf_b[0]], [ps_b[b1]])
            wl = wTt[:, s2 * 128:(s2 + 1) * 128]
            ql = qtt[0:64, s2 * 128:(s2 + 1) * 128]
            kl = ktt[:, s2 * 64:(s2 + 1) * 64]
            P.dve(lambda e: e.scalar_tensor_tensor(wl, psum[b1][:, 0:128], 0.125, dxl, ALU.mult, ALU.mult),
                  [ps_b[b1], dex_b[s2]], [wT_b[s2]])
            P.dve(lambda e: e.scalar_tensor_tensor(ql, q32[:, cs_], 0.125, ebl, ALU.mult, ALU.mult),
                  [aqk_b[0], eb_b[s2]], [qt_b[s2]])
            P.dve(lambda e: e.tensor_scalar(kl, psum[b1][:, 128:192], scl[:, 0:1], None, ALU.mult),
                  [ps_b[b1], sc_b[s2]], [kt_b[s2]])

        def ml_B(T_):
            G, tt = divmod(T_, 4)
            g0 = G * 512
            gs = G % 2
            cs_ = slice(tt * 128, (tt + 1) * 128)
            s2 = T_ % 2
            b0, b1, b2 = (0, 1, 2) if s2 == 0 else (4, 5, 6)
            scl = sct[:, s2 * 8:(s2 + 1) * 8]
            wl = wTt[:, s2 * 128:(s2 + 1) * 128]
            ql = qtt[0:64, s2 * 128:(s2 + 1) * 128]
            kl = ktt[:, s2 * 64:(s2 + 1) * 64]
            cprev = Cbf[0:64, s2 * 256:(s2 + 1) * 256]
            cnext = Cbf[0:64, (1 - s2) * 256:(2 - s2) * 256]
            P.pe(lambda e: e.matmul(psum[b2][:, 0:128], MV3[:, T_, 0:128], wl, start=True, stop=False),
                 [MV_b[T_ // 4], MV_b[16], wT_b[s2]], [ps_b[b2]])
            P.pe(lambda e: e.matmul(psum[b2][:, 0:128], cprev[:, 0:128], ql, start=False, stop=True),
                 [Cbf_b[s2], qt_b[s2]], [ps_b[b2]])
            P.pe(lambda e: e.matmul(psum[b2][:, 128:256], ones_bf, wl, start=True, stop=False), [cbf_b[0], wT_b[s2]], [ps_b[b2]])
            P.pe(lambda e: e.matmul(psum[b2][:, 128:256], cprev[:, 128:256], ql, start=False, stop=True),
                 [Cbf_b[s2], qt_b[s2]], [ps_b[b2]])
            P.pe(lambda e: e.matmul(psum[b1][0:64, 256:385], kl, MV3[:, T_, :], start=True, stop=True),
                 [kt_b[s2], MV_b[T_ // 4], MV_b[16]], [ps_b[b1]])
            P.dve(lambda e: e.scalar_tensor_tensor(Cs[0:64, 0:129], Cs[0:64, 0:129], scl[0:64, 1:2], psum[b1][0:64, 256:385],
                                                   ALU.mult, ALU.add), [Cs_b[0], sc_b[s2], ps_b[b1]], [Cs_b[0]])
            P.act(lambda e: e.activation(cnext[:, 0:128], Cs[0:64, 0:128], AF.Copy), [Cs_b[0]], [Cbf_b[1 - s2]])
            P.dve(lambda e: e.tensor_scalar(cnext[:, 128:256], ones_bf[0:64, :], Cs[0:64, 128:129], None, ALU.mult),
                  [Cs_b[0], cbf_b[0]], [Cbf_b[1 - s2]])
            rl = rdn[:, s2 * 128:(s2 + 1) * 128]
            P.act(lambda e: e.activation(rl, psum[b2][:, 128:256], AF.Abs), [ps_b[b2]], [rdn_b[s2]])
            P.dve(lambda e: e.tensor_scalar(rl, rl, 1.0, None, ALU.max), [rdn_b[s2]], [rdn_b[s2]])
            P.dve(lambda e: e.reciprocal(rl, rl), [rdn_b[s2]], [rdn_b[s2]])
            P.dve(lambda e: e.tensor_tensor(hTm[:, cs_], psum[b2][:, 0:128], rl, ALU.mult),
                  [ps_b[b2], rdn_b[s2]], [hTm_b[0]])
            if tt != 3:
                return
            P.act(lambda e: e.activation(sqm, hTm, AF.Square), [hTm_b[0]], [sqm_b[0]])
            bs = 3 if gs == 0 else 7
            P.pe(lambda e: e.matmul(psum[bs], ones_bf, sqm, start=True, stop=True), [cbf_b[0], sqm_b[0]], [ps_b[bs]])
            rms_rstd(rsm, psum[bs], [ps_b[bs]], [rsm_b[0]], scale=1.0 / 128)
            yl = ysm[:, gs * 512:(gs + 1) * 512]
            P.dve(lambda e: e.scalar_tensor_tensor(hTm, hTm, mlg, rsm, ALU.mult, ALU.mult), [hTm_b[0], rsm_b[0], cst_b[0]], [hTm_b[0]])
            P.dve(lambda e: e.tensor_tensor(yl, hTm, MO[:, g0:g0 + 512], ALU.mult), [hTm_b[0], MO_b[G]], [ysm_b[gs]])
            P.dma("sp", lambda e: e.dma_start(out=gin2_c[g0 // T][128:256, g0 % T:g0 % T + 512], in_=yl),
                  [ysm_b[gs]], [gin2_b[1][G]], ysm_sem[gs])

        for T_ in range(65):
            if T_ < 64:
                ml_A(T_)
            if T_ >= 1:
                ml_B(T_ - 1)
        for n in ("PQK", "PKQ", "MV", "MO", "cand", "GG", "gts", "gw", "gwb", "lfb", "dex", "wT", "eb", "qt", "sc", "kt",
                  "Cs", "Cbf", "rdn", "acc", "aqk", "qkb", "hTm", "sqm", "rsm", "ysm"):
            A.free(n)
        if stop == "mls":
            return

        for b_ in range(4):
            rd = [gin2_b[k_][b_ * 4 + i] for k_ in range(2) for i in range(4)]
            P.dma("pool", lambda e, b_=b_: e.collective_compute("AllGather", ALU.bypass, replica_groups=RG,
                                                               ins=[gin2_c[b_].opt()], outs=[gout2_c[b_].opt()]),
                  rd, [gout2_b[b_]], P.dsem(("cc2", b_)), inc=1)

        yT, yT_b = A.alloc("yT", PH, [NCH * T], BF16, nbuf=NCH * 4)
        yT3 = yT.rearrange("p (c t) -> p c t", c=NCH)
        hO, hO_b = A.alloc("hO", PH + 32768, [NCH * T], F32, nbuf=NCH * 4)
        hO3 = hO.rearrange("p (c t) -> p c t", c=NCH)
        cand, cand_b = A.alloc("cand", PH + 98304, [3 * 2048], BF16, nbuf=3)
        sqo, sqo_b = A.alloc("sqo", PH + 110592, [2 * 512], BF16, nbuf=2)
        rso, rso_b = A.alloc("rso", PH + 112640, [512], F32)
        cand_sem = [P.dsem(("cd", i)) for i in range(3)]
        ci[0] = 0
        for c8 in range(NCH):
            kind_, r = divmod(c8, 4)
            for tt in range(4):
                slot = ci[0] % 3
                ci[0] += 1
                cs = cand[:, slot * 2048:(slot + 1) * 2048].rearrange("p (h t) -> p h t", h=4)
                for b_ in range(4):
                    P.dma("sp", lambda e, cs=cs, b_=b_, r=r, kind_=kind_, tt=tt: e.dma_start(
                        out=cs[:, b_, :], in_=gout2_c[b_][r * 256 + kind_ * 128:r * 256 + kind_ * 128 + 128, tt * 512:(tt + 1) * 512]),
                        [gout2_b[b_]], [cand_b[slot]], cand_sem[slot])
                bk = cnt["ps"] % 2
                cnt["ps"] += 1
                for b_ in range(4):
                    P.pe(lambda e, b_=b_, cs=cs, bk=bk: e.matmul(psum[bk], sel_bf[:, b_ * 128:(b_ + 1) * 128], cs[:, b_, :],
                                                                start=(b_ == 0), stop=(b_ == 3)), [cbf_b[0], cand_b[slot]], [ps_b[bk]])
                evac(yT3[:, c8, tt * 512:(tt + 1) * 512], psum[bk], [ps_b[bk]], [yT_b[c8 * 4 + tt]])
        it = 0
        for m in range(NCH):
            wt, wb = W.get(("wo", l, m))
            wt3 = wt.rearrange("p (c n) -> p c n", c=NCH)
            for tt in range(4):
                ts = slice(tt * 512, (tt + 1) * 512)
                bk = 2 + it % 2
                for c in range(NCH):
                    P.pe(lambda e, c=c, bk=bk, wt3=wt3, ts=ts: e.matmul(psum[bk], wt3[:, c, :], yT3[:, c, ts],
                                                                         start=(c == 0), stop=(c == NCH - 1)),
                         [wb, yT_b[c * 4 + tt]], [ps_b[bk]])
                P.act(lambda e, m=m, ts=ts, bk=bk: e.activation(hO3[:, m, ts], psum[bk], AF.Copy), [ps_b[bk]], [hO_b[m * 4 + tt]])
                sql = sqo[:, (it % 2) * 512:(it % 2 + 1) * 512]
                P.act(lambda e, sql=sql, bk=bk: e.activation(sql, psum[bk], AF.Square), [ps_b[bk]], [sqo_b[it % 2]])
                P.pe(lambda e, m=m, tt=tt, sql=sql: e.matmul(psum[4 + tt], ones_bf, sql, start=(m == 0), stop=(m == NCH - 1)),
                     [cbf_b[0], sqo_b[it % 2]], [ps_b[4 + tt]])
                it += 1
        for tt in range(4):
            ts = slice(tt * 512, (tt + 1) * 512)
            rms_rstd(rso, psum[4 + tt], [ps_b[4 + tt]], [rso_b[0]])
            for m in range(NCH):
                P.dve(lambda e, m=m, ts=ts: e.tensor_tensor(hO3[:, m, ts], hO3[:, m, ts], rso, ALU.mult),
                      [hO_b[m * 4 + tt], rso_b[0]], [hO_b[m * 4 + tt]])
                P.dve(lambda e, m=m, ts=ts: e.scalar_tensor_tensor(xT3[:, m, ts], hO3[:, m, ts], gain(l, 3, m), xT3[:, m, ts],
                                                                   ALU.mult, ALU.add),
                      [hO_b[m * 4 + tt], cst_b[0], xT_b[tt]], [xT_b[tt]])
        for n in ("yT", "hO", "cand", "sqo", "rso"):
            A.free(n)

    for ph in phases:
        if ph[0] == "ffn":
            ffn_phase(ph[1], ph[2])
        else:
            mix_phase(ph[1])

    st = P.new_dsem("st")
    evs = []
    if stop == "att":
        for b_ in range(4):
            evs.append(P.dma("sp", lambda e, b_=b_: e.dma_start(out=dbg_d[0:128, b_ * T:(b_ + 1) * T], in_=gin2_c[b_][0:128, :]),
                             [b for b in gin2_b[0]], [], st))
    if stop == "mls":
        for b_ in range(4):
            evs.append(P.dma("sp", lambda e, b_=b_: e.dma_start(out=dbg_d[:, b_ * T:(b_ + 1) * T], in_=gin2_c[b_]),
                             [b for k_ in range(2) for b in gin2_b[k_]], [], st))
    if stop == "proj":
        evs.append(P.dma("sp", lambda e: e.dma_start(out=dbg_d[0:128, 0:T], in_=gin_kh(1, 1)), [gin_b[1][1]], [], st))
        evs.append(P.dma("sp", lambda e: e.dma_start(out=dbg_d[128:256, 0:T], in_=gin_kh(0, 2)), [gin_b[0][2]], [], st))
    if stop == "gather":
        for r in range(4):
            evs.append(P.dma("sp", lambda e, r=r: e.dma_start(out=dbg_d[0:128, r * 512:(r + 1) * 512], in_=gout_rkh(r, 1, 1)[:, 0:512]),
                             gout_b, [], st))
            evs.append(P.dma("sp", lambda e, r=r: e.dma_start(out=dbg_d[128:256, r * 512:(r + 1) * 512], in_=gout_rkh(r, 0, 2)[:, 0:512]),
                             gout_b, [], st))
    if stop is not None:
        evs.append(P.dma("sp", lambda e: e.dma_start(out=out_d, in_=xT[:, 0:16]), [xT_b[0]], [], st))
    for tt in range(4 if stop is None else 0):
        evs.append(P.dma("sp", lambda e, tt=tt: e.dma_start(
            out=out_d.rearrange("p (c t) -> p c t", c=NCH)[:, :, tt * 512:(tt + 1) * 512],
            in_=xT3[:, :, tt * 512:(tt + 1) * 512]), [xT_b[tt]], [], st))
    P.fence("sp", evs)
    P.emit()
    return nc, P


def _mix_tile_cols():
    tiles = []
    rng = np.arange(128)
    perm = (rng // 64) * 64 + ((rng % 64) + 32) % 64
    for base in (0, 512):
        for h in range(4):
            tiles.append((base + h * 128 + rng, base + h * 128 + perm))
    for h0 in (0, 2):
        tiles.append((1024 + h0 * 128 + rng, 1024 + (h0 + 1) * 128 + rng))
    r64 = np.arange(64)
    for h0 in (0, 2):
        tiles.append(tuple(np.concatenate([1536 + h * 64 + r64, 1792 + h * 64 + r64]) for h in (h0, h0 + 1)))
    for base in (2048, 2560):
        for h0 in (0, 2):
            tiles.append((base + h0 * 128 + rng, base + (h0 + 1) * 128 + rng))
    gcols = np.full(128, -1)
    gcols[:8] = 3072 + np.arange(8)
    gcols[32:40] = 3072 + np.arange(8)
    tiles.append((gcols, np.full(128, -1)))
    return tiles


def _prep_shared(inp, ffn_ids, mix_ids):
    f = np.float32
    out = {}
    w_in = np.asarray(inp["ffn_w_in"], f).reshape(DEPTH * 2, D, 2 * DFF)
    w_out = np.asarray(inp["ffn_w_out"], f).reshape(DEPTH * 2, DFF, D)
    if ffn_ids:
        wi = w_in[ffn_ids]
        n = len(ffn_ids)
        w1 = wi.reshape(n, NCH, 128, 2, NF, 128).transpose(0, 4, 2, 3, 1, 5)
        out["w1"] = np.ascontiguousarray(w1).reshape(n * NF, 128, 2 * NCH * 128)
        wo_ = w_out[ffn_ids]
        w2 = wo_.reshape(n, NF, 128, NCH, 128).transpose(0, 3, 2, 1, 4)
        out["w2"] = np.ascontiguousarray(w2).reshape(n * NCH, 128, NF * 128)
    else:
        out["w1"] = np.zeros((NF, 128, 2 * NCH * 128), f)
        out["w2"] = np.zeros((NCH, 128, NF * 128), f)
    if mix_ids:
        tiles = _mix_tile_cols()
        wm = np.zeros((len(mix_ids), 17, 128, 2, NCH, 128), f)
        wo = np.zeros((len(mix_ids), NCH, 128, NCH, 128), f)
        for i, l in enumerate(mix_ids):
            Wl = np.asarray(inp["mix_w_in"][l], f)
            Wp = np.concatenate([Wl, np.zeros((D, 1), f)], axis=1)
            for ti, groups in enumerate(tiles):
                for g, cols in enumerate(groups):
                    blk = Wp[:, cols]
                    wm[i, ti, :, g] = blk.reshape(NCH, 128, 128).transpose(1, 0, 2)
            Wo = np.asarray(inp["mix_w_out"][l], f)
            wo[i] = Wo.reshape(NCH, 128, NCH, 128).transpose(2, 1, 0, 3)
        out["wm"] = wm.reshape(len(mix_ids) * 17, 128, 2048)
        out["wo"] = wo.reshape(len(mix_ids) * NCH, 128, 1024)
    else:
        out["wm"] = np.zeros((17, 128, 2048), f)
        out["wo"] = np.zeros((NCH, 128, 1024), f)
    return out


def _prep_core(inp, core):
    f = np.float32
    b, j = divmod(core, 4)
    cst = np.zeros((128, NCST), f)
    g = np.asarray(inp["norm_gains"], f)
    cst[:, 0:96] = g.reshape(DEPTH * 6, NCH, 128).transpose(2, 0, 1).reshape(128, 96)
    cst[:, C_EPS] = EPS
    cst[:, C_ONE] = 1.0
    cw = np.asarray(inp["ml_conv_w"], f)
    cb = np.asarray(inp["ml_conv_b"], f)
    gb = np.asarray(inp["ml_gate_b"], f)
    for l in range(DEPTH):
        cst[:, C_SUBG + l] = np.asarray(inp["da_subln_g"][l], f)
        cst[:, C_MLG + l] = np.asarray(inp["ml_norm_g"][l], f)
        cst[:, C_GB + 2 * l] = gb[l, j]
        cst[:, C_GB + 2 * l + 1] = gb[l, 4 + j]
        qi = j * 64 + np.arange(64)
        ki = 256 + j * 64 + np.arange(64)
        for v, idx in enumerate((np.concatenate([qi, ki]), np.concatenate([ki, qi]))):
            c0 = C_CONV + 5 * (2 * l + v)
            cst[:, c0:c0 + 4] = cw[l][:, idx].T
            cst[:, c0 + 4] = cb[l][idx]
        cst[:, C_LAM + 256 * l: C_LAM + 256 * (l + 1)] = np.asarray(inp["da_lambda"][l], f).reshape(1, 256)
    cst[j, C_GSEL] = 1.0
    cst[4 + j, C_GSEL + 1] = 1.0
    cst[32 + j, C_GSEL] = 1.0
    cst[32 + 4 + j, C_GSEL + 1] = 1.0
    sw = np.zeros((128, 128), f)
    sw[(np.arange(128) + 64) % 128, np.arange(128)] = 1.0
    cst[:, C_SELSW + j * 128: C_SELSW + (j + 1) * 128] = sw
    cst[:, C_I:C_I + 128] = np.eye(128, dtype=f)
    ii = np.arange(128)
    cst[:, C_TRI:C_TRI + 128] = (ii[:, None] <= ii[None, :]).astype(f)
    cst[:, C_NEG:C_NEG + 128] = np.where(ii[:, None] > ii[None, :], -30000.0, 0.0).astype(f)
    cst[:, C_SEL + j * 128: C_SEL + (j + 1) * 128] = np.eye(128, dtype=f)
    pos = (j * T + np.arange(T)).astype(np.float64)
    inv = 10000.0 ** (-np.arange(0, 64, 2, dtype=np.float64) / 64.0)
    d = np.arange(128) % 64
    ang = pos[None, :] * inv[d % 32][:, None]
    sgn = np.where(d < 32, -1.0, 1.0)[:, None]
    rope = np.concatenate([np.cos(ang), sgn * np.sin(ang)], axis=1).astype(f)
    return {"cst": cst, "rope": rope}


def _shard_x(x):
    xs = []
    for c in range(8):
        b, q = divmod(c, 4)
        xc = np.asarray(x[b, q * T:(q + 1) * T, :], np.float32)
        xs.append(np.ascontiguousarray(xc.reshape(T, NCH, 128).transpose(2, 1, 0)).reshape(128, NCH * T))
    return xs


def _unshard(outs):
    y = np.zeros((2, SEQ, D), np.float32)
    for c in range(8):
        b, q = divmod(c, 4)
        o = np.asarray(outs[c], np.float32).reshape(128, NCH, T).transpose(2, 1, 0).reshape(T, D)
        y[b, q * T:(q + 1) * T, :] = o
    return y


ALL_PHASES = [("ffn", 0, 0), ("mix", 0), ("ffn", 0, 1), ("ffn", 1, 0), ("mix", 1), ("ffn", 1, 1)]


def run_phases(inputs, phases, stop=None):
    ffn_ids = sorted({ph[1] * 2 + ph[2] for ph in phases if ph[0] == "ffn"})
    mix_ids = sorted({ph[1] for ph in phases if ph[0] == "mix"})
    shared = _prep_shared(inputs, ffn_ids, mix_ids)
    xs = _shard_x(inputs["x"])
    nc, P = build_program(phases, stop=stop)
    in_maps = [dict(shared, xT=xs[c], **_prep_core(inputs, c)) for c in range(8)]
    res = run_bass_kernel_spmd(nc, in_maps, core_ids=list(range(8)))
    if stop is not None:
        return None, [np.asarray(r["dbg"]) for r in res.results]
    return _unshard([r["out"] for r in res.results])


def kernel(**inputs):
    return run_phases(inputs, ALL_PHASES)
```
